# Optimizing a Trainium2 kernel written in Bass

```python
import math
import jax, jax.numpy as jnp
from jax import lax
import numpy as np

D_MODEL = 1024
BATCH = 2
SEQ = 8192
DEPTH = 1

CONV_CH = D_MODEL
CONV_WIDTH = 31
RET_HEADS = 4
RET_QK_DIM = 256
RET_V_DIM = 512
RET_QK = RET_HEADS * RET_QK_DIM
RET_V = RET_HEADS * RET_V_DIM
RET_CHUNK = 128
ROPE_BASE = 10000.0
NORM_EPS = 1e-6
LN_EPS = 1e-5

SPLIT_SIZES = (
    CONV_CH,
    CONV_CH,
    CONV_CH,
    RET_QK,
    RET_QK,
    RET_V,
    RET_V,
    D_MODEL,
    D_MODEL,
)
IN_COLS = int(sum(SPLIT_SIZES))
SPLIT_POINTS = tuple(int(s) for s in np.cumsum(SPLIT_SIZES)[:-1])

kernel_name = "hybrid_conformer_conv_retention_gated_block"


def _rmsnorm(x, g):
    xf = x.astype(jnp.float32)
    y = xf * lax.rsqrt(jnp.mean(xf * xf, axis=-1, keepdims=True) + NORM_EPS)
    return (y * g.astype(jnp.float32)).astype(x.dtype)


def _layernorm(x, g, b):
    xf = x.astype(jnp.float32)
    mu = jnp.mean(xf, axis=-1, keepdims=True)
    var = jnp.mean(jnp.square(xf - mu), axis=-1, keepdims=True)
    y = (xf - mu) * lax.rsqrt(var + LN_EPS)
    return (y * g.astype(jnp.float32) + b.astype(jnp.float32)).astype(x.dtype)


def _rotate_half(t):
    t1, t2 = jnp.split(t, 2, axis=-1)
    return jnp.concatenate([-t2, t1], axis=-1)


def _rotary(t, cos, sin):
    return t * cos[None, :, None, :] + _rotate_half(t) * sin[None, :, None, :]


def _retention_chunkwise(q, k, v, log_gamma):
    B, S, H, dk = q.shape
    dv = v.shape[-1]
    C = RET_CHUNK
    N = S // C
    qc = q.reshape(B, N, C, H, dk)
    kc = k.reshape(B, N, C, H, dk)
    vc = v.reshape(B, N, C, H, dv)

    idx = jnp.arange(C, dtype=jnp.float32)
    diff = idx[:, None] - idx[None, :]
    decay = jnp.where((diff >= 0)[None],
                      jnp.exp(log_gamma[:, None, None] * jnp.maximum(diff, 0.0)[None]),
                      0.0)

    scores = jnp.einsum('bnihd,bnjhd->bnhij', qc, kc) * decay[None, None]
    inner = jnp.einsum('bnhij,bnjhe->bnihe', scores, vc)

    xi = jnp.exp(log_gamma[None, :] * (idx[:, None] + 1.0))
    zeta = jnp.exp(log_gamma[None, :] * (C - 1.0 - idx[:, None]))
    gamma_c = jnp.exp(log_gamma * C)

    def step(R, inp):
        qi, ki, vi = inp
        cross = jnp.einsum('bihd,bhde->bihe', qi, R) * xi[None, :, :, None]
        R = R * gamma_c[None, :, None, None] + jnp.einsum(
            'bjhd,bjhe->bhde', ki * zeta[None, :, :, None], vi)
        return R, cross

    R0 = jnp.zeros((B, H, dk, dv), jnp.float32)
    _, cross = lax.scan(step, R0, (qc.transpose(1, 0, 2, 3, 4),
                                   kc.transpose(1, 0, 2, 3, 4),
                                   vc.transpose(1, 0, 2, 3, 4)))
    cross = cross.transpose(1, 0, 2, 3, 4)
    return (inner + cross).reshape(B, S, H, dv)


def setup_inputs(seed: int = 0) -> dict:
    key = jax.random.key(seed)
    ks = jax.random.split(key, 16)
    f32 = jnp.float32
    x = jax.random.normal(ks[0], (BATCH, SEQ, D_MODEL), f32)
    norm_gain = 1.0 + 0.02 * jax.random.normal(ks[1], (DEPTH, D_MODEL), f32)
    w_in = jax.random.normal(ks[2], (DEPTH, D_MODEL, IN_COLS), f32) * D_MODEL ** -0.5
    conv_dw_w = jax.random.normal(ks[3], (DEPTH, CONV_WIDTH, CONV_CH), f32) * CONV_WIDTH ** -0.5
    conv_dw_b = 0.02 * jax.random.normal(ks[4], (DEPTH, CONV_CH), f32)
    conv_ln_g = 1.0 + 0.02 * jax.random.normal(ks[5], (DEPTH, CONV_CH), f32)
    conv_ln_b = 0.02 * jax.random.normal(ks[6], (DEPTH, CONV_CH), f32)
    w_conv_proj = jax.random.normal(ks[7], (DEPTH, CONV_CH, D_MODEL), f32) * CONV_CH ** -0.5
    ret_gn_g = 1.0 + 0.02 * jax.random.normal(ks[8], (DEPTH, RET_V), f32)
    w_ret_proj = jax.random.normal(ks[9], (DEPTH, RET_V, D_MODEL), f32) * RET_V ** -0.5
    w_out = jax.random.normal(ks[10], (DEPTH, D_MODEL, D_MODEL), f32) * D_MODEL ** -0.5
    final_gain = 1.0 + 0.02 * jax.random.normal(ks[11], (D_MODEL,), f32)
    return {"x": x, "norm_gain": norm_gain, "w_in": w_in, "conv_dw_w": conv_dw_w,
            "conv_dw_b": conv_dw_b, "conv_ln_g": conv_ln_g, "conv_ln_b": conv_ln_b,
            "w_conv_proj": w_conv_proj, "ret_gn_g": ret_gn_g, "w_ret_proj": w_ret_proj,
            "w_out": w_out, "final_gain": final_gain}


def reference(x, norm_gain, w_in, conv_dw_w, conv_dw_b, conv_ln_g, conv_ln_b,
              w_conv_proj, ret_gn_g, w_ret_proj, w_out, final_gain):
    B, S, _ = x.shape
    dt = x.dtype
    pos = jnp.arange(S, dtype=jnp.float32)
    inv_freq = ROPE_BASE ** (-jnp.arange(0, RET_QK_DIM, 2, dtype=jnp.float32) / RET_QK_DIM)
    ang = pos[:, None] * inv_freq[None, :]
    cos = jnp.cos(jnp.concatenate([ang, ang], axis=-1))
    sin = jnp.sin(jnp.concatenate([ang, ang], axis=-1))
    log_gamma = jnp.log1p(-jnp.exp2(-5.0 - jnp.arange(RET_HEADS, dtype=jnp.float32)))

    for l in range(DEPTH):
        h = _rmsnorm(x, norm_gain[l])
        proj = jnp.einsum('bsd,de->bse', h, w_in[l])
        a_val, a_glu, a_gate, q, k, v, r_gate, g_a, g_b = jnp.split(proj, SPLIT_POINTS, axis=-1)

        u = a_val * jax.nn.sigmoid(a_glu)
        u = lax.conv_general_dilated(
            u, conv_dw_w[l][:, None, :].astype(u.dtype), window_strides=(1,),
            padding=[(CONV_WIDTH - 1, 0)],
            dimension_numbers=('NWC', 'WIO', 'NWC'),
            feature_group_count=CONV_CH) + conv_dw_b[l]
        u = jax.nn.silu(_layernorm(u, conv_ln_g[l], conv_ln_b[l]))
        u = u * jax.nn.silu(a_gate)
        y_a = jnp.einsum('bsc,cd->bsd', u, w_conv_proj[l])

        qh = _rotary(q.astype(jnp.float32).reshape(B, S, RET_HEADS, RET_QK_DIM), cos, sin)
        kh = _rotary(k.astype(jnp.float32).reshape(B, S, RET_HEADS, RET_QK_DIM), cos, sin)
        kh = kh * (RET_QK_DIM ** -0.5)
        vh = v.astype(jnp.float32).reshape(B, S, RET_HEADS, RET_V_DIM)
        o = _retention_chunkwise(qh, kh, vh, log_gamma)
        mu = jnp.mean(o, axis=-1, keepdims=True)
        var = jnp.mean(jnp.square(o - mu), axis=-1, keepdims=True)
        o = ((o - mu) * lax.rsqrt(var + LN_EPS)).reshape(B, S, RET_V)
        o = (o * ret_gn_g[l].astype(jnp.float32)).astype(dt)
        o = o * jax.nn.silu(r_gate)
        y_b = jnp.einsum('bsc,cd->bsd', o, w_ret_proj[l])

        m = jax.nn.sigmoid(g_a) * y_a + jax.nn.sigmoid(g_b) * y_b
        x = x + jnp.einsum('bsd,de->bse', m, w_out[l])

    return _rmsnorm(x, final_gain)
```

```python
import math
import numpy as np
from contextlib import ExitStack
import concourse.bass as bass
import concourse.mybir as mybir
from concourse.bass_utils import run_bass_kernel_spmd

F32 = mybir.dt.float32
BF16 = mybir.dt.bfloat16
AF = mybir.ActivationFunctionType
ALU = mybir.AluOpType

ENG_NAMES = ("pe", "act", "dve", "pool", "sp")

S_CORE = 2048
NT = 16
D = 1024
HALO = 32
NCOLS = 11264
C_AVAL, C_AGLU, C_AGATE, C_Q, C_K, C_V, C_R, C_GA, C_GB = 0, 1024, 2048, 3072, 4096, 5120, 7168, 9216, 10240
PP_CB, PP_LG, PP_LB, PP_GN, PP_ZE, PP_EPS, PP_COEF, NPP = 0, 8, 16, 24, 40, 104, 108, 120
GROUPS = [[0, 1, 2, 3], [4, 5, 6, 7]]


class Op:
    __slots__ = ("eng", "fn", "reads", "writes", "is_dma", "ninc", "deps",
                 "needs_inc", "sem", "val", "idx", "semkey", "barrier")

    def __init__(self, eng, fn, reads, writes, is_dma, ninc, semkey):
        self.eng = eng
        self.fn = fn
        self.reads = tuple(reads)
        self.writes = tuple(writes)
        self.is_dma = is_dma
        self.ninc = ninc
        self.deps = ()
        self.needs_inc = False
        self.sem = None
        self.val = 0
        self.semkey = semkey
        self.barrier = False


class Prog:
    def __init__(self):
        self.ops = []

    def op(self, eng, fn, reads=(), writes=()):
        self.ops.append(Op(eng, fn, reads, writes, False, 1, None))

    def dma(self, eng, fn, reads=(), writes=(), semkey=None, ninc=16):
        assert semkey is not None
        self.ops.append(Op(eng, fn, reads, writes, True, ninc, semkey))

    def barrier(self):
        o = Op(None, None, (), (), False, 0, None)
        o.barrier = True
        self.ops.append(o)

    def finalize(self):
        last_w = {}
        readers = {}
        last_on = {}
        pending = {e: [] for e in ENG_NAMES}
        for i, o in enumerate(self.ops):
            o.idx = i
            if o.barrier:
                lst = list(last_on.values())
                for e in ENG_NAMES:
                    pending[e] = list(lst)
                continue
            deps = set()
            for r in o.reads:
                if r in last_w:
                    deps.add(last_w[r])
            for w in o.writes:
                if w in last_w:
                    deps.add(last_w[w])
                for rd in readers.get(w, ()):
                    deps.add(rd)
            if pending[o.eng]:
                deps.update(pending[o.eng])
                pending[o.eng] = []
            deps.discard(i)
            real = []
            for d in deps:
                dop = self.ops[d]
                if dop.fn is None:
                    continue
                if dop.is_dma or dop.eng != o.eng or o.eng != "pe" or o.is_dma:
                    real.append(d)
                    dop.needs_inc = True
            o.deps = tuple(sorted(real))
            for w in o.writes:
                last_w[w] = i
                readers[w] = []
            for r in o.reads:
                readers.setdefault(r, []).append(i)
            if o.fn is not None:
                last_on[("dma", o.semkey) if o.is_dma else ("eng", o.eng)] = i
        cnt = {}
        for o in self.ops:
            if o.barrier:
                continue
            if o.is_dma:
                key = ("dma", o.semkey)
                cnt[key] = cnt.get(key, 0) + o.ninc
                o.val = cnt[key]
            elif o.needs_inc:
                key = ("eng", o.eng)
                cnt[key] = cnt.get(key, 0) + 1
                o.val = cnt[key]
        self.dma_keys = sorted({o.semkey for o in self.ops if o.is_dma}, key=str)

    def emit(self, nc, stack):
        self.finalize()
        sems = {}
        for e in ENG_NAMES:
            sems[("eng", e)] = stack.enter_context(nc.semaphore("s_" + e))
        for k in self.dma_keys:
            sems[("dma", k)] = stack.enter_context(nc.semaphore("d_" + str(k)))
        for o in self.ops:
            if o.barrier:
                continue
            o.sem = sems[("dma", o.semkey)] if o.is_dma else sems[("eng", o.eng)]
        block = stack.enter_context(nc.Block())
        by_eng = {e: [o for o in self.ops if o.eng == e] for e in ENG_NAMES}
        ops = self.ops

        def run(engine, lst):
            waited = {}
            for o in lst:
                need = {}
                for d in o.deps:
                    dop = ops[d]
                    k = id(dop.sem)
                    if dop.val > need.get(k, (0, None))[0]:
                        need[k] = (dop.val, dop.sem)
                for k, (v, s) in need.items():
                    if waited.get(k, 0) < v:
                        engine.wait_ge(s, v)
                        waited[k] = v
                if o.fn is None:
                    continue
                if o.is_dma:
                    o.fn(engine, o.sem)
                else:
                    ins = o.fn(engine)
                    if o.needs_inc:
                        ins.then_inc(o.sem, 1)

        @block.tensor
        def _(e):
            run(e, by_eng["pe"])

        @block.scalar
        def _(e):
            run(e, by_eng["act"])

        @block.vector
        def _(e):
            run(e, by_eng["dve"])

        @block.gpsimd
        def _(e):
            run(e, by_eng["pool"])

        @block.sync
        def _(e):
            run(e, by_eng["sp"])


def log_gammas():
    return [math.log1p(-2.0 ** (-5.0 - h)) for h in range(4)]


def build_program():
    nc = bass.Bass("TRN2", target_bir_lowering=False)
    P = Prog()
    LG = log_gammas()

    def din(name, shape, dt=F32):
        return nc.dram_tensor(name, shape, dt, kind="ExternalInput").ap()

    x = din("x", [S_CORE, D])
    xh = din("xh", [HALO, D])
    w_in = din("w_in", [D, NCOLS])
    w_cp = din("w_cp", [D, D])
    w_rp = din("w_rp", [2 * D, D])
    w_out = din("w_out", [D, D])
    gainB_d = din("gainB", [128, D])
    fgainB_d = din("fgainB", [128, D])
    pp_d = din("pp", [128, NPP])
    convw_d = din("convw", [128, 256])
    i4_d = din("i4", [128, 32])
    mask_d = din("maskT", [128, 4 * 128])
    ident_d = din("ident", [128, 128])
    cos_d = din("cosT", [128, S_CORE])
    sin_d = din("sinT", [128, S_CORE])
    out_d = nc.dram_tensor("out", [S_CORE, D], F32, kind="ExternalOutput").ap()
    st_in = [nc.dram_tensor(f"st_in{h}", [128, 1024], F32, kind="Internal").ap() for h in range(4)]
    st_out = [nc.dram_tensor(f"st_out{h}", [4 * 128, 1024], F32, kind="Internal").ap() for h in range(4)]
    og_d = nc.dram_tensor("og_spill", [S_CORE, 2 * D], BF16, kind="Internal").ap()

    w_in_v = w_in.rearrange("(kc p) n -> p kc n", p=128)

    with ExitStack() as gst:
        def sb(name, shape, dt=F32, st=None):
            return (st or gst).enter_context(nc.sbuf_tensor("sb_" + name, shape, dt))

        def pst(name, shape, dt=F32):
            return gst.enter_context(nc.psum_tensor("ps_" + name, shape, dt))

        hT = sb("hT", [128, 8, HALO + S_CORE], BF16)
        ident = sb("ident", [128, 128])
        identb = sb("identb", [128, 128], BF16)
        pp = sb("pp", [128, NPP])
        eps6 = sb("eps6", [128, 1])
        eps5 = sb("eps5", [128, 1])
        psA = pst("psA", [128, 2, 512])
        psB = pst("psB", [128, 2, 512])
        psS = pst("psS", [128, 2, 512])
        psX = pst("psX", [128, 512])
        psM = pst("psM", [128, 512])
        sps = psM[:, 128:256]
        otr = psM[:, 256:512].bitcast(BF16).rearrange("p (c d) -> p c d", c=4)
        ktr = [psX[:, 0:128].bitcast(BF16).rearrange("p (c d) -> p c d", c=2),
               psM[:, 0:128].bitcast(BF16).rearrange("p (c d) -> p c d", c=2)]
        KTRB = ["bankX", "bankM"]
        ktr4 = ktr + [psB[:, 0, 0:128].bitcast(BF16).rearrange("p (c d) -> p c d", c=2),
                      psB[:, 1, 0:128].bitcast(BF16).rearrange("p (c d) -> p c d", c=2)]
        KTRB4 = KTRB + ["psB0", "psB1"]

        def ld(q, dst, src, key, reads=()):
            P.dma(q, lambda e, s: e.dma_start(out=dst, in_=src).then_inc(s, 16),
                  reads=reads, writes=[key], semkey=key)

        ld("sp", ident[:], ident_d, "ident")
        ld("sp", pp[:], pp_d, "pp")
        P.op("dve", lambda e: e.tensor_copy(out=identb[:], in_=ident[:]), reads=["ident"], writes=["identb"])
        P.op("pool", lambda e: e.memset(eps6[:], 1e-6), writes=["eps6"])
        P.op("pool", lambda e: e.memset(eps5[:], 1e-5), writes=["eps5"])

        st01 = ExitStack()
        st01.__enter__()
        Wkv = sb("Wkv", [128, 8, 768], BF16, st=st01)
        Wqr = sb("Wqr", [128, 8, 768], BF16, st=st01)

        def load_w(buf, key, cols):
            for (do, sc, n) in cols:
                P.dma("pool", lambda e, s, do=do, sc=sc, n=n: e.dma_start(
                    out=buf[:, :, do:do + n], in_=w_in_v[:, :, sc:sc + n]).then_inc(s, 16),
                    writes=[f"{key}_{do}"], semkey=f"{key}_{do}")
        load_w(Wkv, "Wkv", [(0, C_K, 256), (256, C_V, 512)])
        with ExitStack() as st0:
            gainB = sb("gainB", [128, D], st=st0)
            xs = [sb(f"xs{i}", [128, D], st=st0) for i in range(3)]
            xg = [sb(f"xg{i}", [128, D], st=st0) for i in range(3)]
            junk = sb("junk", [128, D], st=st0)
            ss = [sb(f"ss{i}", [128, 1], st=st0) for i in range(3)]
            rs = [sb(f"rs{i}", [128, 1], st=st0) for i in range(3)]
            dg = [sb(f"dg{i}", [128, 128], st=st0) for i in range(3)]
            ld("sp", gainB[:], gainB_d, "gainB")
            pss = [psA, psB, psS]
            def p0_a(t):
                rows = HALO if t == 0 else 128
                src = xh if t == 0 else x[(t - 1) * 128:t * 128, :]
                s = t % 3
                ld("sp", xs[s][0:rows, :], src, f"xs{s}")
                P.op("act", lambda e, s=s, rows=rows: e.activation(
                    out=junk[0:rows, :], in_=xs[s][0:rows, :], func=AF.Square, accum_out=ss[s][0:rows, :]),
                    reads=[f"xs{s}"], writes=["junk", f"ss{s}"])
                P.op("act", lambda e, s=s, rows=rows: e.activation(
                    out=ss[s][0:rows, :], in_=ss[s][0:rows, :], func=AF.Sqrt, bias=eps6[0:rows, :], scale=1.0 / D),
                    reads=[f"ss{s}", "eps6"], writes=[f"ss{s}"])
                P.op("dve", lambda e, s=s, rows=rows: e.reciprocal(out=rs[s][0:rows, :], in_=ss[s][0:rows, :]),
                     reads=[f"ss{s}"], writes=[f"rs{s}"])
                P.op("dve", lambda e, s=s, rows=rows: e.tensor_scalar(
                    out=dg[s][0:rows, 0:rows], in0=ident[0:rows, 0:rows], scalar1=rs[s][0:rows, :], scalar2=None,
                    op0=ALU.mult), reads=[f"rs{s}", "ident"], writes=[f"dg{s}"])
                P.op("pool", lambda e, s=s, rows=rows: e.tensor_tensor(
                    out=xg[s][0:rows, :], in0=xs[s][0:rows, :], in1=gainB[0:rows, :], op=ALU.mult),
                    reads=[f"xs{s}", "gainB"], writes=[f"xg{s}"])

            def p0_b(t):
                rows = HALO if t == 0 else 128
                col0 = 0 if t == 0 else HALO + (t - 1) * 128
                s = t % 3

                def tr(e, s=s, rows=rows):
                    last = None
                    for dc in range(8):
                        last = e.matmul(pss[s][:, dc // 4, (dc % 4) * 128:(dc % 4) * 128 + rows],
                                        lhsT=xg[s][0:rows, dc * 128:(dc + 1) * 128],
                                        rhs=dg[s][0:rows, 0:rows], start=True, stop=True)
                    return last
                P.op("pe", tr, reads=[f"xg{s}", f"dg{s}"], writes=[f"ps{'ABS'[s]}0", f"ps{'ABS'[s]}1"])
                for half in range(2):
                    eng = "act" if half == 0 else "dve"

                    def ev(e, s=s, rows=rows, half=half, col0=col0, eng=eng):
                        src_ap = pss[s][:, half, :].rearrange("p (c t) -> p c t", c=4)[:, :, 0:rows]
                        dst_ap = hT[:, half * 4:(half + 1) * 4, col0:col0 + rows]
                        if eng == "act":
                            return e.activation(out=dst_ap, in_=src_ap, func=AF.Copy)
                        return e.tensor_copy(out=dst_ap, in_=src_ap)
                    P.op(eng, ev, reads=[f"ps{'ABS'[s]}{half}"], writes=[f"hT_{t}_{half}"])

            p0_a(0)
            p0_a(1)
            for t in range(NT + 1):
                if t + 2 < NT + 1:
                    p0_a(t + 2)
                p0_b(t)
                if t == 12:
                    load_w(Wqr, "Wqr", [(0, C_Q, 256), (256, C_R, 512)])
        HT_ALL = [f"hT_{t}_{half}" for t in range(NT + 1) for half in range(2)]

        def ht_keys(tiles):
            return [f"hT_{t + 1}_{half}" for t in tiles for half in range(2)]
        P.barrier()

        with ExitStack() as st1:
            maskT = sb("maskT", [128, 4, 128], st=st1)
            ld("sp", maskT[:], mask_d.rearrange("p (h i) -> p h i", h=4), "maskT")
            dgc = sb("dgc", [128, 12, 128], st=st1)
            for hr in range(12):
                P.op("dve", lambda e, hr=hr: e.tensor_scalar(
                    out=dgc[:, hr, :], in0=ident[:], scalar1=pp[:, PP_COEF + hr:PP_COEF + hr + 1], scalar2=None,
                    op0=ALU.mult), reads=["ident", "pp"], writes=[f"dgc{hr}"])
            kT2 = [sb(f"kT{i}", [128, 2, S_CORE], BF16, st=st1) for i in range(2)]
            qT = sb("qT", [128, 2, S_CORE], BF16, st=st1)
            kend = sb("kend", [128, NT, 256], BF16, st=st1)
            vv2 = [sb(f"vv{i}", [128, NT, 512], BF16, st=st1) for i in range(2)]
            srall = sb("srall", [128, NT, 512], BF16, st=st1)
            cs = [sb(f"cs{i}", [128, 512], st=st1) for i in range(2)]
            sn = [sb(f"sn{i}", [128, 512], st=st1) for i in range(2)]
            rt = [[sb(f"rt{i}_{k}", [128, 512], st=st1) for k in range(4)] for i in range(1)]
            Rloc = sb("Rloc", [128, 2, 512], st=st1)
            Rg = sb("Rg", [128, 3, 1024], st=st1)
            Rb = sb("Rb", [128, 2, 512], BF16, st=st1)
            sT = [sb(f"sT{i}", [128, 128], BF16, st=st1) for i in range(2)]
            on = [sb(f"on{i}", [128, 512], st=st1) for i in range(2)]
            og = [sb(f"og{i}", [128, 512], BF16, st=st1) for i in range(4)]
            st6 = [sb(f"st6_{i}", [128, 6], st=st1) for i in range(2)]
            mv = [sb(f"mv{i}", [128, 2], st=st1) for i in range(2)]
            rsd = [sb(f"rsd{i}", [128, 1], st=st1) for i in range(2)]
            sdv = [sb(f"sdv{i}", [128, 1], st=st1) for i in range(2)]
            nmr = [sb(f"nmr{i}", [128, 1], st=st1) for i in range(2)]
            tabcnt = [0]

            def proj_rot(W, wkeys, woff, dst, dstkey, tb, ceng="pool", pp_=None, ppk="psA"):
                pp_ = psA if pp_ is None else pp_
                ti = tabcnt[0] % 2
                tabcnt[0] += 1
                ld("sp", cs[ti][:], cos_d[:, tb * 512:(tb + 1) * 512], f"cs{ti}")
                ld("sp", sn[ti][:], sin_d[:, tb * 512:(tb + 1) * 512], f"sn{ti}")
                c0 = HALO + tb * 512

                def mmf(e):
                    last = None
                    for c in range(2):
                        for kc in range(8):
                            last = e.matmul(pp_[:, c, :], lhsT=W[:, kc, woff + c * 128:woff + (c + 1) * 128],
                                            rhs=hT[:, kc, c0:c0 + 512], start=(kc == 0), stop=(kc == 7))
                    return last
                P.op("pe", mmf, reads=wkeys + ht_keys(range(tb * 4, tb * 4 + 4)), writes=[ppk + "0", ppk + "1"])
                r = rt[0]
                ri = 0
                P.op("dve", lambda e: e.tensor_tensor(out=r[0][:], in0=pp_[:, 0, :], in1=cs[ti][:], op=ALU.mult),
                     reads=[ppk + "0", f"cs{ti}"], writes=[f"rt{ri}0"])
                P.op("dve", lambda e: e.tensor_tensor(out=r[1][:], in0=pp_[:, 1, :], in1=sn[ti][:], op=ALU.mult),
                     reads=[ppk + "1", f"sn{ti}"], writes=[f"rt{ri}1"])
                P.op("dve", lambda e: e.tensor_tensor(out=r[2][:], in0=pp_[:, 1, :], in1=cs[ti][:], op=ALU.mult),
                     reads=[ppk + "1", f"cs{ti}"], writes=[f"rt{ri}2"])
                P.op("dve", lambda e: e.tensor_tensor(out=r[3][:], in0=pp_[:, 0, :], in1=sn[ti][:], op=ALU.mult),
                     reads=[ppk + "0", f"sn{ti}"], writes=[f"rt{ri}3"])
                P.op(ceng, lambda e: e.tensor_tensor(out=dst[:, 0, tb * 512:(tb + 1) * 512], in0=r[0][:], in1=r[1][:],
                                                     op=ALU.subtract),
                     reads=[f"rt{ri}0", f"rt{ri}1"], writes=[f"{dstkey}0_{tb}"])
                P.op(ceng, lambda e: e.tensor_tensor(out=dst[:, 1, tb * 512:(tb + 1) * 512], in0=r[2][:], in1=r[3][:],
                                                     op=ALU.add),
                     reads=[f"rt{ri}2", f"rt{ri}3"], writes=[f"{dstkey}1_{tb}"])

            WKV_KEYS_K = ["Wkv_0"]
            WKV_KEYS_V = ["Wkv_256"]
            WQR_KEYS_Q = ["Wqr_0"]
            WQR_KEYS_R = ["Wqr_256"]
            for h in range(4):
                lg = LG[h]
                hp = h % 2
                kT = kT2[hp]
                vv = vv2[hp]
                KT = f"kT{hp}c"
                VV = f"vv{hp}_"

                def p1_vproj(n, par=hp, use_x=False):
                    vb = n % 2
                    dst_ps = psX[:, :] if use_x else psB[:, vb, :]
                    pkey = "bankX" if use_x else f"psB{vb}"
                    vdst = vv2[par]

                    def mmv(e, n=n, dst_ps=dst_ps):
                        last = None
                        for kc in range(8):
                            last = e.matmul(dst_ps, lhsT=hT[:, kc, HALO + n * 128:HALO + (n + 1) * 128],
                                            rhs=Wkv[:, kc, 256:768], start=(kc == 0), stop=(kc == 7))
                        return last
                    P.op("pe", mmv, reads=WKV_KEYS_V + ht_keys([n]), writes=[pkey])
                    P.op("act", lambda e, n=n, dst_ps=dst_ps, vdst=vdst: e.activation(out=vdst[:, n, :], in_=dst_ps, func=AF.Copy),
                         reads=[], writes=[f"vv{par}_{n}", pkey])

                def p1_trk(n, h=h):
                    tb = n // 4
                    slots, skeys = (ktr, KTRB) if h == 0 else (ktr4, KTRB4)
                    ks = n % len(slots)
                    kt_ap, kt_key = slots[ks], skeys[ks]

                    def trk(e, n=n, kT=kT, kt_ap=kt_ap):
                        last = None
                        for c in range(2):
                            last = e.transpose(kt_ap[:, c, :], kT[:, c, n * 128:(n + 1) * 128], identb[:])
                        return last
                    P.op("pe", trk, reads=[f"{KT}0_{tb}", f"{KT}1_{tb}", "identb"], writes=[kt_key])
                    P.op("act", lambda e, n=n, h=h, kt_ap=kt_ap: e.activation(
                        out=kend[:, n, :], in_=kt_ap[:].rearrange("p c d -> p (c d)"), func=AF.Copy,
                        scale=pp[:, PP_ZE + h * 16 + n:PP_ZE + h * 16 + n + 1]),
                        reads=["pp"], writes=[f"kend{n}", kt_key])

                def p1_st(n):
                    def mms(e, n=n, vv=vv):
                        last = None
                        for c in range(2):
                            last = e.matmul(psS[:, c, :], lhsT=kend[:, n, c * 128:(c + 1) * 128], rhs=vv[:, n, :],
                                            start=(n == 0), stop=True, skip_group_check=True)
                        return last
                    P.op("pe", mms, reads=[f"kend{n}", f"{VV}{n}"], writes=["psS0", "psS1"])

                if h == 0:
                    for tb in range(4):
                        proj_rot(Wkv, WKV_KEYS_K, 0, kT, KT, tb)
                        for n in range(tb * 4, tb * 4 + 4):
                            p1_vproj(n)
                        if tb >= 1:
                            for n in range((tb - 1) * 4, tb * 4):
                                p1_trk(n)
                            for n in range((tb - 1) * 4, tb * 4):
                                p1_st(n)
                    for n in range(12, 16):
                        p1_trk(n)
                    for n in range(12, 16):
                        p1_st(n)
                else:
                    for g in range(4):
                        for n in range(g * 4, g * 4 + 4):
                            p1_trk(n)
                        for n in range(g * 4, g * 4 + 4):
                            p1_st(n)
                for c in range(2):
                    P.op("dve", lambda e, c=c: e.tensor_copy(out=Rloc[:, c, :], in_=psS[:, c, :]),
                         reads=[f"psS{c}"], writes=[f"Rloc{c}"])
                P.dma("sp", lambda e, s, h=h: e.dma_start(
                    out=st_in[h], in_=Rloc[:].rearrange("p c e -> p (c e)")).then_inc(s, 16),
                    reads=["Rloc0", "Rloc1"], writes=[f"st_in{h}"], semkey=f"st_in{h}")
                P.dma("pool", lambda e, s, h=h: e.collective_compute(
                    "AllGather", ALU.bypass, replica_groups=GROUPS, ins=[st_in[h]], outs=[st_out[h]]).then_inc(s, 1),
                    reads=[f"st_in{h}"], writes=[f"st_out{h}"], semkey=f"cc{h}", ninc=1)
                if h < 3:
                    load_w(Wkv, "Wkv", [(0, C_K + 256 * (h + 1), 256), (256, C_V + 512 * (h + 1), 512)])

                def p2_rproj(n):
                    rb = n % 2

                    def mmr(e, n=n, rb=rb):
                        last = None
                        for kc in range(8):
                            last = e.matmul(psB[:, rb, :], lhsT=hT[:, kc, HALO + n * 128:HALO + (n + 1) * 128],
                                            rhs=Wqr[:, kc, 256:768], start=(kc == 0), stop=(kc == 7))
                        return last
                    P.op("pe", mmr, reads=WQR_KEYS_R + ht_keys([n]), writes=[f"psB{rb}"])
                    P.op("act", lambda e, rb=rb, n=n: e.activation(out=srall[:, n, :], in_=psB[:, rb, :], func=AF.Silu),
                         reads=[f"psB{rb}"], writes=[f"sr{n}"])

                for tb in range(4):
                    proj_rot(Wqr, WQR_KEYS_Q, 0, qT, "qT", tb, ceng="dve")
                    for n in range(tb * 4, tb * 4 + 4):
                        p2_rproj(n)

                P.dma("sp", lambda e, s, h=h: e.dma_start(
                    out=Rg[:], in_=st_out[h][0:384, :].rearrange("(r p) f -> p r f", p=128)).then_inc(s, 16),
                    reads=[f"st_out{h}"], writes=["Rg"], semkey="Rg")

                def inj(e, h=h):
                    last = None
                    for c in range(2):
                        for r in range(3):
                            last = e.matmul(psS[:, c, :], lhsT=dgc[:, h * 3 + r, :], rhs=Rg[:, r, c * 512:(c + 1) * 512],
                                            start=(r == 0), stop=(r == 2), skip_group_check=True)
                    return last
                P.op("pe", inj, reads=["Rg"] + [f"dgc{h * 3 + r}" for r in range(3)] + ["Rloc0", "Rloc1"],
                     writes=["psS0", "psS1"])

                sps_h, spk = (psX[:, 0:128], "bankX") if h == 3 else (sps, "bankM")

                def p2_scores(n, h=h, sps_h=sps_h, spk=spk):
                    tb = n // 4
                    b = n % 2

                    def mmsc(e, n=n, b=b, kT=kT, sps_h=sps_h):
                        last = None
                        for c in range(2):
                            last = e.matmul(sps_h, lhsT=kT[:, c, n * 128:(n + 1) * 128], rhs=qT[:, c, n * 128:(n + 1) * 128],
                                            start=(c == 0), stop=(c == 1))
                        return last
                    P.op("pe", mmsc, reads=[f"{KT}0_{tb}", f"{KT}1_{tb}", f"qT0_{tb}", f"qT1_{tb}"], writes=[spk])
                    P.op("dve", lambda e, h=h, b=b, sps_h=sps_h: e.tensor_tensor(out=sT[b][:], in0=sps_h, in1=maskT[:, h, :],
                                                                              op=ALU.mult),
                         reads=["maskT"], writes=[f"sT{b}", spk])

                def p2_rb(n, lg=lg):
                    scl = math.exp(lg * 128.0 * (n - 16))
                    P.op("act", lambda e, scl=scl: e.activation(out=Rb[:, 0, :], in_=psS[:, 0, :], func=AF.Copy, scale=float(scl)),
                         reads=["psS0"], writes=["Rb0"])
                    P.op("dve", lambda e, scl=scl: e.tensor_scalar(out=Rb[:, 1, :], in0=psS[:, 1, :], scalar1=float(scl), scalar2=None,
                                                                   op0=ALU.mult),
                         reads=["psS1"], writes=["Rb1"])

                OB = [(psA[:, 0, :], "psA0"), (psA[:, 1, :], "psA1")]
                if h == 3:
                    OB = OB + [(psB[:, 0, :], "psB0"), (psB[:, 1, :], "psB1")]

                def p2_o(n, OB=OB):
                    tb = n // 4
                    b = n % 2
                    oap, okey = OB[n % len(OB)]

                    P.op("pe", lambda e, n=n, b=b, vv=vv, oap=oap: e.matmul(oap, lhsT=sT[b][:], rhs=vv[:, n, :],
                                                                             start=True, stop=False),
                         reads=[f"sT{b}", f"{VV}{n}"], writes=[okey])

                    def mmo(e, n=n, b=b, oap=oap):
                        last = None
                        for c in range(2):
                            last = e.matmul(oap, lhsT=qT[:, c, n * 128:(n + 1) * 128], rhs=Rb[:, c, :],
                                            start=False, stop=(c == 1))
                        return last
                    P.op("pe", mmo, reads=[f"qT0_{tb}", f"qT1_{tb}", "Rb0", "Rb1"], writes=[okey])
                    if n < NT - 1:
                        def mms2(e, n=n, vv=vv):
                            last = None
                            for c in range(2):
                                last = e.matmul(psS[:, c, :], lhsT=kend[:, n, c * 128:(c + 1) * 128], rhs=vv[:, n, :],
                                                start=False, stop=True, skip_group_check=True)
                            return last
                        P.op("pe", mms2, reads=[f"kend{n}", f"{VV}{n}", "Rb0", "Rb1"], writes=["psS0", "psS1"])

                def p2_stats_a(n, OB=OB):
                    b = n % 2
                    oap, okey = OB[n % len(OB)]
                    P.op("dve", lambda e, b=b, oap=oap: e.bn_stats(out=st6[b][:], in_=oap), reads=[okey], writes=[f"st6_{b}"])

                def p2_stats_b(n):
                    b = n % 2
                    P.op("dve", lambda e, b=b: e.bn_aggr(out=mv[b][:], in_=st6[b][:]), reads=[f"st6_{b}"], writes=[f"mv{b}"])

                def p2_norm_act1(n, h=h):
                    b = n % 2
                    P.op("act", lambda e, h=h, b=b: e.activation(
                        out=sdv[b][:], in_=mv[b][:, 1:2], func=AF.Sqrt, bias=pp[:, PP_EPS + h:PP_EPS + h + 1], scale=1.0),
                        reads=[f"mv{b}", "pp"], writes=[f"sdv{b}"])

                def p2_norm_dve(n):
                    b = n % 2
                    P.op("dve", lambda e, b=b: e.reciprocal(out=rsd[b][:], in_=sdv[b][:]), reads=[f"sdv{b}"], writes=[f"rsd{b}"])
                    P.op("dve", lambda e, b=b: e.scalar_tensor_tensor(
                        out=nmr[b][:], in0=mv[b][:, 0:1], scalar=-1.0, in1=rsd[b][:], op0=ALU.mult, op1=ALU.mult),
                        reads=[f"mv{b}", f"rsd{b}"], writes=[f"nmr{b}"])

                def p2_norm_act2(n, OB=OB):
                    b = n % 2
                    oap, okey = OB[n % len(OB)]
                    P.op("act", lambda e, b=b, oap=oap: e.activation(out=on[b][:], in_=oap, func=AF.Identity,
                                                                       bias=nmr[b][:], scale=rsd[b][:]),
                         reads=[okey, f"rsd{b}", f"nmr{b}"], writes=[f"on{b}"])
                    P.op("pool", lambda e, b=b, n=n: e.tensor_tensor(out=og[n % 4][:], in0=on[b][:], in1=srall[:, n, :], op=ALU.mult),
                         reads=[f"on{b}", f"sr{n}"], writes=[f"og{n % 4}"])

                def p2_tr(n, h=h):
                    ob = n % 4
                    P.dma("sp", lambda e, s, ob=ob, h=h, n=n: e.dma_start(
                        out=og_d[n * 128:(n + 1) * 128, h * 512:(h + 1) * 512], in_=og[ob][:]).then_inc(s, 16),
                        reads=[f"og{ob}"], writes=[f"ogd_{n}_{h}"], semkey=f"ogd{ob}")

                p2_scores(0)
                p2_rb(0)
                for n in range(NT):
                    if n + 1 < NT:
                        p2_scores(n + 1)
                    if n >= 1:
                        p2_norm_act1(n - 1)
                        p2_norm_dve(n - 1)
                    p2_o(n)
                    p2_stats_a(n)
                    if n + 1 < NT:
                        p2_rb(n + 1)
                    p2_stats_b(n)
                    if n >= 1:
                        p2_norm_act2(n - 1)
                    if n >= 2:
                        p2_tr(n - 2)
                    if h < 3:
                        if n % 4 == 0:
                            proj_rot(Wkv, WKV_KEYS_K, 0, kT2[1 - hp], f"kT{1 - hp}c", n // 4, pp_=psB, ppk="psB")
                        p1_vproj(n, par=1 - hp, use_x=True)
                p2_norm_act1(NT - 1)
                p2_norm_dve(NT - 1)
                p2_norm_act2(NT - 1)
                p2_tr(NT - 2)
                p2_tr(NT - 1)
                if h < 3:
                    load_w(Wqr, "Wqr", [(0, C_Q + 256 * (h + 1), 256), (256, C_R + 512 * (h + 1), 512)])
        P.barrier()
        st01.close()

        u2 = sb("u2", [128, 8, S_CORE], BF16)
        Wc = [sb(f"Wc{i}", [128, 8, 128], BF16) for i in range(2)]
        Wa = [sb(f"Wa{i}", [128, 8, 128], BF16) for i in range(2)]
        w_cp_v = w_cp.rearrange("(kc p) n -> p kc n", p=128)

        def load_wca(dcol):
            s = dcol % 2
            P.dma("pool", lambda e, s_, s=s, dcol=dcol: e.dma_start(
                out=Wc[s][:], in_=w_cp_v[:, :, dcol * 128:(dcol + 1) * 128]).then_inc(s_, 16),
                writes=[f"Wc{s}"], semkey=f"Wc{s}")
            P.dma("pool", lambda e, s_, s=s, dcol=dcol: e.dma_start(
                out=Wa[s][:], in_=w_in_v[:, :, C_GA + dcol * 128:C_GA + (dcol + 1) * 128]).then_inc(s_, 16),
                writes=[f"Wa{s}"], semkey=f"Wa{s}")
        with ExitStack() as st3:
            Wg = [sb(f"Wg{i}", [128, 8, 128], BF16, st=st3) for i in range(2)]

            def load_wg(cc):
                s = cc % 2
                P.dma("pool", lambda e, s_, s=s, cc=cc: e.dma_start(
                    out=Wg[s][:], in_=w_in_v[:, :, C_AGATE + cc * 128:C_AGATE + (cc + 1) * 128]).then_inc(s_, 16),
                    writes=[f"Wg{s}"], semkey=f"Wg{s}")
            cv = sb("cv", [128, 8, S_CORE], st=st3)
            convw = sb("convw", [128, 256], st=st3)
            ld("sp", convw[:], convw_d, "convw")
            i4 = sb("i4", [128, 32], st=st3)
            ld("sp", i4[:], i4_d, "i4")
            onesf = sb("onesf", [128, 128], st=st3)
            P.op("pool", lambda e: e.memset(onesf[:], 1.0 / D), writes=["onesf"])
            with ExitStack() as st3a:
                Wag = [sb(f"Wag{i}", [128, 8, 256], BF16, st=st3a) for i in range(2)]
                uT = [sb(f"uT{i}", [128, HALO + S_CORE], BF16, st=st3a) for i in range(2)]
                Ust = [sb(f"Ust{i}", [128, 4, HALO + S_CORE], BF16, st=st3a) for i in range(2)]
                Wp = [sb(f"Wp{i}", [128, 32, 32], BF16, st=st3a) for i in range(2)]
                for i in range(2):
                    P.op("pool", lambda e, i=i: e.memset(Ust[i][:, :, HALO + S_CORE - 4:HALO + S_CORE], 0.0), writes=[f"U{i}"])
                sgl = [sb(f"sgl{i}", [128, 512], st=st3a) for i in range(2)]
                def load_wag(cc):
                    s = cc % 2
                    for part, colbase in enumerate((C_AVAL, C_AGLU)):
                        P.dma("pool", lambda e, s_, s=s, part=part, colbase=colbase, cc=cc: e.dma_start(
                            out=Wag[s][:, :, part * 128:(part + 1) * 128],
                            in_=w_in_v[:, :, colbase + cc * 128:colbase + (cc + 1) * 128]).then_inc(s_, 16),
                            writes=[f"Wag{s}_{part}"], semkey=f"Wag{s}_{part}")
                def conv_mm(cc):
                    s = cc % 2
                    for tb in range(4):
                        b = tb % 2

                        def mmc(e, s=s, tb=tb, b=b):
                            last = None
                            for m in range(8):
                                o0 = 2 + 4 * m + tb * 512
                                for g in range(4):
                                    last = e.matmul(psS[32 * g:32 * (g + 1), b, :], lhsT=Wp[s][:, g * 8 + m, :],
                                                    rhs=Ust[s][:, g, o0:o0 + 512], start=(m == 0), stop=(m == 7),
                                                    tile_position=(0, 32 * g))
                            return last
                        P.op("pe", mmc, reads=[f"Wp{s}_{gm}" for gm in range(32)] + [f"U{s}"], writes=[f"psS{b}"])
                        P.op("act", lambda e, b=b, cc=cc, tb=tb: e.activation(
                            out=cv[:, cc, tb * 512:(tb + 1) * 512], in_=psS[:, b, :], func=AF.Identity,
                            bias=pp[:, PP_CB + cc:PP_CB + cc + 1], scale=1.0),
                            reads=[f"psS{b}", "pp"], writes=[f"cv{cc}_{tb}"])

                load_wag(0)
                load_wag(1)
                for cc in range(8):
                    s = cc % 2
                    for blk in range(5):
                        c0 = 0 if blk == 0 else HALO + (blk - 1) * 512
                        n = HALO if blk == 0 else 512
                        hk = ["hT_0_0", "hT_0_1"] if blk == 0 else ht_keys(range((blk - 1) * 4, (blk - 1) * 4 + 4))

                        for part in (1, 0):
                            def mma(e, s=s, c0=c0, n=n, part=part):
                                last = None
                                for kc in range(8):
                                    last = e.matmul(psA[:, part, 0:n], lhsT=Wag[s][:, kc, part * 128:(part + 1) * 128],
                                                    rhs=hT[:, kc, c0:c0 + n], start=(kc == 0), stop=(kc == 7))
                                return last
                            P.op("pe", mma, reads=[f"Wag{s}_{part}"] + hk, writes=[f"psA{part}"])
                        b = blk % 2
                        P.op("act", lambda e, b=b, n=n: e.activation(out=sgl[b][:, 0:n], in_=psA[:, 1, 0:n], func=AF.Sigmoid),
                             reads=["psA1"], writes=[f"sgl{b}"])
                        P.op("dve", lambda e, b=b, n=n, s=s, c0=c0: e.tensor_tensor(
                            out=uT[s][:, c0:c0 + n], in0=psA[:, 0, 0:n], in1=sgl[b][:, 0:n], op=ALU.mult),
                            reads=["psA0", f"sgl{b}"], writes=[f"uT{s}_{blk}"])
                    for gm in range(32):
                        P.op("dve", lambda e, s=s, gm=gm, cc=cc: e.tensor_scalar(
                            out=Wp[s][:, gm, :], in0=i4[:], scalar1=convw[:, cc * 32 + gm:cc * 32 + gm + 1], scalar2=None,
                            op0=ALU.mult), reads=["i4", "convw"], writes=[f"Wp{s}_{gm}"])
                    def mku(e, s_, s=s):
                        for g in range(4):
                            for j in range(4):
                                e.dma_start(out=Ust[s][j * 32:(j + 1) * 32, g, 0:HALO + S_CORE - j],
                                            in_=uT[s][g * 32:(g + 1) * 32, j:HALO + S_CORE]).then_inc(s_, 16)
                    P.dma("sp", mku, reads=[f"uT{s}_{i}" for i in range(5)], writes=[f"U{s}"], semkey=f"U{s}", ninc=256)
                    if cc >= 1:
                        conv_mm(cc - 1)
                    if cc + 2 < 8:
                        load_wag(cc + 2)
                conv_mm(7)
            P.barrier()
            rstd_t = sb("rstd_t", [128, S_CORE], st=st3)
            nmr_t = sb("nmr_t", [128, S_CORE], st=st3)
            sga = [sb(f"sga{i}", [128, 512], st=st3) for i in range(2)]
            t1 = [sb(f"t1_{i}", [128, 512], st=st3) for i in range(2)]
            t2 = [sb(f"t2_{i}", [128, 512], st=st3) for i in range(2)]
            zz = [sb(f"zz{i}", [128, 512], st=st3) for i in range(2)]
            with ExitStack() as st3b:
                load_wg(0)
                load_wg(1)
                sq = [sb(f"sq{i}", [128, 512], BF16, st=st3b) for i in range(2)]
                cvb = [sb(f"cvb{i}", [128, 512], BF16, st=st3b) for i in range(2)]
                onesb = sb("onesb", [128, 128], BF16, st=st3b)
                P.op("dve", lambda e: e.tensor_copy(out=onesb[:], in_=onesf[:]), reads=["onesf"], writes=["onesb"])
                mean_t2 = [sb(f"mean_t{i}", [128, 512], st=st3b) for i in range(2)]
                msq_t = sb("msq_t", [128, 512], st=st3b)
                LNB = [(psA, "psA"), (psB, "psB")]

                def ln_front(tb):
                    pt, pk = LNB[tb % 2]
                    for cc in range(8):
                        b = cc % 2
                        P.op("dve", lambda e, b=b, cc=cc, tb=tb: e.tensor_copy(
                            out=cvb[b][:], in_=cv[:, cc, tb * 512:(tb + 1) * 512]),
                            reads=[f"cv{cc}_{tb}"], writes=[f"cvb{b}"])
                        P.op("pe", lambda e, b=b, cc=cc, pt=pt: e.matmul(pt[:, 0, :], lhsT=onesb[:], rhs=cvb[b][:],
                                                                         start=(cc == 0), stop=(cc == 7)),
                             reads=["onesb", f"cvb{b}"], writes=[pk + "0"])
                        P.op("act", lambda e, b=b, cc=cc, tb=tb: e.activation(
                            out=sq[b][:], in_=cv[:, cc, tb * 512:(tb + 1) * 512], func=AF.Square),
                            reads=[f"cv{cc}_{tb}"], writes=[f"sq{b}"])
                        P.op("pe", lambda e, b=b, cc=cc, pt=pt: e.matmul(pt[:, 1, :], lhsT=onesb[:], rhs=sq[b][:],
                                                                         start=(cc == 0), stop=(cc == 7)),
                             reads=["onesb", f"sq{b}"], writes=[pk + "1"])

                def ln_back(tb):
                    pt, pk = LNB[tb % 2]
                    sl = slice(tb * 512, (tb + 1) * 512)
                    mean_t = mean_t2[tb % 2]
                    mk = f"mean_t{tb % 2}"
                    P.op("dve", lambda e, pt=pt, mean_t=mean_t: e.tensor_copy(out=mean_t[:], in_=pt[:, 0, :]), reads=[pk + "0"], writes=[mk])
                    P.op("dve", lambda e, mean_t=mean_t: e.tensor_tensor(out=msq_t[:], in0=mean_t[:], in1=mean_t[:], op=ALU.mult),
                         reads=[mk], writes=["msq_t"])
                    P.op("dve", lambda e, pt=pt: e.tensor_tensor(out=msq_t[:], in0=pt[:, 1, :], in1=msq_t[:], op=ALU.subtract),
                         reads=[pk + "1", "msq_t"], writes=["msq_t"])
                    P.op("act", lambda e: e.activation(out=msq_t[:], in_=msq_t[:], func=AF.Ln, bias=eps5[:], scale=1.0),
                         reads=["msq_t", "eps5"], writes=["msq_t"])
                    P.op("act", lambda e, sl=sl: e.activation(out=rstd_t[:, sl], in_=msq_t[:], func=AF.Exp, scale=-0.5),
                         reads=["msq_t"], writes=[f"rstd_t{tb}"])
                    P.op("pool", lambda e, sl=sl, mean_t=mean_t: e.tensor_tensor(out=nmr_t[:, sl], in0=mean_t[:], in1=rstd_t[:, sl],
                                                                                op=ALU.mult),
                         reads=[mk, f"rstd_t{tb}"], writes=[f"nmr_t{tb}"])

                for tb in range(5):
                    if tb < 4:
                        ln_front(tb)
                    if tb >= 1:
                        ln_back(tb - 1)
            with ExitStack() as st3c:
                load_wca(0)
                load_wca(1)
                items = [(cc, tb) for cc in range(8) for tb in range(4)]

                def n_front(i):
                    cc, tb = items[i]
                    s = cc % 2
                    b = i % 2
                    c0 = HALO + tb * 512
                    sl = slice(tb * 512, (tb + 1) * 512)

                    def mmg2(e, s=s, b=b, c0=c0):
                        last = None
                        for kc in range(8):
                            last = e.matmul(psB[:, b, :], lhsT=Wg[s][:, kc, :], rhs=hT[:, kc, c0:c0 + 512],
                                            start=(kc == 0), stop=(kc == 7))
                        return last
                    P.op("pe", mmg2, reads=[f"Wg{s}"] + ht_keys(range(tb * 4, tb * 4 + 4)), writes=[f"psB{b}"])
                    P.op("dve", lambda e, b=b, cc=cc, sl=sl: e.tensor_tensor(
                        out=t1[b][:], in0=cv[:, cc, sl], in1=rstd_t[:, sl], op=ALU.mult),
                        reads=[f"cv{cc}_{tb}", f"rstd_t{tb}"], writes=[f"t1_{b}"])
                    P.op("dve", lambda e, b=b, sl=sl: e.tensor_tensor(
                        out=t2[b][:], in0=t1[b][:], in1=nmr_t[:, sl], op=ALU.subtract),
                        reads=[f"t1_{b}", f"nmr_t{tb}"], writes=[f"t2_{b}"])

                def n_back(i):
                    cc, tb = items[i]
                    b = i % 2
                    sl = slice(tb * 512, (tb + 1) * 512)
                    P.op("act", lambda e, b=b: e.activation(out=sga[b][:], in_=psB[:, b, :], func=AF.Silu),
                         reads=[f"psB{b}"], writes=[f"sga{b}"])
                    P.op("act", lambda e, b=b, cc=cc: e.activation(
                        out=zz[b][:], in_=t2[b][:], func=AF.Silu,
                        bias=pp[:, PP_LB + cc:PP_LB + cc + 1], scale=pp[:, PP_LG + cc:PP_LG + cc + 1]),
                        reads=[f"t2_{b}", "pp"], writes=[f"zz{b}"])
                    P.op("dve", lambda e, b=b, cc=cc, sl=sl: e.tensor_tensor(
                        out=u2[:, cc, sl], in0=zz[b][:], in1=sga[b][:], op=ALU.mult),
                        reads=[f"zz{b}", f"sga{b}"], writes=[f"u2_{cc}_{tb}"])

                for i in range(len(items) + 1):
                    if i < len(items):
                        n_front(i)
                    if i >= 1:
                        n_back(i - 1)
                    if i % 4 == 3 and i // 4 + 2 < 8:
                        load_wg(i // 4 + 2)
        P.barrier()
        mT = sb("mT", [128, 8, S_CORE], BF16)
        Wo = sb("Wo", [128, 8, D], BF16)
        with ExitStack() as st5:
            Wrp = sb("Wrp", [128, 16, D], BF16, st=st5)
            Wgb = sb("Wgb", [128, 8, D], BF16, st=st5)
            def load_wgb(half):
                P.dma("pool", lambda e, s, half=half: e.dma_start(
                    out=Wgb[:, :, half * 512:(half + 1) * 512],
                    in_=w_in_v[:, :, C_GB + half * 512:C_GB + (half + 1) * 512]).then_inc(s, 16),
                    writes=[f"Wgb{half}"], semkey=f"Wgb{half}")

            def load_wrp(ec):
                s = ec % 2
                ld("sp", wst[s][:], w_rp[ec * 128:(ec + 1) * 128, :], f"wst{s}")
                P.op("act", lambda e, s=s, ec=ec: e.activation(out=Wrp[:, ec, :], in_=wst[s][:], func=AF.Copy,
                                                               scale=pp[:, PP_GN + ec:PP_GN + ec + 1]),
                     reads=[f"wst{s}", "pp"], writes=[f"Wrp{ec}"])
            WRP_KEYS = [f"Wrp{ec}" for ec in range(16)]
            ogT0 = sb("ogT0", [128, 16, 512], BF16, st=st5)

            def rd_ogT_op(dst, key, tb):
                def rd_ogT(e, s_, dst=dst, tb=tb):
                    for ec in range(16):
                        e.dma_start_transpose(out=dst[:, ec, :],
                                              in_=og_d[tb * 512:(tb + 1) * 512, ec * 128:(ec + 1) * 128]).then_inc(s_, 16)
                P.dma("sp", rd_ogT, reads=[f"ogd_{n}_{h}" for n in range(tb * 4, tb * 4 + 4) for h in range(4)],
                      writes=[key], semkey=key, ninc=256)
            with ExitStack() as st3d:
                wst = [sb(f"wst{i}", [128, D], st=st3d) for i in range(2)]
                sgq = [sb(f"sgq{i}", [128, 512], st=st3d) for i in range(2)]
                ta = [sb(f"ta{i}", [128, 512], st=st3d) for i in range(2)]
                it = 0
                for dcol in range(8):
                    s = dcol % 2
                    for tb in range(4):
                        b = it % 2
                        it += 1
                        c0 = HALO + tb * 512
                        sl = slice(tb * 512, (tb + 1) * 512)

                        def mmga(e, s=s, b=b, c0=c0):
                            last = None
                            for kc in range(8):
                                last = e.matmul(psB[:, b, :], lhsT=Wa[s][:, kc, :], rhs=hT[:, kc, c0:c0 + 512],
                                                start=(kc == 0), stop=(kc == 7))
                            return last
                        P.op("pe", mmga, reads=[f"Wa{s}"] + ht_keys(range(tb * 4, tb * 4 + 4)), writes=[f"psB{b}"])
                        P.op("act", lambda e, b=b: e.activation(out=sgq[b][:], in_=psB[:, b, :], func=AF.Sigmoid),
                             reads=[f"psB{b}"], writes=[f"sgq{b}"])

                        def mmya(e, s=s, b=b, sl=sl):
                            last = None
                            for cc in range(8):
                                last = e.matmul(psA[:, b, :], lhsT=Wc[s][:, cc, :], rhs=u2[:, cc, sl],
                                                start=(cc == 0), stop=(cc == 7))
                            return last
                        P.op("pe", mmya, reads=[f"Wc{s}"] + [f"u2_{cc}_{tb}" for cc in range(8)], writes=[f"psA{b}"])
                        P.op("dve", lambda e, b=b, dcol=dcol, sl=sl: e.tensor_tensor(
                            out=mT[:, dcol, sl], in0=psA[:, b, :], in1=sgq[b][:], op=ALU.mult),
                            reads=[f"psA{b}", f"sgq{b}"], writes=[f"mT{dcol}_{tb}"])
                    if dcol + 2 < 8:
                        load_wca(dcol + 2)
                    load_wrp(2 * dcol)
                    load_wrp(2 * dcol + 1)
                    if dcol in (5, 6):
                        load_wgb(dcol - 5)
                    if dcol == 3:
                        rd_ogT_op(ogT0, "ogT0", 0)
            P.barrier()
            with ExitStack() as st2:
                ogT = [ogT0, sb("ogT1", [128, 16, 512], BF16, st=st2)]
                sg0 = sb("sg0", [128, 512], st=st2)
                tb0 = sb("tb_0", [128, 512], st=st2)
                sg = [sg0, sg0]
                tb_ = [tb0, tb0]
                for tb in range(4):
                    s = tb % 2
                    if tb == 0:
                        rd_ogT_op(ogT[1], "ogT1", 1)
                    elif tb + 1 < 4:
                        rd_ogT_op(ogT[(tb + 1) % 2], f"ogT{(tb + 1) % 2}", tb + 1)
                    c0 = HALO + tb * 512
                    for dcol in range(8):
                        b = dcol % 2

                        def mmg(e, dcol=dcol, b=b, c0=c0):
                            last = None
                            for kc in range(8):
                                last = e.matmul(psB[:, b, :], lhsT=Wgb[:, kc, dcol * 128:(dcol + 1) * 128],
                                                rhs=hT[:, kc, c0:c0 + 512], start=(kc == 0), stop=(kc == 7))
                            return last
                        P.op("pe", mmg, reads=["Wgb0", "Wgb1"] + ht_keys(range(tb * 4, tb * 4 + 4)), writes=[f"psB{b}"])
                        P.op("act", lambda e, b=b: e.activation(out=sg[b][:], in_=psB[:, b, :], func=AF.Sigmoid),
                             reads=[f"psB{b}"], writes=["sg0"])

                        def mmy(e, dcol=dcol, b=b, s=s):
                            last = None
                            for ec in range(16):
                                last = e.matmul(psA[:, b, :], lhsT=Wrp[:, ec, dcol * 128:(dcol + 1) * 128],
                                                rhs=ogT[s][:, ec, :], start=(ec == 0), stop=(ec == 15))
                            return last
                        P.op("pe", mmy, reads=WRP_KEYS + [f"ogT{s}"], writes=[f"psA{b}"])
                        P.op("dve", lambda e, b=b: e.tensor_tensor(out=tb_[b][:], in0=psA[:, b, :], in1=sg[b][:], op=ALU.mult),
                             reads=[f"psA{b}", "sg0"], writes=["tb_0"])
                        P.op("pool", lambda e, dcol=dcol, b=b, tb=tb: e.tensor_tensor(
                            out=mT[:, dcol, tb * 512:(tb + 1) * 512], in0=tb_[b][:], in1=mT[:, dcol, tb * 512:(tb + 1) * 512],
                            op=ALU.add),
                            reads=["tb_0", f"mT{dcol}_{tb}"], writes=[f"mT{dcol}_{tb}"])
                    if tb == 1:
                        w_out_v = w_out.rearrange("(kc p) n -> p kc n", p=128)
                        for half in range(2):
                            P.dma("pool", lambda e, s_, half=half: e.dma_start(
                                out=Wo[:, :, half * 512:(half + 1) * 512],
                                in_=w_out_v[:, :, half * 512:(half + 1) * 512]).then_inc(s_, 16),
                                writes=[f"Wo{half}"], semkey=f"Wo{half}")

        P.barrier()

        with ExitStack() as st4:
            fgainB = sb("fgainB", [128, D], st=st4)
            ld("sp", fgainB[:], fgainB_d, "fgainB")
            xr = [sb(f"xr{i}", [128, D], st=st4) for i in range(3)]
            yy = [sb(f"yy{i}", [128, D], st=st4) for i in range(2)]
            oo = [sb(f"oo{i}", [128, D], st=st4) for i in range(2)]
            junk2 = sb("junk2", [128, D], st=st4)
            ss2 = [sb(f"ss2_{i}", [128, 1], st=st4) for i in range(2)]
            rs2 = [sb(f"rs2_{i}", [128, 1], st=st4) for i in range(2)]
            pso = [psA, psB]
            out_keys = []

            def o_load(n):
                ld("sp", xr[n % 3][:], x[n * 128:(n + 1) * 128, :], f"xr{n % 3}")

            def o_mm(n):
                s = n % 2
                tb = n // 4

                def mmo2(e, n=n, s=s):
                    last = None
                    for half in range(2):
                        for dc in range(8):
                            last = e.matmul(pso[s][:, half, :], lhsT=mT[:, dc, n * 128:(n + 1) * 128],
                                            rhs=Wo[:, dc, half * 512:(half + 1) * 512], start=(dc == 0), stop=(dc == 7))
                    return last
                P.op("pe", mmo2, reads=["Wo0", "Wo1"] + [f"mT{dc}_{tb}" for dc in range(8)],
                     writes=[f"ps{'AB'[s]}0", f"ps{'AB'[s]}1"])

            def o_y(n):
                s = n % 2
                for half in range(2):
                    P.op("dve", lambda e, s=s, half=half, n=n: e.tensor_tensor(
                        out=yy[s][:, half * 512:(half + 1) * 512], in0=pso[s][:, half, :],
                        in1=xr[n % 3][:, half * 512:(half + 1) * 512], op=ALU.add),
                        reads=[f"ps{'AB'[s]}{half}", f"xr{n % 3}"], writes=[f"yy{s}_{half}"])
                P.op("act", lambda e, s=s: e.activation(out=junk2[:], in_=yy[s][:], func=AF.Square, accum_out=ss2[s][:]),
                     reads=[f"yy{s}_0", f"yy{s}_1"], writes=["junk2", f"ss2_{s}"])
                P.op("act", lambda e, s=s: e.activation(out=ss2[s][:], in_=ss2[s][:], func=AF.Sqrt, bias=eps6[:], scale=1.0 / D),
                     reads=[f"ss2_{s}", "eps6"], writes=[f"ss2_{s}"])

            def o_z(n):
                s = n % 2
                P.op("dve", lambda e, s=s: e.reciprocal(out=rs2[s][:], in_=ss2[s][:]),
                     reads=[f"ss2_{s}"], writes=[f"rs2_{s}"])
                P.op("dve", lambda e, s=s: e.scalar_tensor_tensor(
                    out=oo[s][:], in0=yy[s][:], scalar=rs2[s][:], in1=fgainB[:], op0=ALU.mult, op1=ALU.mult),
                    reads=[f"yy{s}_0", f"yy{s}_1", f"rs2_{s}", "fgainB"], writes=[f"oo{s}"])
                P.dma("sp", lambda e, s_, s=s, n=n: e.dma_start(out=out_d[n * 128:(n + 1) * 128, :], in_=oo[s][:]).then_inc(s_, 16),
                      reads=[f"oo{s}"], writes=[f"out{n}"], semkey=f"outd{s}")
                out_keys.append(f"out{n}")

            o_load(0)
            o_load(1)
            o_mm(0)
            for n in range(NT + 1):
                if n + 2 < NT:
                    o_load(n + 2)
                if n + 1 < NT:
                    o_mm(n + 1)
                if n < NT:
                    o_y(n)
                if n >= 1:
                    o_z(n - 1)
            P.op("sp", None, reads=out_keys)
        P.emit(nc, gst)
    return nc


_NC_CACHE = {}


def _consts(j):
    LG = log_gammas()
    pos = (j * S_CORE + np.arange(S_CORE, dtype=np.float64))
    inv_freq = 10000.0 ** (-np.arange(0, 256, 2, dtype=np.float64) / 256.0)
    ang = inv_freq[:, None] * pos[None, :]
    cosT = np.cos(ang).astype(np.float32)
    sinT = np.sin(ang).astype(np.float32)
    pc = np.zeros((128, NPP), np.float64)
    p = np.arange(128, dtype=np.float64)
    for h in range(4):
        for n in range(NT):
            pc[:, PP_ZE + h * 16 + n] = np.exp(LG[h] * (2047.0 - (128.0 * n + p))) / 16.0
        pc[:, PP_EPS + h] = 1e-5 / np.exp(2.0 * LG[h] * (p + 1.0))
        for r in range(3):
            pc[:, PP_COEF + h * 3 + r] = math.exp(LG[h] * 2048.0 * (j - r)) if r < j else 0.0
    mask = np.zeros((128, 4, 128), np.float64)
    jj = np.arange(128)[:, None]
    ii = np.arange(128)[None, :]
    for h in range(4):
        mask[:, h, :] = np.where(jj <= ii, np.exp(-LG[h] * (jj + 1.0)) / 16.0, 0.0)
    return cosT, sinT, pc, mask.reshape(128, 512).astype(np.float32)


def kernel(x, norm_gain, w_in, conv_dw_w, conv_dw_b, conv_ln_g, conv_ln_b,
           w_conv_proj, ret_gn_g, w_ret_proj, w_out, final_gain):
    x = np.asarray(x, np.float32)
    f = lambda a: np.ascontiguousarray(np.asarray(a, np.float32))
    w_in0 = f(w_in[0])
    w_cp0 = f(w_conv_proj[0])
    w_rp0 = f(w_ret_proj[0])
    w_out0 = f(w_out[0])
    gainB = f(np.broadcast_to(np.asarray(norm_gain, np.float32)[0][None, :], (128, D)))
    fgainB = f(np.broadcast_to(np.asarray(final_gain, np.float32)[None, :], (128, D)))
    cw = np.zeros((32, D), np.float32)
    cw[:31] = np.asarray(conv_dw_w, np.float32)[0]
    convw = f(cw.reshape(8, 4, 8, 4, 32).transpose(1, 4, 2, 3, 0).reshape(128, 256))
    i4 = f(np.tile(np.eye(32, dtype=np.float32), (4, 1)))

    def colT(v, nch):
        return np.asarray(v, np.float32).reshape(nch, 128).T

    ident = np.eye(128, dtype=np.float32)
    if "nc" not in _NC_CACHE:
        _NC_CACHE["nc"] = build_program()
    nc = _NC_CACHE["nc"]
    in_maps = []
    for c in range(8):
        b, j = c // 4, c % 4
        cosT, sinT, pc, mask = _consts(j)
        pc[:, PP_CB:PP_CB + 8] = colT(conv_dw_b[0], 8)
        pc[:, PP_LG:PP_LG + 8] = colT(conv_ln_g[0], 8)
        pc[:, PP_LB:PP_LB + 8] = colT(conv_ln_b[0], 8)
        pc[:, PP_GN:PP_GN + 16] = colT(ret_gn_g[0], 16)
        xs = f(x[b, j * S_CORE:(j + 1) * S_CORE, :])
        if j == 0:
            xhalo = np.zeros((HALO, D), np.float32)
        else:
            xhalo = f(x[b, j * S_CORE - HALO:j * S_CORE, :])
        in_maps.append({
            "x": xs, "xh": xhalo, "w_in": w_in0, "w_cp": w_cp0, "w_rp": w_rp0, "w_out": w_out0,
            "gainB": gainB, "fgainB": fgainB, "pp": f(pc.astype(np.float32)), "convw": convw,
            "maskT": mask, "ident": ident, "cosT": cosT, "sinT": sinT, "i4": i4,
        })
    res = run_bass_kernel_spmd(nc, in_maps, core_ids=list(range(8)))
    out = np.empty((2, 4 * S_CORE, D), np.float32)
    for c in range(8):
        b, j = c // 4, c % 4
        out[b, j * S_CORE:(j + 1) * S_CORE, :] = res.results[c]["out"]
    return out
```

```python
import math
import numpy as np
from contextlib import ExitStack
import concourse.bass as bass
import concourse.mybir as mybir
from concourse.bass_utils import run_bass_kernel_spmd

F32 = mybir.dt.float32
BF16 = mybir.dt.bfloat16
AF = mybir.ActivationFunctionType
ALU = mybir.AluOpType

ENG_NAMES = ("pe", "act", "dve", "pool", "sp")

S_CORE = 2048
NT = 16
D = 1024
HALO = 32
NCOLS = 11264
C_AVAL, C_AGLU, C_AGATE, C_Q, C_K, C_V, C_R, C_GA, C_GB = 0, 1024, 2048, 3072, 4096, 5120, 7168, 9216, 10240
PP_CB, PP_LG, PP_LB, PP_GN, PP_ZE, PP_EPS, PP_COEF, NPP = 0, 8, 16, 24, 40, 104, 108, 120
GROUPS = [[0, 1, 2, 3], [4, 5, 6, 7]]


class Op:
    __slots__ = ("eng", "fn", "reads", "writes", "is_dma", "ninc", "deps",
                 "needs_inc", "sem", "val", "idx", "semkey", "barrier")

    def __init__(self, eng, fn, reads, writes, is_dma, ninc, semkey):
        self.eng = eng
        self.fn = fn
        self.reads = tuple(reads)
        self.writes = tuple(writes)
        self.is_dma = is_dma
        self.ninc = ninc
        self.deps = ()
        self.needs_inc = False
        self.sem = None
        self.val = 0
        self.semkey = semkey
        self.barrier = False


class Prog:
    def __init__(self):
        self.ops = []

    def op(self, eng, fn, reads=(), writes=()):
        self.ops.append(Op(eng, fn, reads, writes, False, 1, None))

    def dma(self, eng, fn, reads=(), writes=(), semkey=None, ninc=16):
        assert semkey is not None
        self.ops.append(Op(eng, fn, reads, writes, True, ninc, semkey))

    def barrier(self):
        o = Op(None, None, (), (), False, 0, None)
        o.barrier = True
        self.ops.append(o)

    def finalize(self):
        last_w = {}
        readers = {}
        last_on = {}
        pending = {e: [] for e in ENG_NAMES}
        for i, o in enumerate(self.ops):
            o.idx = i
            if o.barrier:
                lst = list(last_on.values())
                for e in ENG_NAMES:
                    pending[e] = list(lst)
                continue
            deps = set()
            for r in o.reads:
                if r in last_w:
                    deps.add(last_w[r])
            for w in o.writes:
                if w in last_w:
                    deps.add(last_w[w])
                for rd in readers.get(w, ()):
                    deps.add(rd)
            if pending[o.eng]:
                deps.update(pending[o.eng])
                pending[o.eng] = []
            deps.discard(i)
            real = []
            for d in deps:
                dop = self.ops[d]
                if dop.fn is None:
                    continue
                if dop.is_dma or dop.eng != o.eng or o.eng != "pe" or o.is_dma:
                    real.append(d)
                    dop.needs_inc = True
            o.deps = tuple(sorted(real))
            for w in o.writes:
                last_w[w] = i
                readers[w] = []
            for r in o.reads:
                readers.setdefault(r, []).append(i)
            if o.fn is not None:
                last_on[("dma", o.semkey) if o.is_dma else ("eng", o.eng)] = i
        cnt = {}
        for o in self.ops:
            if o.barrier:
                continue
            if o.is_dma:
                key = ("dma", o.semkey)
                cnt[key] = cnt.get(key, 0) + o.ninc
                o.val = cnt[key]
            elif o.needs_inc:
                key = ("eng", o.eng)
                cnt[key] = cnt.get(key, 0) + 1
                o.val = cnt[key]
        self.dma_keys = sorted({o.semkey for o in self.ops if o.is_dma}, key=str)

    def emit(self, nc, stack):
        self.finalize()
        sems = {}
        for e in ENG_NAMES:
            sems[("eng", e)] = stack.enter_context(nc.semaphore("s_" + e))
        for k in self.dma_keys:
            sems[("dma", k)] = stack.enter_context(nc.semaphore("d_" + str(k)))
        for o in self.ops:
            if o.barrier:
                continue
            o.sem = sems[("dma", o.semkey)] if o.is_dma else sems[("eng", o.eng)]
        block = stack.enter_context(nc.Block())
        by_eng = {e: [o for o in self.ops if o.eng == e] for e in ENG_NAMES}
        ops = self.ops

        def run(engine, lst):
            waited = {}
            for o in lst:
                need = {}
                for d in o.deps:
                    dop = ops[d]
                    k = id(dop.sem)
                    if dop.val > need.get(k, (0, None))[0]:
                        need[k] = (dop.val, dop.sem)
                for k, (v, s) in need.items():
                    if waited.get(k, 0) < v:
                        engine.wait_ge(s, v)
                        waited[k] = v
                if o.fn is None:
                    continue
                if o.is_dma:
                    o.fn(engine, o.sem)
                else:
                    ins = o.fn(engine)
                    if o.needs_inc:
                        ins.then_inc(o.sem, 1)

        @block.tensor
        def _(e):
            run(e, by_eng["pe"])

        @block.scalar
        def _(e):
            run(e, by_eng["act"])

        @block.vector
        def _(e):
            run(e, by_eng["dve"])

        @block.gpsimd
        def _(e):
            run(e, by_eng["pool"])

        @block.sync
        def _(e):
            run(e, by_eng["sp"])


def log_gammas():
    return [math.log1p(-2.0 ** (-5.0 - h)) for h in range(4)]


def build_program():
    nc = bass.Bass("TRN2", target_bir_lowering=False)
    P = Prog()
    LG = log_gammas()

    def din(name, shape, dt=F32):
        return nc.dram_tensor(name, shape, dt, kind="ExternalInput").ap()

    x = din("x", [S_CORE, D])
    xh = din("xh", [HALO, D])
    w_in = din("w_in", [D, NCOLS])
    w_cp = din("w_cp", [D, D])
    w_rp = din("w_rp", [2 * D, D])
    w_out = din("w_out", [D, D])
    gainB_d = din("gainB", [128, D])
    fgainB_d = din("fgainB", [128, D])
    pp_d = din("pp", [128, NPP])
    convw_d = din("convw", [128, 256])
    i4_d = din("i4", [128, 32])
    mask_d = din("maskT", [128, 4 * 128])
    ident_d = din("ident", [128, 128])
    cos_d = din("cosT", [128, S_CORE])
    sin_d = din("sinT", [128, S_CORE])
    out_d = nc.dram_tensor("out", [S_CORE, D], F32, kind="ExternalOutput").ap()
    st_in = [nc.dram_tensor(f"st_in{h}", [128, 1024], F32, kind="Internal").ap() for h in range(4)]
    st_out = [nc.dram_tensor(f"st_out{h}", [4 * 128, 1024], F32, kind="Internal").ap() for h in range(4)]
    og_d = nc.dram_tensor("og_spill", [S_CORE, 2 * D], BF16, kind="Internal").ap()

    w_in_v = w_in.rearrange("(kc p) n -> p kc n", p=128)

    with ExitStack() as gst:
        def sb(name, shape, dt=F32, st=None):
            return (st or gst).enter_context(nc.sbuf_tensor("sb_" + name, shape, dt))

        def pst(name, shape, dt=F32):
            return gst.enter_context(nc.psum_tensor("ps_" + name, shape, dt))

        hT = sb("hT", [128, 8, HALO + S_CORE], BF16)
        ident = sb("ident", [128, 128])
        identb = sb("identb", [128, 128], BF16)
        pp = sb("pp", [128, NPP])
        eps6 = sb("eps6", [128, 1])
        eps5 = sb("eps5", [128, 1])
        psA = pst("psA", [128, 2, 512])
        psB = pst("psB", [128, 2, 512])
        psS = pst("psS", [128, 2, 512])
        psX = pst("psX", [128, 512])
        psM = pst("psM", [128, 512])
        sps = psM[:, 128:256]
        otr = psM[:, 256:512].bitcast(BF16).rearrange("p (c d) -> p c d", c=4)
        ktr = [psX[:, 0:128].bitcast(BF16).rearrange("p (c d) -> p c d", c=2),
               psM[:, 0:128].bitcast(BF16).rearrange("p (c d) -> p c d", c=2)]
        KTRB = ["bankX", "bankM"]
        ktr4 = ktr + [psB[:, 0, 0:128].bitcast(BF16).rearrange("p (c d) -> p c d", c=2),
                      psB[:, 1, 0:128].bitcast(BF16).rearrange("p (c d) -> p c d", c=2)]
        KTRB4 = KTRB + ["psB0", "psB1"]

        def ld(q, dst, src, key, reads=()):
            P.dma(q, lambda e, s: e.dma_start(out=dst, in_=src).then_inc(s, 16),
                  reads=reads, writes=[key], semkey=key)

        ld("sp", ident[:], ident_d, "ident")
        ld("sp", pp[:], pp_d, "pp")
        P.op("dve", lambda e: e.tensor_copy(out=identb[:], in_=ident[:]), reads=["ident"], writes=["identb"])
        P.op("pool", lambda e: e.memset(eps6[:], 1e-6), writes=["eps6"])
        P.op("pool", lambda e: e.memset(eps5[:], 1e-5), writes=["eps5"])

        st01 = ExitStack()
        st01.__enter__()
        Wkv = sb("Wkv", [128, 8, 768], BF16, st=st01)
        Wqr = sb("Wqr", [128, 8, 768], BF16, st=st01)

        def load_w(buf, key, cols):
            for (do, sc, n) in cols:
                P.dma("pool", lambda e, s, do=do, sc=sc, n=n: e.dma_start(
                    out=buf[:, :, do:do + n], in_=w_in_v[:, :, sc:sc + n]).then_inc(s, 16),
                    writes=[f"{key}_{do}"], semkey=f"{key}_{do}")
        load_w(Wkv, "Wkv", [(0, C_K, 256), (256, C_V, 512)])
        with ExitStack() as st0:
            gainB = sb("gainB", [128, D], st=st0)
            xs = [sb(f"xs{i}", [128, D], st=st0) for i in range(3)]
            xg = [sb(f"xg{i}", [128, D], st=st0) for i in range(3)]
            junk = sb("junk", [128, D], st=st0)
            ss = [sb(f"ss{i}", [128, 1], st=st0) for i in range(3)]
            rs = [sb(f"rs{i}", [128, 1], st=st0) for i in range(3)]
            dg = [sb(f"dg{i}", [128, 128], st=st0) for i in range(3)]
            ld("sp", gainB[:], gainB_d, "gainB")
            pss = [psA, psB, psS]
            def p0_a(t):
                rows = HALO if t == 0 else 128
                src = xh if t == 0 else x[(t - 1) * 128:t * 128, :]
                s = t % 3
                ld("sp", xs[s][0:rows, :], src, f"xs{s}")
                P.op("act", lambda e, s=s, rows=rows: e.activation(
                    out=junk[0:rows, :], in_=xs[s][0:rows, :], func=AF.Square, accum_out=ss[s][0:rows, :]),
                    reads=[f"xs{s}"], writes=["junk", f"ss{s}"])
                P.op("act", lambda e, s=s, rows=rows: e.activation(
                    out=ss[s][0:rows, :], in_=ss[s][0:rows, :], func=AF.Sqrt, bias=eps6[0:rows, :], scale=1.0 / D),
                    reads=[f"ss{s}", "eps6"], writes=[f"ss{s}"])
                P.op("dve", lambda e, s=s, rows=rows: e.reciprocal(out=rs[s][0:rows, :], in_=ss[s][0:rows, :]),
                     reads=[f"ss{s}"], writes=[f"rs{s}"])
                P.op("dve", lambda e, s=s, rows=rows: e.tensor_scalar(
                    out=dg[s][0:rows, 0:rows], in0=ident[0:rows, 0:rows], scalar1=rs[s][0:rows, :], scalar2=None,
                    op0=ALU.mult), reads=[f"rs{s}", "ident"], writes=[f"dg{s}"])
                P.op("pool", lambda e, s=s, rows=rows: e.tensor_tensor(
                    out=xg[s][0:rows, :], in0=xs[s][0:rows, :], in1=gainB[0:rows, :], op=ALU.mult),
                    reads=[f"xs{s}", "gainB"], writes=[f"xg{s}"])

            def p0_b(t):
                rows = HALO if t == 0 else 128
                col0 = 0 if t == 0 else HALO + (t - 1) * 128
                s = t % 3

                def tr(e, s=s, rows=rows):
                    last = None
                    for dc in range(8):
                        last = e.matmul(pss[s][:, dc // 4, (dc % 4) * 128:(dc % 4) * 128 + rows],
                                        lhsT=xg[s][0:rows, dc * 128:(dc + 1) * 128],
                                        rhs=dg[s][0:rows, 0:rows], start=True, stop=True)
                    return last
                P.op("pe", tr, reads=[f"xg{s}", f"dg{s}"], writes=[f"ps{'ABS'[s]}0", f"ps{'ABS'[s]}1"])
                for half in range(2):
                    eng = "act" if half == 0 else "dve"

                    def ev(e, s=s, rows=rows, half=half, col0=col0, eng=eng):
                        src_ap = pss[s][:, half, :].rearrange("p (c t) -> p c t", c=4)[:, :, 0:rows]
                        dst_ap = hT[:, half * 4:(half + 1) * 4, col0:col0 + rows]
                        if eng == "act":
                            return e.activation(out=dst_ap, in_=src_ap, func=AF.Copy)
                        return e.tensor_copy(out=dst_ap, in_=src_ap)
                    P.op(eng, ev, reads=[f"ps{'ABS'[s]}{half}"], writes=[f"hT_{t}_{half}"])

            p0_a(0)
            p0_a(1)
            for t in range(NT + 1):
                if t + 2 < NT + 1:
                    p0_a(t + 2)
                p0_b(t)
                if t == 12:
                    load_w(Wqr, "Wqr", [(0, C_Q, 256), (256, C_R, 512)])
        HT_ALL = [f"hT_{t}_{half}" for t in range(NT + 1) for half in range(2)]

        def ht_keys(tiles):
            return [f"hT_{t + 1}_{half}" for t in tiles for half in range(2)]
        P.barrier()

        with ExitStack() as st1:
            maskT = sb("maskT", [128, 4, 128], st=st1)
            ld("sp", maskT[:], mask_d.rearrange("p (h i) -> p h i", h=4), "maskT")
            dgc = sb("dgc", [128, 12, 128], st=st1)
            for hr in range(12):
                P.op("dve", lambda e, hr=hr: e.tensor_scalar(
                    out=dgc[:, hr, :], in0=ident[:], scalar1=pp[:, PP_COEF + hr:PP_COEF + hr + 1], scalar2=None,
                    op0=ALU.mult), reads=["ident", "pp"], writes=[f"dgc{hr}"])
            kT2 = [sb(f"kT{i}", [128, 2, S_CORE], BF16, st=st1) for i in range(2)]
            qT = sb("qT", [128, 2, S_CORE], BF16, st=st1)
            kend = sb("kend", [128, NT, 256], BF16, st=st1)
            vv2 = [sb(f"vv{i}", [128, NT, 512], BF16, st=st1) for i in range(2)]
            srall = sb("srall", [128, NT, 512], BF16, st=st1)
            cosS = sb("cosS", [128, S_CORE], st=st1)
            sinS = sb("sinS", [128, S_CORE], st=st1)
            ld("sp", cosS[:], cos_d, "cosS")
            ld("sp", sinS[:], sin_d, "sinS")
            rt = [[sb(f"rt{i}_{k}", [128, 512], st=st1) for k in range(4)] for i in range(1)]
            Rloc = sb("Rloc", [128, 2, 512], st=st1)
            Rg = sb("Rg", [128, 3, 1024], st=st1)
            Rb = sb("Rb", [128, 2, 512], BF16, st=st1)
            sT = [sb(f"sT{i}", [128, 128], BF16, st=st1) for i in range(2)]
            on = [sb(f"on{i}", [128, 512], st=st1) for i in range(2)]
            og = [sb(f"og{i}", [128, 512], BF16, st=st1) for i in range(4)]
            st6 = [sb(f"st6_{i}", [128, 6], st=st1) for i in range(2)]
            mv = [sb(f"mv{i}", [128, 2], st=st1) for i in range(2)]
            rsd = [sb(f"rsd{i}", [128, 1], st=st1) for i in range(2)]
            sdv = [sb(f"sdv{i}", [128, 1], st=st1) for i in range(2)]
            nmr = [sb(f"nmr{i}", [128, 1], st=st1) for i in range(2)]
            tabcnt = [0]

            def proj_rot(W, wkeys, woff, dst, dstkey, tb, ceng="pool", pp_=None, ppk="psA"):
                pp_ = psA if pp_ is None else pp_
                ti = tabcnt[0] % 2
                tabcnt[0] += 1
                cs_ap = cosS[:, tb * 512:(tb + 1) * 512]
                sn_ap = sinS[:, tb * 512:(tb + 1) * 512]
                c0 = HALO + tb * 512

                def mmf(e):
                    last = None
                    for c in range(2):
                        for kc in range(8):
                            last = e.matmul(pp_[:, c, :], lhsT=W[:, kc, woff + c * 128:woff + (c + 1) * 128],
                                            rhs=hT[:, kc, c0:c0 + 512], start=(kc == 0), stop=(kc == 7))
                    return last
                P.op("pe", mmf, reads=wkeys + ht_keys(range(tb * 4, tb * 4 + 4)), writes=[ppk + "0", ppk + "1"])
                r = rt[0]
                ri = 0
                P.op("dve", lambda e: e.tensor_tensor(out=r[0][:], in0=pp_[:, 0, :], in1=cs_ap, op=ALU.mult),
                     reads=[ppk + "0", "cosS"], writes=[f"rt{ri}0"])
                P.op("dve", lambda e: e.tensor_tensor(out=r[1][:], in0=pp_[:, 1, :], in1=sn_ap, op=ALU.mult),
                     reads=[ppk + "1", "sinS"], writes=[f"rt{ri}1"])
                P.op("dve", lambda e: e.tensor_tensor(out=r[2][:], in0=pp_[:, 1, :], in1=cs_ap, op=ALU.mult),
                     reads=[ppk + "1", "cosS"], writes=[f"rt{ri}2"])
                P.op("dve", lambda e: e.tensor_tensor(out=r[3][:], in0=pp_[:, 0, :], in1=sn_ap, op=ALU.mult),
                     reads=[ppk + "0", "sinS"], writes=[f"rt{ri}3"])
                P.op(ceng, lambda e: e.tensor_tensor(out=dst[:, 0, tb * 512:(tb + 1) * 512], in0=r[0][:], in1=r[1][:],
                                                     op=ALU.subtract),
                     reads=[f"rt{ri}0", f"rt{ri}1"], writes=[f"{dstkey}0_{tb}"])
                P.op(ceng, lambda e: e.tensor_tensor(out=dst[:, 1, tb * 512:(tb + 1) * 512], in0=r[2][:], in1=r[3][:],
                                                     op=ALU.add),
                     reads=[f"rt{ri}2", f"rt{ri}3"], writes=[f"{dstkey}1_{tb}"])

            WKV_KEYS_K = ["Wkv_0"]
            WKV_KEYS_V = ["Wkv_256"]
            WQR_KEYS_Q = ["Wqr_0"]
            WQR_KEYS_R = ["Wqr_256"]
            for h in range(4):
                lg = LG[h]
                hp = h % 2
                kT = kT2[hp]
                vv = vv2[hp]
                KT = f"kT{hp}c"
                VV = f"vv{hp}_"

                def p1_vproj(n, par=hp, use_x=False):
                    vb = n % 2
                    dst_ps = psX[:, :] if use_x else psB[:, vb, :]
                    pkey = "bankX" if use_x else f"psB{vb}"
                    vdst = vv2[par]

                    def mmv(e, n=n, dst_ps=dst_ps):
                        last = None
                        for kc in range(8):
                            last = e.matmul(dst_ps, lhsT=hT[:, kc, HALO + n * 128:HALO + (n + 1) * 128],
                                            rhs=Wkv[:, kc, 256:768], start=(kc == 0), stop=(kc == 7))
                        return last
                    P.op("pe", mmv, reads=WKV_KEYS_V + ht_keys([n]), writes=[pkey])
                    P.op("act", lambda e, n=n, dst_ps=dst_ps, vdst=vdst: e.activation(out=vdst[:, n, :], in_=dst_ps, func=AF.Copy),
                         reads=[], writes=[f"vv{par}_{n}", pkey])

                def p1_trk(n, h=h):
                    tb = n // 4
                    slots, skeys = (ktr, KTRB) if h == 0 else (ktr4, KTRB4)
                    ks = n % len(slots)
                    kt_ap, kt_key = slots[ks], skeys[ks]

                    def trk(e, n=n, kT=kT, kt_ap=kt_ap):
                        last = None
                        for c in range(2):
                            last = e.transpose(kt_ap[:, c, :], kT[:, c, n * 128:(n + 1) * 128], identb[:])
                        return last
                    P.op("pe", trk, reads=[f"{KT}0_{tb}", f"{KT}1_{tb}", "identb"], writes=[kt_key])
                    P.op("act", lambda e, n=n, h=h, kt_ap=kt_ap: e.activation(
                        out=kend[:, n, :], in_=kt_ap[:].rearrange("p c d -> p (c d)"), func=AF.Copy,
                        scale=pp[:, PP_ZE + h * 16 + n:PP_ZE + h * 16 + n + 1]),
                        reads=["pp"], writes=[f"kend{n}", kt_key])

                def p1_st(n):
                    def mms(e, n=n, vv=vv):
                        last = None
                        for c in range(2):
                            last = e.matmul(psS[:, c, :], lhsT=kend[:, n, c * 128:(c + 1) * 128], rhs=vv[:, n, :],
                                            start=(n == 0), stop=True, skip_group_check=True)
                        return last
                    P.op("pe", mms, reads=[f"kend{n}", f"{VV}{n}"], writes=["psS0", "psS1"])

                if h == 0:
                    for tb in range(4):
                        proj_rot(Wkv, WKV_KEYS_K, 0, kT, KT, tb)
                        for n in range(tb * 4, tb * 4 + 4):
                            p1_vproj(n)
                        if tb >= 1:
                            for n in range((tb - 1) * 4, tb * 4):
                                p1_trk(n)
                            for n in range((tb - 1) * 4, tb * 4):
                                p1_st(n)
                    for n in range(12, 16):
                        p1_trk(n)
                    for n in range(12, 16):
                        p1_st(n)
                else:
                    for g in range(4):
                        for n in range(g * 4, g * 4 + 4):
                            p1_trk(n)
                        for n in range(g * 4, g * 4 + 4):
                            p1_st(n)
                for c in range(2):
                    P.op("dve", lambda e, c=c: e.tensor_copy(out=Rloc[:, c, :], in_=psS[:, c, :]),
                         reads=[f"psS{c}"], writes=[f"Rloc{c}"])
                P.dma("sp", lambda e, s, h=h: e.dma_start(
                    out=st_in[h], in_=Rloc[:].rearrange("p c e -> p (c e)")).then_inc(s, 16),
                    reads=["Rloc0", "Rloc1"], writes=[f"st_in{h}"], semkey=f"st_in{h}")
                P.dma("pool", lambda e, s, h=h: e.collective_compute(
                    "AllGather", ALU.bypass, replica_groups=GROUPS, ins=[st_in[h]], outs=[st_out[h]]).then_inc(s, 1),
                    reads=[f"st_in{h}"], writes=[f"st_out{h}"], semkey=f"cc{h}", ninc=1)
                if h < 3:
                    load_w(Wkv, "Wkv", [(0, C_K + 256 * (h + 1), 256), (256, C_V + 512 * (h + 1), 512)])

                def p2_rproj(n):
                    rb = n % 2

                    def mmr(e, n=n, rb=rb):
                        last = None
                        for kc in range(8):
                            last = e.matmul(psB[:, rb, :], lhsT=hT[:, kc, HALO + n * 128:HALO + (n + 1) * 128],
                                            rhs=Wqr[:, kc, 256:768], start=(kc == 0), stop=(kc == 7))
                        return last
                    P.op("pe", mmr, reads=WQR_KEYS_R + ht_keys([n]), writes=[f"psB{rb}"])
                    P.op("act", lambda e, rb=rb, n=n: e.activation(out=srall[:, n, :], in_=psB[:, rb, :], func=AF.Silu),
                         reads=[f"psB{rb}"], writes=[f"sr{n}"])

                for tb in range(4):
                    proj_rot(Wqr, WQR_KEYS_Q, 0, qT, "qT", tb, ceng="dve")
                    for n in range(tb * 4, tb * 4 + 4):
                        p2_rproj(n)

                P.dma("sp", lambda e, s, h=h: e.dma_start(
                    out=Rg[:], in_=st_out[h][0:384, :].rearrange("(r p) f -> p r f", p=128)).then_inc(s, 16),
                    reads=[f"st_out{h}"], writes=["Rg"], semkey="Rg")

                def inj(e, h=h):
                    last = None
                    for c in range(2):
                        for r in range(3):
                            last = e.matmul(psS[:, c, :], lhsT=dgc[:, h * 3 + r, :], rhs=Rg[:, r, c * 512:(c + 1) * 512],
                                            start=(r == 0), stop=(r == 2), skip_group_check=True)
                    return last
                P.op("pe", inj, reads=["Rg"] + [f"dgc{h * 3 + r}" for r in range(3)] + ["Rloc0", "Rloc1"],
                     writes=["psS0", "psS1"])

                sps_h, spk = (psX[:, 0:128], "bankX") if h == 3 else (sps, "bankM")

                def p2_scores(n, h=h, sps_h=sps_h, spk=spk):
                    tb = n // 4
                    b = n % 2

                    def mmsc(e, n=n, b=b, kT=kT, sps_h=sps_h):
                        last = None
                        for c in range(2):
                            last = e.matmul(sps_h, lhsT=kT[:, c, n * 128:(n + 1) * 128], rhs=qT[:, c, n * 128:(n + 1) * 128],
                                            start=(c == 0), stop=(c == 1))
                        return last
                    P.op("pe", mmsc, reads=[f"{KT}0_{tb}", f"{KT}1_{tb}", f"qT0_{tb}", f"qT1_{tb}"], writes=[spk])
                    P.op("dve", lambda e, h=h, b=b, sps_h=sps_h: e.tensor_tensor(out=sT[b][:], in0=sps_h, in1=maskT[:, h, :],
                                                                              op=ALU.mult),
                         reads=["maskT"], writes=[f"sT{b}", spk])

                def p2_rb(n, lg=lg):
                    scl = math.exp(lg * 128.0 * (n - 16))
                    P.op("act", lambda e, scl=scl: e.activation(out=Rb[:, 0, :], in_=psS[:, 0, :], func=AF.Copy, scale=float(scl)),
                         reads=["psS0"], writes=["Rb0"])
                    P.op("dve", lambda e, scl=scl: e.tensor_scalar(out=Rb[:, 1, :], in0=psS[:, 1, :], scalar1=float(scl), scalar2=None,
                                                                   op0=ALU.mult),
                         reads=["psS1"], writes=["Rb1"])

                OB = [(psA[:, 0, :], "psA0"), (psA[:, 1, :], "psA1")]
                if h == 3:
                    OB = OB + [(psB[:, 0, :], "psB0"), (psB[:, 1, :], "psB1")]

                def p2_o(n, OB=OB):
                    tb = n // 4
                    b = n % 2
                    oap, okey = OB[n % len(OB)]

                    P.op("pe", lambda e, n=n, b=b, vv=vv, oap=oap: e.matmul(oap, lhsT=sT[b][:], rhs=vv[:, n, :],
                                                                             start=True, stop=False),
                         reads=[f"sT{b}", f"{VV}{n}"], writes=[okey])

                    def mmo(e, n=n, b=b, oap=oap):
                        last = None
                        for c in range(2):
                            last = e.matmul(oap, lhsT=qT[:, c, n * 128:(n + 1) * 128], rhs=Rb[:, c, :],
                                            start=False, stop=(c == 1))
                        return last
                    P.op("pe", mmo, reads=[f"qT0_{tb}", f"qT1_{tb}", "Rb0", "Rb1"], writes=[okey])
                    if n < NT - 1:
                        def mms2(e, n=n, vv=vv):
                            last = None
                            for c in range(2):
                                last = e.matmul(psS[:, c, :], lhsT=kend[:, n, c * 128:(c + 1) * 128], rhs=vv[:, n, :],
                                                start=False, stop=True, skip_group_check=True)
                            return last
                        P.op("pe", mms2, reads=[f"kend{n}", f"{VV}{n}", "Rb0", "Rb1"], writes=["psS0", "psS1"])

                def p2_stats_a(n, OB=OB):
                    b = n % 2
                    oap, okey = OB[n % len(OB)]
                    P.op("dve", lambda e, b=b, oap=oap: e.bn_stats(out=st6[b][:], in_=oap), reads=[okey], writes=[f"st6_{b}"])

                def p2_stats_b(n):
                    b = n % 2
                    P.op("dve", lambda e, b=b: e.bn_aggr(out=mv[b][:], in_=st6[b][:]), reads=[f"st6_{b}"], writes=[f"mv{b}"])

                def p2_norm_act1(n, h=h):
                    b = n % 2
                    P.op("act", lambda e, h=h, b=b: e.activation(
                        out=sdv[b][:], in_=mv[b][:, 1:2], func=AF.Sqrt, bias=pp[:, PP_EPS + h:PP_EPS + h + 1], scale=1.0),
                        reads=[f"mv{b}", "pp"], writes=[f"sdv{b}"])

                def p2_norm_dve(n):
                    b = n % 2
                    P.op("dve", lambda e, b=b: e.reciprocal(out=rsd[b][:], in_=sdv[b][:]), reads=[f"sdv{b}"], writes=[f"rsd{b}"])
                    P.op("dve", lambda e, b=b: e.scalar_tensor_tensor(
                        out=nmr[b][:], in0=mv[b][:, 0:1], scalar=-1.0, in1=rsd[b][:], op0=ALU.mult, op1=ALU.mult),
                        reads=[f"mv{b}", f"rsd{b}"], writes=[f"nmr{b}"])

                def p2_norm_act2(n, OB=OB):
                    b = n % 2
                    oap, okey = OB[n % len(OB)]
                    P.op("act", lambda e, b=b, oap=oap: e.activation(out=on[b][:], in_=oap, func=AF.Identity,
                                                                       bias=nmr[b][:], scale=rsd[b][:]),
                         reads=[okey, f"rsd{b}", f"nmr{b}"], writes=[f"on{b}"])
                    P.op("pool", lambda e, b=b, n=n: e.tensor_tensor(out=og[n % 4][:], in0=on[b][:], in1=srall[:, n, :], op=ALU.mult),
                         reads=[f"on{b}", f"sr{n}"], writes=[f"og{n % 4}"])

                def p2_tr(n, h=h):
                    ob = n % 4
                    P.dma("sp", lambda e, s, ob=ob, h=h, n=n: e.dma_start(
                        out=og_d[n * 128:(n + 1) * 128, h * 512:(h + 1) * 512], in_=og[ob][:]).then_inc(s, 16),
                        reads=[f"og{ob}"], writes=[f"ogd_{n}_{h}"], semkey=f"ogd{ob}")

                p2_scores(0)
                p2_rb(0)
                for n in range(NT):
                    if n + 1 < NT:
                        p2_scores(n + 1)
                    if n >= 1:
                        p2_norm_act1(n - 1)
                        p2_norm_dve(n - 1)
                    p2_o(n)
                    p2_stats_a(n)
                    if n + 1 < NT:
                        p2_rb(n + 1)
                    p2_stats_b(n)
                    if n >= 1:
                        p2_norm_act2(n - 1)
                    if n >= 2:
                        p2_tr(n - 2)
                    if h < 3:
                        if n % 4 == 0:
                            proj_rot(Wkv, WKV_KEYS_K, 0, kT2[1 - hp], f"kT{1 - hp}c", n // 4, pp_=psB, ppk="psB")
                        p1_vproj(n, par=1 - hp, use_x=True)
                p2_norm_act1(NT - 1)
                p2_norm_dve(NT - 1)
                p2_norm_act2(NT - 1)
                p2_tr(NT - 2)
                p2_tr(NT - 1)
                if h < 3:
                    load_w(Wqr, "Wqr", [(0, C_Q + 256 * (h + 1), 256), (256, C_R + 512 * (h + 1), 512)])
        P.barrier()
        st01.close()

        u2 = sb("u2", [128, 8, S_CORE], BF16)
        Wc = [sb(f"Wc{i}", [128, 8, 128], BF16) for i in range(2)]
        Wa = [sb(f"Wa{i}", [128, 8, 128], BF16) for i in range(2)]
        w_cp_v = w_cp.rearrange("(kc p) n -> p kc n", p=128)

        def load_wca(dcol):
            s = dcol % 2
            P.dma("pool", lambda e, s_, s=s, dcol=dcol: e.dma_start(
                out=Wc[s][:], in_=w_cp_v[:, :, dcol * 128:(dcol + 1) * 128]).then_inc(s_, 16),
                writes=[f"Wc{s}"], semkey=f"Wc{s}")
            P.dma("pool", lambda e, s_, s=s, dcol=dcol: e.dma_start(
                out=Wa[s][:], in_=w_in_v[:, :, C_GA + dcol * 128:C_GA + (dcol + 1) * 128]).then_inc(s_, 16),
                writes=[f"Wa{s}"], semkey=f"Wa{s}")
        with ExitStack() as st3:
            Wg = [sb(f"Wg{i}", [128, 8, 128], BF16, st=st3) for i in range(2)]

            def load_wg(cc):
                s = cc % 2
                P.dma("pool", lambda e, s_, s=s, cc=cc: e.dma_start(
                    out=Wg[s][:], in_=w_in_v[:, :, C_AGATE + cc * 128:C_AGATE + (cc + 1) * 128]).then_inc(s_, 16),
                    writes=[f"Wg{s}"], semkey=f"Wg{s}")
            cv = sb("cv", [128, 8, S_CORE], st=st3)
            convw = sb("convw", [128, 256], st=st3)
            ld("sp", convw[:], convw_d, "convw")
            i4 = sb("i4", [128, 32], st=st3)
            ld("sp", i4[:], i4_d, "i4")
            onesf = sb("onesf", [128, 128], st=st3)
            P.op("pool", lambda e: e.memset(onesf[:], 1.0 / D), writes=["onesf"])
            with ExitStack() as st3a:
                Wag = [sb(f"Wag{i}", [128, 8, 256], BF16, st=st3a) for i in range(2)]
                uT = [sb(f"uT{i}", [128, HALO + S_CORE], BF16, st=st3a) for i in range(2)]
                Ust = [sb(f"Ust{i}", [128, 4, HALO + S_CORE], BF16, st=st3a) for i in range(2)]
                Wp = [sb(f"Wp{i}", [128, 32, 32], BF16, st=st3a) for i in range(2)]
                for i in range(2):
                    P.op("pool", lambda e, i=i: e.memset(Ust[i][:, :, HALO + S_CORE - 4:HALO + S_CORE], 0.0), writes=[f"U{i}"])
                sgl = [sb(f"sgl{i}", [128, 512], st=st3a) for i in range(2)]
                def load_wag(cc):
                    s = cc % 2
                    for part, colbase in enumerate((C_AVAL, C_AGLU)):
                        P.dma("pool", lambda e, s_, s=s, part=part, colbase=colbase, cc=cc: e.dma_start(
                            out=Wag[s][:, :, part * 128:(part + 1) * 128],
                            in_=w_in_v[:, :, colbase + cc * 128:colbase + (cc + 1) * 128]).then_inc(s_, 16),
                            writes=[f"Wag{s}_{part}"], semkey=f"Wag{s}_{part}")
                def conv_mm(cc):
                    s = cc % 2
                    for tb in range(4):
                        b = tb % 2

                        def mmc(e, s=s, tb=tb, b=b):
                            last = None
                            for m in range(8):
                                o0 = 2 + 4 * m + tb * 512
                                for g in range(4):
                                    last = e.matmul(psS[32 * g:32 * (g + 1), b, :], lhsT=Wp[s][:, g * 8 + m, :],
                                                    rhs=Ust[s][:, g, o0:o0 + 512], start=(m == 0), stop=(m == 7),
                                                    tile_position=(0, 32 * g))
                            return last
                        P.op("pe", mmc, reads=[f"Wp{s}_{gm}" for gm in range(32)] + [f"U{s}"], writes=[f"psS{b}"])
                        P.op("act", lambda e, b=b, cc=cc, tb=tb: e.activation(
                            out=cv[:, cc, tb * 512:(tb + 1) * 512], in_=psS[:, b, :], func=AF.Identity,
                            bias=pp[:, PP_CB + cc:PP_CB + cc + 1], scale=1.0),
                            reads=[f"psS{b}", "pp"], writes=[f"cv{cc}_{tb}"])

                load_wag(0)
                load_wag(1)
                for cc in range(8):
                    s = cc % 2
                    for blk in range(5):
                        c0 = 0 if blk == 0 else HALO + (blk - 1) * 512
                        n = HALO if blk == 0 else 512
                        hk = ["hT_0_0", "hT_0_1"] if blk == 0 else ht_keys(range((blk - 1) * 4, (blk - 1) * 4 + 4))

                        for part in (1, 0):
                            def mma(e, s=s, c0=c0, n=n, part=part):
                                last = None
                                for kc in range(8):
                                    last = e.matmul(psA[:, part, 0:n], lhsT=Wag[s][:, kc, part * 128:(part + 1) * 128],
                                                    rhs=hT[:, kc, c0:c0 + n], start=(kc == 0), stop=(kc == 7))
                                return last
                            P.op("pe", mma, reads=[f"Wag{s}_{part}"] + hk, writes=[f"psA{part}"])
                        b = blk % 2
                        P.op("act", lambda e, b=b, n=n: e.activation(out=sgl[b][:, 0:n], in_=psA[:, 1, 0:n], func=AF.Sigmoid),
                             reads=["psA1"], writes=[f"sgl{b}"])
                        P.op("dve", lambda e, b=b, n=n, s=s, c0=c0: e.tensor_tensor(
                            out=uT[s][:, c0:c0 + n], in0=psA[:, 0, 0:n], in1=sgl[b][:, 0:n], op=ALU.mult),
                            reads=["psA0", f"sgl{b}"], writes=[f"uT{s}_{blk}"])
                    for gm in range(32):
                        P.op("dve", lambda e, s=s, gm=gm, cc=cc: e.tensor_scalar(
                            out=Wp[s][:, gm, :], in0=i4[:], scalar1=convw[:, cc * 32 + gm:cc * 32 + gm + 1], scalar2=None,
                            op0=ALU.mult), reads=["i4", "convw"], writes=[f"Wp{s}_{gm}"])
                    def mku(e, s_, s=s):
                        for g in range(4):
                            for j in range(4):
                                e.dma_start(out=Ust[s][j * 32:(j + 1) * 32, g, 0:HALO + S_CORE - j],
                                            in_=uT[s][g * 32:(g + 1) * 32, j:HALO + S_CORE]).then_inc(s_, 16)
                    P.dma("sp", mku, reads=[f"uT{s}_{i}" for i in range(5)], writes=[f"U{s}"], semkey=f"U{s}", ninc=256)
                    if cc >= 1:
                        conv_mm(cc - 1)
                    if cc + 2 < 8:
                        load_wag(cc + 2)
                conv_mm(7)
            P.barrier()
            rstd_t = sb("rstd_t", [128, S_CORE], st=st3)
            nmr_t = sb("nmr_t", [128, S_CORE], st=st3)
            sga = [sb(f"sga{i}", [128, 512], st=st3) for i in range(2)]
            t1 = [sb(f"t1_{i}", [128, 512], st=st3) for i in range(2)]
            t2 = [sb(f"t2_{i}", [128, 512], st=st3) for i in range(2)]
            zz = [sb(f"zz{i}", [128, 512], st=st3) for i in range(2)]
            with ExitStack() as st3b:
                load_wg(0)
                load_wg(1)
                sq = [sb(f"sq{i}", [128, 512], BF16, st=st3b) for i in range(2)]
                cvb = [sb(f"cvb{i}", [128, 512], BF16, st=st3b) for i in range(2)]
                onesb = sb("onesb", [128, 128], BF16, st=st3b)
                P.op("dve", lambda e: e.tensor_copy(out=onesb[:], in_=onesf[:]), reads=["onesf"], writes=["onesb"])
                mean_t2 = [sb(f"mean_t{i}", [128, 512], st=st3b) for i in range(2)]
                msq_t = sb("msq_t", [128, 512], st=st3b)
                LNB = [(psA, "psA"), (psB, "psB")]

                def ln_front(tb):
                    pt, pk = LNB[tb % 2]
                    for cc in range(8):
                        b = cc % 2
                        P.op("dve", lambda e, b=b, cc=cc, tb=tb: e.tensor_copy(
                            out=cvb[b][:], in_=cv[:, cc, tb * 512:(tb + 1) * 512]),
                            reads=[f"cv{cc}_{tb}"], writes=[f"cvb{b}"])
                        P.op("pe", lambda e, b=b, cc=cc, pt=pt: e.matmul(pt[:, 0, :], lhsT=onesb[:], rhs=cvb[b][:],
                                                                         start=(cc == 0), stop=(cc == 7)),
                             reads=["onesb", f"cvb{b}"], writes=[pk + "0"])
                        P.op("act", lambda e, b=b, cc=cc, tb=tb: e.activation(
                            out=sq[b][:], in_=cv[:, cc, tb * 512:(tb + 1) * 512], func=AF.Square),
                            reads=[f"cv{cc}_{tb}"], writes=[f"sq{b}"])
                        P.op("pe", lambda e, b=b, cc=cc, pt=pt: e.matmul(pt[:, 1, :], lhsT=onesb[:], rhs=sq[b][:],
                                                                         start=(cc == 0), stop=(cc == 7)),
                             reads=["onesb", f"sq{b}"], writes=[pk + "1"])

                def ln_back(tb):
                    pt, pk = LNB[tb % 2]
                    sl = slice(tb * 512, (tb + 1) * 512)
                    mean_t = mean_t2[tb % 2]
                    mk = f"mean_t{tb % 2}"
                    P.op("dve", lambda e, pt=pt, mean_t=mean_t: e.tensor_copy(out=mean_t[:], in_=pt[:, 0, :]), reads=[pk + "0"], writes=[mk])
                    P.op("dve", lambda e, mean_t=mean_t: e.tensor_tensor(out=msq_t[:], in0=mean_t[:], in1=mean_t[:], op=ALU.mult),
                         reads=[mk], writes=["msq_t"])
                    P.op("dve", lambda e, pt=pt: e.tensor_tensor(out=msq_t[:], in0=pt[:, 1, :], in1=msq_t[:], op=ALU.subtract),
                         reads=[pk + "1", "msq_t"], writes=["msq_t"])
                    P.op("act", lambda e: e.activation(out=msq_t[:], in_=msq_t[:], func=AF.Ln, bias=eps5[:], scale=1.0),
                         reads=["msq_t", "eps5"], writes=["msq_t"])
                    P.op("act", lambda e, sl=sl: e.activation(out=rstd_t[:, sl], in_=msq_t[:], func=AF.Exp, scale=-0.5),
                         reads=["msq_t"], writes=[f"rstd_t{tb}"])
                    P.op("pool", lambda e, sl=sl, mean_t=mean_t: e.tensor_tensor(out=nmr_t[:, sl], in0=mean_t[:], in1=rstd_t[:, sl],
                                                                                op=ALU.mult),
                         reads=[mk, f"rstd_t{tb}"], writes=[f"nmr_t{tb}"])

                for tb in range(5):
                    if tb < 4:
                        ln_front(tb)
                    if tb >= 1:
                        ln_back(tb - 1)
            with ExitStack() as st3c:
                load_wca(0)
                load_wca(1)
                items = [(cc, tb) for cc in range(8) for tb in range(4)]

                def n_front(i):
                    cc, tb = items[i]
                    s = cc % 2
                    b = i % 2
                    c0 = HALO + tb * 512
                    sl = slice(tb * 512, (tb + 1) * 512)

                    def mmg2(e, s=s, b=b, c0=c0):
                        last = None
                        for kc in range(8):
                            last = e.matmul(psB[:, b, :], lhsT=Wg[s][:, kc, :], rhs=hT[:, kc, c0:c0 + 512],
                                            start=(kc == 0), stop=(kc == 7))
                        return last
                    P.op("pe", mmg2, reads=[f"Wg{s}"] + ht_keys(range(tb * 4, tb * 4 + 4)), writes=[f"psB{b}"])
                    P.op("dve", lambda e, b=b, cc=cc, sl=sl: e.tensor_tensor(
                        out=t1[b][:], in0=cv[:, cc, sl], in1=rstd_t[:, sl], op=ALU.mult),
                        reads=[f"cv{cc}_{tb}", f"rstd_t{tb}"], writes=[f"t1_{b}"])
                    P.op("dve", lambda e, b=b, sl=sl: e.tensor_tensor(
                        out=t2[b][:], in0=t1[b][:], in1=nmr_t[:, sl], op=ALU.subtract),
                        reads=[f"t1_{b}", f"nmr_t{tb}"], writes=[f"t2_{b}"])

                def n_back(i):
                    cc, tb = items[i]
                    b = i % 2
                    sl = slice(tb * 512, (tb + 1) * 512)
                    P.op("act", lambda e, b=b: e.activation(out=sga[b][:], in_=psB[:, b, :], func=AF.Silu),
                         reads=[f"psB{b}"], writes=[f"sga{b}"])
                    P.op("act", lambda e, b=b, cc=cc: e.activation(
                        out=zz[b][:], in_=t2[b][:], func=AF.Silu,
                        bias=pp[:, PP_LB + cc:PP_LB + cc + 1], scale=pp[:, PP_LG + cc:PP_LG + cc + 1]),
                        reads=[f"t2_{b}", "pp"], writes=[f"zz{b}"])
                    P.op("dve", lambda e, b=b, cc=cc, sl=sl: e.tensor_tensor(
                        out=u2[:, cc, sl], in0=zz[b][:], in1=sga[b][:], op=ALU.mult),
                        reads=[f"zz{b}", f"sga{b}"], writes=[f"u2_{cc}_{tb}"])

                for i in range(len(items) + 1):
                    if i < len(items):
                        n_front(i)
                    if i >= 1:
                        n_back(i - 1)
                    if i % 4 == 3 and i // 4 + 2 < 8:
                        load_wg(i // 4 + 2)
        P.barrier()
        mT = sb("mT", [128, 8, S_CORE], BF16)
        Wo = sb("Wo", [128, 8, D], BF16)
        with ExitStack() as st5:
            Wrp = sb("Wrp", [128, 16, D], BF16, st=st5)
            Wgb = sb("Wgb", [128, 8, D], BF16, st=st5)
            def load_wgb(half):
                P.dma("pool", lambda e, s, half=half: e.dma_start(
                    out=Wgb[:, :, half * 512:(half + 1) * 512],
                    in_=w_in_v[:, :, C_GB + half * 512:C_GB + (half + 1) * 512]).then_inc(s, 16),
                    writes=[f"Wgb{half}"], semkey=f"Wgb{half}")

            def load_wrp(ec):
                s = ec % 2
                ld("sp", wst[s][:], w_rp[ec * 128:(ec + 1) * 128, :], f"wst{s}")
                P.op("act", lambda e, s=s, ec=ec: e.activation(out=Wrp[:, ec, :], in_=wst[s][:], func=AF.Copy,
                                                               scale=pp[:, PP_GN + ec:PP_GN + ec + 1]),
                     reads=[f"wst{s}", "pp"], writes=[f"Wrp{ec}"])
            WRP_KEYS = [f"Wrp{ec}" for ec in range(16)]
            ogT0 = sb("ogT0", [128, 16, 512], BF16, st=st5)

            def rd_ogT_op(dst, key, tb):
                def rd_ogT(e, s_, dst=dst, tb=tb):
                    for ec in range(16):
                        e.dma_start_transpose(out=dst[:, ec, :],
                                              in_=og_d[tb * 512:(tb + 1) * 512, ec * 128:(ec + 1) * 128]).then_inc(s_, 16)
                P.dma("sp", rd_ogT, reads=[f"ogd_{n}_{h}" for n in range(tb * 4, tb * 4 + 4) for h in range(4)],
                      writes=[key], semkey=key, ninc=256)
            with ExitStack() as st3d:
                wst = [sb(f"wst{i}", [128, D], st=st3d) for i in range(2)]
                sgq = [sb(f"sgq{i}", [128, 512], st=st3d) for i in range(2)]
                ta = [sb(f"ta{i}", [128, 512], st=st3d) for i in range(2)]
                it = 0
                for dcol in range(8):
                    s = dcol % 2
                    for tb in range(4):
                        b = it % 2
                        it += 1
                        c0 = HALO + tb * 512
                        sl = slice(tb * 512, (tb + 1) * 512)

                        def mmga(e, s=s, b=b, c0=c0):
                            last = None
                            for kc in range(8):
                                last = e.matmul(psB[:, b, :], lhsT=Wa[s][:, kc, :], rhs=hT[:, kc, c0:c0 + 512],
                                                start=(kc == 0), stop=(kc == 7))
                            return last
                        P.op("pe", mmga, reads=[f"Wa{s}"] + ht_keys(range(tb * 4, tb * 4 + 4)), writes=[f"psB{b}"])
                        P.op("act", lambda e, b=b: e.activation(out=sgq[b][:], in_=psB[:, b, :], func=AF.Sigmoid),
                             reads=[f"psB{b}"], writes=[f"sgq{b}"])

                        def mmya(e, s=s, b=b, sl=sl):
                            last = None
                            for cc in range(8):
                                last = e.matmul(psA[:, b, :], lhsT=Wc[s][:, cc, :], rhs=u2[:, cc, sl],
                                                start=(cc == 0), stop=(cc == 7))
                            return last
                        P.op("pe", mmya, reads=[f"Wc{s}"] + [f"u2_{cc}_{tb}" for cc in range(8)], writes=[f"psA{b}"])
                        P.op("dve", lambda e, b=b, dcol=dcol, sl=sl: e.tensor_tensor(
                            out=mT[:, dcol, sl], in0=psA[:, b, :], in1=sgq[b][:], op=ALU.mult),
                            reads=[f"psA{b}", f"sgq{b}"], writes=[f"mT{dcol}_{tb}"])
                    if dcol + 2 < 8:
                        load_wca(dcol + 2)
                    load_wrp(2 * dcol)
                    load_wrp(2 * dcol + 1)
                    if dcol in (5, 6):
                        load_wgb(dcol - 5)
                    if dcol == 3:
                        rd_ogT_op(ogT0, "ogT0", 0)
            P.barrier()
            with ExitStack() as st2:
                ogT = [ogT0, sb("ogT1", [128, 16, 512], BF16, st=st2)]
                sg0 = sb("sg0", [128, 512], st=st2)
                tb0 = sb("tb_0", [128, 512], st=st2)
                sg = [sg0, sg0]
                tb_ = [tb0, tb0]
                for tb in range(4):
                    s = tb % 2
                    if tb == 0:
                        rd_ogT_op(ogT[1], "ogT1", 1)
                    elif tb + 1 < 4:
                        rd_ogT_op(ogT[(tb + 1) % 2], f"ogT{(tb + 1) % 2}", tb + 1)
                    c0 = HALO + tb * 512
                    for dcol in range(8):
                        b = dcol % 2

                        def mmg(e, dcol=dcol, b=b, c0=c0):
                            last = None
                            for kc in range(8):
                                last = e.matmul(psB[:, b, :], lhsT=Wgb[:, kc, dcol * 128:(dcol + 1) * 128],
                                                rhs=hT[:, kc, c0:c0 + 512], start=(kc == 0), stop=(kc == 7))
                            return last
                        P.op("pe", mmg, reads=["Wgb0", "Wgb1"] + ht_keys(range(tb * 4, tb * 4 + 4)), writes=[f"psB{b}"])
                        P.op("act", lambda e, b=b: e.activation(out=sg[b][:], in_=psB[:, b, :], func=AF.Sigmoid),
                             reads=[f"psB{b}"], writes=["sg0"])

                        def mmy(e, dcol=dcol, b=b, s=s):
                            last = None
                            for ec in range(16):
                                last = e.matmul(psA[:, b, :], lhsT=Wrp[:, ec, dcol * 128:(dcol + 1) * 128],
                                                rhs=ogT[s][:, ec, :], start=(ec == 0), stop=(ec == 15))
                            return last
                        P.op("pe", mmy, reads=WRP_KEYS + [f"ogT{s}"], writes=[f"psA{b}"])
                        P.op("dve", lambda e, b=b: e.tensor_tensor(out=tb_[b][:], in0=psA[:, b, :], in1=sg[b][:], op=ALU.mult),
                             reads=[f"psA{b}", "sg0"], writes=["tb_0"])
                        P.op("pool", lambda e, dcol=dcol, b=b, tb=tb: e.tensor_tensor(
                            out=mT[:, dcol, tb * 512:(tb + 1) * 512], in0=tb_[b][:], in1=mT[:, dcol, tb * 512:(tb + 1) * 512],
                            op=ALU.add),
                            reads=["tb_0", f"mT{dcol}_{tb}"], writes=[f"mT{dcol}_{tb}"])
                    if tb == 1:
                        w_out_v = w_out.rearrange("(kc p) n -> p kc n", p=128)
                        for half in range(2):
                            P.dma("pool", lambda e, s_, half=half: e.dma_start(
                                out=Wo[:, :, half * 512:(half + 1) * 512],
                                in_=w_out_v[:, :, half * 512:(half + 1) * 512]).then_inc(s_, 16),
                                writes=[f"Wo{half}"], semkey=f"Wo{half}")

        P.barrier()

        with ExitStack() as st4:
            fgainB = sb("fgainB", [128, D], st=st4)
            ld("sp", fgainB[:], fgainB_d, "fgainB")
            xr = [sb(f"xr{i}", [128, D], st=st4) for i in range(3)]
            yy = [sb(f"yy{i}", [128, D], st=st4) for i in range(2)]
            oo = [sb(f"oo{i}", [128, D], st=st4) for i in range(2)]
            junk2 = sb("junk2", [128, D], st=st4)
            ss2 = [sb(f"ss2_{i}", [128, 1], st=st4) for i in range(2)]
            rs2 = [sb(f"rs2_{i}", [128, 1], st=st4) for i in range(2)]
            pso = [psA, psB]
            out_keys = []

            def o_load(n):
                ld("sp", xr[n % 3][:], x[n * 128:(n + 1) * 128, :], f"xr{n % 3}")

            def o_mm(n):
                s = n % 2
                tb = n // 4

                def mmo2(e, n=n, s=s):
                    last = None
                    for half in range(2):
                        for dc in range(8):
                            last = e.matmul(pso[s][:, half, :], lhsT=mT[:, dc, n * 128:(n + 1) * 128],
                                            rhs=Wo[:, dc, half * 512:(half + 1) * 512], start=(dc == 0), stop=(dc == 7))
                    return last
                P.op("pe", mmo2, reads=["Wo0", "Wo1"] + [f"mT{dc}_{tb}" for dc in range(8)],
                     writes=[f"ps{'AB'[s]}0", f"ps{'AB'[s]}1"])

            def o_y(n):
                s = n % 2
                for half in range(2):
                    P.op("dve", lambda e, s=s, half=half, n=n: e.tensor_tensor(
                        out=yy[s][:, half * 512:(half + 1) * 512], in0=pso[s][:, half, :],
                        in1=xr[n % 3][:, half * 512:(half + 1) * 512], op=ALU.add),
                        reads=[f"ps{'AB'[s]}{half}", f"xr{n % 3}"], writes=[f"yy{s}_{half}"])
                P.op("act", lambda e, s=s: e.activation(out=junk2[:], in_=yy[s][:], func=AF.Square, accum_out=ss2[s][:]),
                     reads=[f"yy{s}_0", f"yy{s}_1"], writes=["junk2", f"ss2_{s}"])
                P.op("act", lambda e, s=s: e.activation(out=ss2[s][:], in_=ss2[s][:], func=AF.Sqrt, bias=eps6[:], scale=1.0 / D),
                     reads=[f"ss2_{s}", "eps6"], writes=[f"ss2_{s}"])

            def o_z(n):
                s = n % 2
                P.op("dve", lambda e, s=s: e.reciprocal(out=rs2[s][:], in_=ss2[s][:]),
                     reads=[f"ss2_{s}"], writes=[f"rs2_{s}"])
                P.op("dve", lambda e, s=s: e.scalar_tensor_tensor(
                    out=oo[s][:], in0=yy[s][:], scalar=rs2[s][:], in1=fgainB[:], op0=ALU.mult, op1=ALU.mult),
                    reads=[f"yy{s}_0", f"yy{s}_1", f"rs2_{s}", "fgainB"], writes=[f"oo{s}"])
                P.dma("sp", lambda e, s_, s=s, n=n: e.dma_start(out=out_d[n * 128:(n + 1) * 128, :], in_=oo[s][:]).then_inc(s_, 16),
                      reads=[f"oo{s}"], writes=[f"out{n}"], semkey=f"outd{s}")
                out_keys.append(f"out{n}")

            o_load(0)
            o_load(1)
            o_mm(0)
            for n in range(NT + 1):
                if n + 2 < NT:
                    o_load(n + 2)
                if n + 1 < NT:
                    o_mm(n + 1)
                if n < NT:
                    o_y(n)
                if n >= 1:
                    o_z(n - 1)
            P.op("sp", None, reads=out_keys)
        P.emit(nc, gst)
    return nc


_NC_CACHE = {}


def _consts(j):
    LG = log_gammas()
    pos = (j * S_CORE + np.arange(S_CORE, dtype=np.float64))
    inv_freq = 10000.0 ** (-np.arange(0, 256, 2, dtype=np.float64) / 256.0)
    ang = inv_freq[:, None] * pos[None, :]
    cosT = np.cos(ang).astype(np.float32)
    sinT = np.sin(ang).astype(np.float32)
    pc = np.zeros((128, NPP), np.float64)
    p = np.arange(128, dtype=np.float64)
    for h in range(4):
        for n in range(NT):
            pc[:, PP_ZE + h * 16 + n] = np.exp(LG[h] * (2047.0 - (128.0 * n + p))) / 16.0
        pc[:, PP_EPS + h] = 1e-5 / np.exp(2.0 * LG[h] * (p + 1.0))
        for r in range(3):
            pc[:, PP_COEF + h * 3 + r] = math.exp(LG[h] * 2048.0 * (j - r)) if r < j else 0.0
    mask = np.zeros((128, 4, 128), np.float64)
    jj = np.arange(128)[:, None]
    ii = np.arange(128)[None, :]
    for h in range(4):
        mask[:, h, :] = np.where(jj <= ii, np.exp(-LG[h] * (jj + 1.0)) / 16.0, 0.0)
    return cosT, sinT, pc, mask.reshape(128, 512).astype(np.float32)


def kernel(x, norm_gain, w_in, conv_dw_w, conv_dw_b, conv_ln_g, conv_ln_b,
           w_conv_proj, ret_gn_g, w_ret_proj, w_out, final_gain):
    x = np.asarray(x, np.float32)
    f = lambda a: np.ascontiguousarray(np.asarray(a, np.float32))
    w_in0 = f(w_in[0])
    w_cp0 = f(w_conv_proj[0])
    w_rp0 = f(w_ret_proj[0])
    w_out0 = f(w_out[0])
    gainB = f(np.broadcast_to(np.asarray(norm_gain, np.float32)[0][None, :], (128, D)))
    fgainB = f(np.broadcast_to(np.asarray(final_gain, np.float32)[None, :], (128, D)))
    cw = np.zeros((32, D), np.float32)
    cw[:31] = np.asarray(conv_dw_w, np.float32)[0]
    convw = f(cw.reshape(8, 4, 8, 4, 32).transpose(1, 4, 2, 3, 0).reshape(128, 256))
    i4 = f(np.tile(np.eye(32, dtype=np.float32), (4, 1)))

    def colT(v, nch):
        return np.asarray(v, np.float32).reshape(nch, 128).T

    ident = np.eye(128, dtype=np.float32)
    if "nc" not in _NC_CACHE:
        _NC_CACHE["nc"] = build_program()
    nc = _NC_CACHE["nc"]
    in_maps = []
    for c in range(8):
        b, j = c // 4, c % 4
        cosT, sinT, pc, mask = _consts(j)
        pc[:, PP_CB:PP_CB + 8] = colT(conv_dw_b[0], 8)
        pc[:, PP_LG:PP_LG + 8] = colT(conv_ln_g[0], 8)
        pc[:, PP_LB:PP_LB + 8] = colT(conv_ln_b[0], 8)
        pc[:, PP_GN:PP_GN + 16] = colT(ret_gn_g[0], 16)
        xs = f(x[b, j * S_CORE:(j + 1) * S_CORE, :])
        if j == 0:
            xhalo = np.zeros((HALO, D), np.float32)
        else:
            xhalo = f(x[b, j * S_CORE - HALO:j * S_CORE, :])
        in_maps.append({
            "x": xs, "xh": xhalo, "w_in": w_in0, "w_cp": w_cp0, "w_rp": w_rp0, "w_out": w_out0,
            "gainB": gainB, "fgainB": fgainB, "pp": f(pc.astype(np.float32)), "convw": convw,
            "maskT": mask, "ident": ident, "cosT": cosT, "sinT": sinT, "i4": i4,
        })
    res = run_bass_kernel_spmd(nc, in_maps, core_ids=list(range(8)))
    out = np.empty((2, 4 * S_CORE, D), np.float32)
    for c in range(8):
        b, j = c // 4, c % 4
        out[b, j * S_CORE:(j + 1) * S_CORE, :] = res.results[c]["out"]
    return out
```

```python
import math
import numpy as np
from contextlib import ExitStack
import concourse.bass as bass
import concourse.mybir as mybir
from concourse.bass_utils import run_bass_kernel_spmd

F32 = mybir.dt.float32
BF16 = mybir.dt.bfloat16
AF = mybir.ActivationFunctionType
ALU = mybir.AluOpType

ENG_NAMES = ("pe", "act", "dve", "pool", "sp")

S_CORE = 2048
NT = 16
D = 1024
HALO = 32
NCOLS = 11264
C_AVAL, C_AGLU, C_AGATE, C_Q, C_K, C_V, C_R, C_GA, C_GB = 0, 1024, 2048, 3072, 4096, 5120, 7168, 9216, 10240
PP_CB, PP_LG, PP_LB, PP_GN, PP_ZE, PP_EPS, PP_COEF, NPP = 0, 8, 16, 24, 40, 104, 108, 120
GROUPS = [[0, 1, 2, 3], [4, 5, 6, 7]]


class Op:
    __slots__ = ("eng", "fn", "reads", "writes", "is_dma", "ninc", "deps",
                 "needs_inc", "sem", "val", "idx", "semkey", "barrier")

    def __init__(self, eng, fn, reads, writes, is_dma, ninc, semkey):
        self.eng = eng
        self.fn = fn
        self.reads = tuple(reads)
        self.writes = tuple(writes)
        self.is_dma = is_dma
        self.ninc = ninc
        self.deps = ()
        self.needs_inc = False
        self.sem = None
        self.val = 0
        self.semkey = semkey
        self.barrier = False


class Prog:
    def __init__(self):
        self.ops = []

    def op(self, eng, fn, reads=(), writes=()):
        self.ops.append(Op(eng, fn, reads, writes, False, 1, None))

    def dma(self, eng, fn, reads=(), writes=(), semkey=None, ninc=16):
        assert semkey is not None
        self.ops.append(Op(eng, fn, reads, writes, True, ninc, semkey))

    def barrier(self):
        o = Op(None, None, (), (), False, 0, None)
        o.barrier = True
        self.ops.append(o)

    def finalize(self):
        last_w = {}
        readers = {}
        last_on = {}
        pending = {e: [] for e in ENG_NAMES}
        for i, o in enumerate(self.ops):
            o.idx = i
            if o.barrier:
                lst = list(last_on.values())
                for e in ENG_NAMES:
                    pending[e] = list(lst)
                continue
            deps = set()
            for r in o.reads:
                if r in last_w:
                    deps.add(last_w[r])
            for w in o.writes:
                if w in last_w:
                    deps.add(last_w[w])
                for rd in readers.get(w, ()):
                    deps.add(rd)
            if pending[o.eng]:
                deps.update(pending[o.eng])
                pending[o.eng] = []
            deps.discard(i)
            real = []
            for d in deps:
                dop = self.ops[d]
                if dop.fn is None:
                    continue
                if dop.is_dma or dop.eng != o.eng or o.eng != "pe" or o.is_dma:
                    real.append(d)
                    dop.needs_inc = True
            o.deps = tuple(sorted(real))
            for w in o.writes:
                last_w[w] = i
                readers[w] = []
            for r in o.reads:
                readers.setdefault(r, []).append(i)
            if o.fn is not None:
                last_on[("dma", o.semkey) if o.is_dma else ("eng", o.eng)] = i
        cnt = {}
        for o in self.ops:
            if o.barrier:
                continue
            if o.is_dma:
                key = ("dma", o.semkey)
                cnt[key] = cnt.get(key, 0) + o.ninc
                o.val = cnt[key]
            elif o.needs_inc:
                key = ("eng", o.eng)
                cnt[key] = cnt.get(key, 0) + 1
                o.val = cnt[key]
        self.dma_keys = sorted({o.semkey for o in self.ops if o.is_dma}, key=str)

    def emit(self, nc, stack):
        self.finalize()
        sems = {}
        for e in ENG_NAMES:
            sems[("eng", e)] = stack.enter_context(nc.semaphore("s_" + e))
        for k in self.dma_keys:
            sems[("dma", k)] = stack.enter_context(nc.semaphore("d_" + str(k)))
        for o in self.ops:
            if o.barrier:
                continue
            o.sem = sems[("dma", o.semkey)] if o.is_dma else sems[("eng", o.eng)]
        block = stack.enter_context(nc.Block())
        by_eng = {e: [o for o in self.ops if o.eng == e] for e in ENG_NAMES}
        ops = self.ops

        def run(engine, lst):
            waited = {}
            for o in lst:
                need = {}
                for d in o.deps:
                    dop = ops[d]
                    k = id(dop.sem)
                    if dop.val > need.get(k, (0, None))[0]:
                        need[k] = (dop.val, dop.sem)
                for k, (v, s) in need.items():
                    if waited.get(k, 0) < v:
                        engine.wait_ge(s, v)
                        waited[k] = v
                if o.fn is None:
                    continue
                if o.is_dma:
                    o.fn(engine, o.sem)
                else:
                    ins = o.fn(engine)
                    if o.needs_inc:
                        ins.then_inc(o.sem, 1)

        @block.tensor
        def _(e):
            run(e, by_eng["pe"])

        @block.scalar
        def _(e):
            run(e, by_eng["act"])

        @block.vector
        def _(e):
            run(e, by_eng["dve"])

        @block.gpsimd
        def _(e):
            run(e, by_eng["pool"])

        @block.sync
        def _(e):
            run(e, by_eng["sp"])


def log_gammas():
    return [math.log1p(-2.0 ** (-5.0 - h)) for h in range(4)]


def build_program():
    nc = bass.Bass("TRN2", target_bir_lowering=False)
    P = Prog()
    LG = log_gammas()

    def din(name, shape, dt=F32):
        return nc.dram_tensor(name, shape, dt, kind="ExternalInput").ap()

    x = din("x", [S_CORE, D])
    xh = din("xh", [HALO, D])
    w_in = din("w_in", [D, NCOLS])
    w_cp = din("w_cp", [D, D])
    w_rp = din("w_rp", [2 * D, D])
    w_out = din("w_out", [D, D])
    gainB_d = din("gainB", [128, D])
    fgainB_d = din("fgainB", [128, D])
    pp_d = din("pp", [128, NPP])
    convw_d = din("convw", [128, 256])
    i4_d = din("i4", [128, 32])
    mask_d = din("maskT", [128, 4 * 128])
    ident_d = din("ident", [128, 128])
    cos_d = din("cosT", [128, S_CORE])
    sin_d = din("sinT", [128, S_CORE])
    out_d = nc.dram_tensor("out", [S_CORE, D], F32, kind="ExternalOutput").ap()
    st_in = [nc.dram_tensor(f"st_in{h}", [128, 1024], F32, kind="Internal").ap() for h in range(4)]
    st_out = [nc.dram_tensor(f"st_out{h}", [4 * 128, 1024], F32, kind="Internal").ap() for h in range(4)]
    og_d = nc.dram_tensor("og_spill", [S_CORE, 2 * D], BF16, kind="Internal").ap()

    w_in_v = w_in.rearrange("(kc p) n -> p kc n", p=128)

    with ExitStack() as gst:
        def sb(name, shape, dt=F32, st=None):
            return (st or gst).enter_context(nc.sbuf_tensor("sb_" + name, shape, dt))

        def pst(name, shape, dt=F32):
            return gst.enter_context(nc.psum_tensor("ps_" + name, shape, dt))

        hT = sb("hT", [128, 8, HALO + S_CORE], BF16)
        ident = sb("ident", [128, 128])
        identb = sb("identb", [128, 128], BF16)
        pp = sb("pp", [128, NPP])
        eps6 = sb("eps6", [128, 1])
        eps5 = sb("eps5", [128, 1])
        psA = pst("psA", [128, 2, 512])
        psB = pst("psB", [128, 2, 512])
        psS = pst("psS", [128, 2, 512])
        psX = pst("psX", [128, 512])
        psM = pst("psM", [128, 512])
        sps = psM[:, 128:256]
        otr = psM[:, 256:512].bitcast(BF16).rearrange("p (c d) -> p c d", c=4)
        ktr = [psX[:, 0:128].bitcast(BF16).rearrange("p (c d) -> p c d", c=2),
               psM[:, 0:128].bitcast(BF16).rearrange("p (c d) -> p c d", c=2)]
        KTRB = ["bankX", "bankM"]
        ktr4 = ktr + [psB[:, 0, 0:128].bitcast(BF16).rearrange("p (c d) -> p c d", c=2),
                      psB[:, 1, 0:128].bitcast(BF16).rearrange("p (c d) -> p c d", c=2)]
        KTRB4 = KTRB + ["psB0", "psB1"]

        def ld(q, dst, src, key, reads=()):
            P.dma(q, lambda e, s: e.dma_start(out=dst, in_=src).then_inc(s, 16),
                  reads=reads, writes=[key], semkey=key)

        ld("sp", ident[:], ident_d, "ident")
        ld("sp", pp[:], pp_d, "pp")
        P.op("dve", lambda e: e.tensor_copy(out=identb[:], in_=ident[:]), reads=["ident"], writes=["identb"])
        P.op("pool", lambda e: e.memset(eps6[:], 1e-6), writes=["eps6"])
        P.op("pool", lambda e: e.memset(eps5[:], 1e-5), writes=["eps5"])

        st01 = ExitStack()
        st01.__enter__()
        Wkv = sb("Wkv", [128, 8, 768], BF16, st=st01)
        Wqr = sb("Wqr", [128, 8, 768], BF16, st=st01)

        def load_w(buf, key, cols):
            for (do, sc, n) in cols:
                P.dma("pool", lambda e, s, do=do, sc=sc, n=n: e.dma_start(
                    out=buf[:, :, do:do + n], in_=w_in_v[:, :, sc:sc + n]).then_inc(s, 16),
                    writes=[f"{key}_{do}"], semkey=f"{key}_{do}")
        load_w(Wkv, "Wkv", [(0, C_K, 256), (256, C_V, 512)])
        with ExitStack() as st0:
            gainB = sb("gainB", [128, D], st=st0)
            xs = [sb(f"xs{i}", [128, D], st=st0) for i in range(3)]
            xg = [sb(f"xg{i}", [128, D], st=st0) for i in range(3)]
            junk = sb("junk", [128, D], st=st0)
            ss = [sb(f"ss{i}", [128, 1], st=st0) for i in range(3)]
            rs = [sb(f"rs{i}", [128, 1], st=st0) for i in range(3)]
            dg = [sb(f"dg{i}", [128, 128], st=st0) for i in range(3)]
            ld("sp", gainB[:], gainB_d, "gainB")
            pss = [psA, psB, psS]
            def p0_a(t):
                rows = HALO if t == 0 else 128
                src = xh if t == 0 else x[(t - 1) * 128:t * 128, :]
                s = t % 3
                ld("sp", xs[s][0:rows, :], src, f"xs{s}")
                P.op("act", lambda e, s=s, rows=rows: e.activation(
                    out=junk[0:rows, :], in_=xs[s][0:rows, :], func=AF.Square, accum_out=ss[s][0:rows, :]),
                    reads=[f"xs{s}"], writes=["junk", f"ss{s}"])
                P.op("act", lambda e, s=s, rows=rows: e.activation(
                    out=ss[s][0:rows, :], in_=ss[s][0:rows, :], func=AF.Sqrt, bias=eps6[0:rows, :], scale=1.0 / D),
                    reads=[f"ss{s}", "eps6"], writes=[f"ss{s}"])
                P.op("dve", lambda e, s=s, rows=rows: e.reciprocal(out=rs[s][0:rows, :], in_=ss[s][0:rows, :]),
                     reads=[f"ss{s}"], writes=[f"rs{s}"])
                P.op("dve", lambda e, s=s, rows=rows: e.tensor_scalar(
                    out=dg[s][0:rows, 0:rows], in0=ident[0:rows, 0:rows], scalar1=rs[s][0:rows, :], scalar2=None,
                    op0=ALU.mult), reads=[f"rs{s}", "ident"], writes=[f"dg{s}"])
                P.op("pool", lambda e, s=s, rows=rows: e.tensor_tensor(
                    out=xg[s][0:rows, :], in0=xs[s][0:rows, :], in1=gainB[0:rows, :], op=ALU.mult),
                    reads=[f"xs{s}", "gainB"], writes=[f"xg{s}"])

            def p0_b(t):
                rows = HALO if t == 0 else 128
                col0 = 0 if t == 0 else HALO + (t - 1) * 128
                s = t % 3

                def tr(e, s=s, rows=rows):
                    last = None
                    for dc in range(8):
                        last = e.matmul(pss[s][:, dc // 4, (dc % 4) * 128:(dc % 4) * 128 + rows],
                                        lhsT=xg[s][0:rows, dc * 128:(dc + 1) * 128],
                                        rhs=dg[s][0:rows, 0:rows], start=True, stop=True)
                    return last
                P.op("pe", tr, reads=[f"xg{s}", f"dg{s}"], writes=[f"ps{'ABS'[s]}0", f"ps{'ABS'[s]}1"])
                for half in range(2):
                    eng = "act" if half == 0 else "dve"

                    def ev(e, s=s, rows=rows, half=half, col0=col0, eng=eng):
                        src_ap = pss[s][:, half, :].rearrange("p (c t) -> p c t", c=4)[:, :, 0:rows]
                        dst_ap = hT[:, half * 4:(half + 1) * 4, col0:col0 + rows]
                        if eng == "act":
                            return e.activation(out=dst_ap, in_=src_ap, func=AF.Copy)
                        return e.tensor_copy(out=dst_ap, in_=src_ap)
                    P.op(eng, ev, reads=[f"ps{'ABS'[s]}{half}"], writes=[f"hT_{t}_{half}"])

            p0_a(0)
            p0_a(1)
            for t in range(NT + 1):
                if t + 2 < NT + 1:
                    p0_a(t + 2)
                p0_b(t)
                if t == 12:
                    load_w(Wqr, "Wqr", [(0, C_Q, 256), (256, C_R, 512)])
        HT_ALL = [f"hT_{t}_{half}" for t in range(NT + 1) for half in range(2)]

        def ht_keys(tiles):
            return [f"hT_{t + 1}_{half}" for t in tiles for half in range(2)]
        P.barrier()

        with ExitStack() as st1:
            maskT = sb("maskT", [128, 4, 128], st=st1)
            ld("sp", maskT[:], mask_d.rearrange("p (h i) -> p h i", h=4), "maskT")
            dgc = sb("dgc", [128, 12, 128], st=st1)
            for hr in range(12):
                P.op("dve", lambda e, hr=hr: e.tensor_scalar(
                    out=dgc[:, hr, :], in0=ident[:], scalar1=pp[:, PP_COEF + hr:PP_COEF + hr + 1], scalar2=None,
                    op0=ALU.mult), reads=["ident", "pp"], writes=[f"dgc{hr}"])
            kT2 = [sb(f"kT{i}", [128, 2, S_CORE], BF16, st=st1) for i in range(2)]
            qT = sb("qT", [128, 2, S_CORE], BF16, st=st1)
            kend = sb("kend", [128, NT, 256], BF16, st=st1)
            vv2 = [sb(f"vv{i}", [128, NT, 512], BF16, st=st1) for i in range(2)]
            srall = sb("srall", [128, NT, 512], BF16, st=st1)
            cosS = sb("cosS", [128, S_CORE], st=st1)
            sinS = sb("sinS", [128, S_CORE], st=st1)
            ld("sp", cosS[:], cos_d, "cosS")
            ld("sp", sinS[:], sin_d, "sinS")
            rt = [[sb(f"rt{i}_{k}", [128, 512], st=st1) for k in range(4)] for i in range(1)]
            Rloc = sb("Rloc", [128, 2, 512], st=st1)
            Rg = sb("Rg", [128, 3, 1024], st=st1)
            Rb = sb("Rb", [128, 2, 512], BF16, st=st1)
            sT = [sb(f"sT{i}", [128, 128], BF16, st=st1) for i in range(2)]
            on = [sb(f"on{i}", [128, 512], st=st1) for i in range(2)]
            og = [sb(f"og{i}", [128, 512], BF16, st=st1) for i in range(4)]
            st6 = [sb(f"st6_{i}", [128, 6], st=st1) for i in range(2)]
            mv = [sb(f"mv{i}", [128, 2], st=st1) for i in range(2)]
            rsd = [sb(f"rsd{i}", [128, 1], st=st1) for i in range(2)]
            sdv = [sb(f"sdv{i}", [128, 1], st=st1) for i in range(2)]
            nmr = [sb(f"nmr{i}", [128, 1], st=st1) for i in range(2)]
            tabcnt = [0]

            def proj_rot(W, wkeys, woff, dst, dstkey, tb, ceng="pool", pp_=None, ppk="psA"):
                pp_ = psA if pp_ is None else pp_
                ti = tabcnt[0] % 2
                tabcnt[0] += 1
                cs_ap = cosS[:, tb * 512:(tb + 1) * 512]
                sn_ap = sinS[:, tb * 512:(tb + 1) * 512]
                c0 = HALO + tb * 512

                def mmf(e):
                    last = None
                    for c in range(2):
                        for kc in range(8):
                            last = e.matmul(pp_[:, c, :], lhsT=W[:, kc, woff + c * 128:woff + (c + 1) * 128],
                                            rhs=hT[:, kc, c0:c0 + 512], start=(kc == 0), stop=(kc == 7))
                    return last
                P.op("pe", mmf, reads=wkeys + ht_keys(range(tb * 4, tb * 4 + 4)), writes=[ppk + "0", ppk + "1"])
                r = rt[0]
                ri = 0
                P.op("dve", lambda e: e.tensor_tensor(out=r[0][:], in0=pp_[:, 0, :], in1=cs_ap, op=ALU.mult),
                     reads=[ppk + "0", "cosS"], writes=[f"rt{ri}0"])
                P.op("dve", lambda e: e.tensor_tensor(out=r[1][:], in0=pp_[:, 1, :], in1=sn_ap, op=ALU.mult),
                     reads=[ppk + "1", "sinS"], writes=[f"rt{ri}1"])
                P.op("dve", lambda e: e.tensor_tensor(out=r[2][:], in0=pp_[:, 1, :], in1=cs_ap, op=ALU.mult),
                     reads=[ppk + "1", "cosS"], writes=[f"rt{ri}2"])
                P.op("dve", lambda e: e.tensor_tensor(out=r[3][:], in0=pp_[:, 0, :], in1=sn_ap, op=ALU.mult),
                     reads=[ppk + "0", "sinS"], writes=[f"rt{ri}3"])
                P.op(ceng, lambda e: e.tensor_tensor(out=dst[:, 0, tb * 512:(tb + 1) * 512], in0=r[0][:], in1=r[1][:],
                                                     op=ALU.subtract),
                     reads=[f"rt{ri}0", f"rt{ri}1"], writes=[f"{dstkey}0_{tb}"])
                P.op(ceng, lambda e: e.tensor_tensor(out=dst[:, 1, tb * 512:(tb + 1) * 512], in0=r[2][:], in1=r[3][:],
                                                     op=ALU.add),
                     reads=[f"rt{ri}2", f"rt{ri}3"], writes=[f"{dstkey}1_{tb}"])

            WKV_KEYS_K = ["Wkv_0"]
            WKV_KEYS_V = ["Wkv_256"]
            WQR_KEYS_Q = ["Wqr_0"]
            WQR_KEYS_R = ["Wqr_256"]
            for h in range(4):
                lg = LG[h]
                hp = h % 2
                kT = kT2[hp]
                vv = vv2[hp]
                KT = f"kT{hp}c"
                VV = f"vv{hp}_"

                def p1_vproj(n, par=hp, use_x=False):
                    vb = n % 2
                    dst_ps = psX[:, :] if use_x else psB[:, vb, :]
                    pkey = "bankX" if use_x else f"psB{vb}"
                    vdst = vv2[par]

                    def mmv(e, n=n, dst_ps=dst_ps):
                        last = None
                        for kc in range(8):
                            last = e.matmul(dst_ps, lhsT=hT[:, kc, HALO + n * 128:HALO + (n + 1) * 128],
                                            rhs=Wkv[:, kc, 256:768], start=(kc == 0), stop=(kc == 7))
                        return last
                    P.op("pe", mmv, reads=WKV_KEYS_V + ht_keys([n]), writes=[pkey])
                    P.op("act", lambda e, n=n, dst_ps=dst_ps, vdst=vdst: e.activation(out=vdst[:, n, :], in_=dst_ps, func=AF.Copy),
                         reads=[], writes=[f"vv{par}_{n}", pkey])

                def p1_trk(n, h=h):
                    tb = n // 4
                    slots, skeys = (ktr, KTRB) if h == 0 else (ktr4, KTRB4)
                    ks = n % len(slots)
                    kt_ap, kt_key = slots[ks], skeys[ks]

                    def trk(e, n=n, kT=kT, kt_ap=kt_ap):
                        last = None
                        for c in range(2):
                            last = e.transpose(kt_ap[:, c, :], kT[:, c, n * 128:(n + 1) * 128], identb[:])
                        return last
                    P.op("pe", trk, reads=[f"{KT}0_{tb}", f"{KT}1_{tb}", "identb"], writes=[kt_key])
                    P.op("act", lambda e, n=n, h=h, kt_ap=kt_ap: e.activation(
                        out=kend[:, n, :], in_=kt_ap[:].rearrange("p c d -> p (c d)"), func=AF.Copy,
                        scale=pp[:, PP_ZE + h * 16 + n:PP_ZE + h * 16 + n + 1]),
                        reads=["pp"], writes=[f"kend{n}", kt_key])

                def p1_st(n):
                    def mms(e, n=n, vv=vv):
                        last = None
                        for c in range(2):
                            last = e.matmul(psS[:, c, :], lhsT=kend[:, n, c * 128:(c + 1) * 128], rhs=vv[:, n, :],
                                            start=(n == 0), stop=True, skip_group_check=True)
                        return last
                    P.op("pe", mms, reads=[f"kend{n}", f"{VV}{n}"], writes=["psS0", "psS1"])

                if h == 0:
                    for tb in range(4):
                        proj_rot(Wkv, WKV_KEYS_K, 0, kT, KT, tb)
                        for n in range(tb * 4, tb * 4 + 4):
                            p1_vproj(n)
                        if tb >= 1:
                            for n in range((tb - 1) * 4, tb * 4):
                                p1_trk(n)
                            for n in range((tb - 1) * 4, tb * 4):
                                p1_st(n)
                    for n in range(12, 16):
                        p1_trk(n)
                    for n in range(12, 16):
                        p1_st(n)
                else:
                    for g in range(4):
                        for n in range(g * 4, g * 4 + 4):
                            p1_trk(n)
                        for n in range(g * 4, g * 4 + 4):
                            p1_st(n)
                for c in range(2):
                    P.op("dve", lambda e, c=c: e.tensor_copy(out=Rloc[:, c, :], in_=psS[:, c, :]),
                         reads=[f"psS{c}"], writes=[f"Rloc{c}"])
                P.dma("sp", lambda e, s, h=h: e.dma_start(
                    out=st_in[h], in_=Rloc[:].rearrange("p c e -> p (c e)")).then_inc(s, 16),
                    reads=["Rloc0", "Rloc1"], writes=[f"st_in{h}"], semkey=f"st_in{h}")
                P.dma("pool", lambda e, s, h=h: e.collective_compute(
                    "AllGather", ALU.bypass, replica_groups=GROUPS, ins=[st_in[h]], outs=[st_out[h]]).then_inc(s, 1),
                    reads=[f"st_in{h}"], writes=[f"st_out{h}"], semkey=f"cc{h}", ninc=1)
                if h < 3:
                    load_w(Wkv, "Wkv", [(0, C_K + 256 * (h + 1), 256), (256, C_V + 512 * (h + 1), 512)])

                def p2_rproj(n):
                    rb = n % 2

                    def mmr(e, n=n, rb=rb):
                        last = None
                        for kc in range(8):
                            last = e.matmul(psB[:, rb, :], lhsT=hT[:, kc, HALO + n * 128:HALO + (n + 1) * 128],
                                            rhs=Wqr[:, kc, 256:768], start=(kc == 0), stop=(kc == 7))
                        return last
                    P.op("pe", mmr, reads=WQR_KEYS_R + ht_keys([n]), writes=[f"psB{rb}"])
                    P.op("act", lambda e, rb=rb, n=n: e.activation(out=srall[:, n, :], in_=psB[:, rb, :], func=AF.Silu),
                         reads=[f"psB{rb}"], writes=[f"sr{n}"])

                for tb in range(4):
                    proj_rot(Wqr, WQR_KEYS_Q, 0, qT, "qT", tb, ceng="dve")
                    for n in range(tb * 4, tb * 4 + 4):
                        p2_rproj(n)

                P.dma("sp", lambda e, s, h=h: e.dma_start(
                    out=Rg[:], in_=st_out[h][0:384, :].rearrange("(r p) f -> p r f", p=128)).then_inc(s, 16),
                    reads=[f"st_out{h}"], writes=["Rg"], semkey="Rg")

                def inj(e, h=h):
                    last = None
                    for c in range(2):
                        for r in range(3):
                            last = e.matmul(psS[:, c, :], lhsT=dgc[:, h * 3 + r, :], rhs=Rg[:, r, c * 512:(c + 1) * 512],
                                            start=(r == 0), stop=(r == 2), skip_group_check=True)
                    return last
                P.op("pe", inj, reads=["Rg"] + [f"dgc{h * 3 + r}" for r in range(3)] + ["Rloc0", "Rloc1"],
                     writes=["psS0", "psS1"])

                sps_h, spk = (psX[:, 0:128], "bankX") if h == 3 else (sps, "bankM")

                def p2_scores(n, h=h, sps_h=sps_h, spk=spk):
                    tb = n // 4
                    b = n % 2

                    def mmsc(e, n=n, b=b, kT=kT, sps_h=sps_h):
                        last = None
                        for c in range(2):
                            last = e.matmul(sps_h, lhsT=kT[:, c, n * 128:(n + 1) * 128], rhs=qT[:, c, n * 128:(n + 1) * 128],
                                            start=(c == 0), stop=(c == 1))
                        return last
                    P.op("pe", mmsc, reads=[f"{KT}0_{tb}", f"{KT}1_{tb}", f"qT0_{tb}", f"qT1_{tb}"], writes=[spk])
                    P.op("dve", lambda e, h=h, b=b, sps_h=sps_h: e.tensor_tensor(out=sT[b][:], in0=sps_h, in1=maskT[:, h, :],
                                                                              op=ALU.mult),
                         reads=["maskT"], writes=[f"sT{b}", spk])

                def p2_rb(n, lg=lg):
                    scl = math.exp(lg * 128.0 * (n - 16))
                    P.op("act", lambda e, scl=scl: e.activation(out=Rb[:, 0, :], in_=psS[:, 0, :], func=AF.Copy, scale=float(scl)),
                         reads=["psS0"], writes=["Rb0"])
                    P.op("dve", lambda e, scl=scl: e.tensor_scalar(out=Rb[:, 1, :], in0=psS[:, 1, :], scalar1=float(scl), scalar2=None,
                                                                   op0=ALU.mult),
                         reads=["psS1"], writes=["Rb1"])

                OB = [(psA[:, 0, :], "psA0"), (psA[:, 1, :], "psA1")]
                if h == 3:
                    OB = OB + [(psB[:, 0, :], "psB0"), (psB[:, 1, :], "psB1")]

                def p2_o(n, OB=OB):
                    tb = n // 4
                    b = n % 2
                    oap, okey = OB[n % len(OB)]

                    P.op("pe", lambda e, n=n, b=b, vv=vv, oap=oap: e.matmul(oap, lhsT=sT[b][:], rhs=vv[:, n, :],
                                                                             start=True, stop=False),
                         reads=[f"sT{b}", f"{VV}{n}"], writes=[okey])

                    def mmo(e, n=n, b=b, oap=oap):
                        last = None
                        for c in range(2):
                            last = e.matmul(oap, lhsT=qT[:, c, n * 128:(n + 1) * 128], rhs=Rb[:, c, :],
                                            start=False, stop=(c == 1))
                        return last
                    P.op("pe", mmo, reads=[f"qT0_{tb}", f"qT1_{tb}", "Rb0", "Rb1"], writes=[okey])
                    if n < NT - 1:
                        def mms2(e, n=n, vv=vv):
                            last = None
                            for c in range(2):
                                last = e.matmul(psS[:, c, :], lhsT=kend[:, n, c * 128:(c + 1) * 128], rhs=vv[:, n, :],
                                                start=False, stop=True, skip_group_check=True)
                            return last
                        P.op("pe", mms2, reads=[f"kend{n}", f"{VV}{n}", "Rb0", "Rb1"], writes=["psS0", "psS1"])

                def p2_stats_a(n, OB=OB):
                    b = n % 2
                    oap, okey = OB[n % len(OB)]
                    P.op("dve", lambda e, b=b, oap=oap: e.bn_stats(out=st6[b][:], in_=oap), reads=[okey], writes=[f"st6_{b}"])

                def p2_stats_b(n):
                    b = n % 2
                    P.op("dve", lambda e, b=b: e.bn_aggr(out=mv[b][:], in_=st6[b][:]), reads=[f"st6_{b}"], writes=[f"mv{b}"])

                def p2_norm_act1(n, h=h):
                    b = n % 2
                    P.op("act", lambda e, h=h, b=b: e.activation(
                        out=sdv[b][:], in_=mv[b][:, 1:2], func=AF.Sqrt, bias=pp[:, PP_EPS + h:PP_EPS + h + 1], scale=1.0),
                        reads=[f"mv{b}", "pp"], writes=[f"sdv{b}"])

                def p2_norm_dve(n):
                    b = n % 2
                    P.op("dve", lambda e, b=b: e.reciprocal(out=rsd[b][:], in_=sdv[b][:]), reads=[f"sdv{b}"], writes=[f"rsd{b}"])
                    P.op("dve", lambda e, b=b: e.scalar_tensor_tensor(
                        out=nmr[b][:], in0=mv[b][:, 0:1], scalar=-1.0, in1=rsd[b][:], op0=ALU.mult, op1=ALU.mult),
                        reads=[f"mv{b}", f"rsd{b}"], writes=[f"nmr{b}"])

                def p2_norm_act2(n, OB=OB):
                    b = n % 2
                    oap, okey = OB[n % len(OB)]
                    P.op("act", lambda e, b=b, oap=oap: e.activation(out=on[b][:], in_=oap, func=AF.Identity,
                                                                       bias=nmr[b][:], scale=rsd[b][:]),
                         reads=[okey, f"rsd{b}", f"nmr{b}"], writes=[f"on{b}"])
                    P.op("pool", lambda e, b=b, n=n: e.tensor_tensor(out=og[n % 4][:], in0=on[b][:], in1=srall[:, n, :], op=ALU.mult),
                         reads=[f"on{b}", f"sr{n}"], writes=[f"og{n % 4}"])

                def p2_tr(n, h=h):
                    ob = n % 4
                    P.dma("sp", lambda e, s, ob=ob, h=h, n=n: e.dma_start(
                        out=og_d[n * 128:(n + 1) * 128, h * 512:(h + 1) * 512], in_=og[ob][:]).then_inc(s, 16),
                        reads=[f"og{ob}"], writes=[f"ogd_{n}_{h}"], semkey=f"ogd{ob}")

                p2_scores(0)
                p2_rb(0)
                for n in range(NT):
                    if n + 1 < NT:
                        p2_scores(n + 1)
                    if n >= 1:
                        p2_norm_act1(n - 1)
                        p2_norm_dve(n - 1)
                    p2_o(n)
                    p2_stats_a(n)
                    if n + 1 < NT:
                        p2_rb(n + 1)
                    p2_stats_b(n)
                    if n >= 1:
                        p2_norm_act2(n - 1)
                    if n >= 2:
                        p2_tr(n - 2)
                    if h < 3:
                        if n % 4 == 0:
                            proj_rot(Wkv, WKV_KEYS_K, 0, kT2[1 - hp], f"kT{1 - hp}c", n // 4, pp_=psB, ppk="psB")
                        p1_vproj(n, par=1 - hp, use_x=True)
                p2_norm_act1(NT - 1)
                p2_norm_dve(NT - 1)
                p2_norm_act2(NT - 1)
                p2_tr(NT - 2)
                p2_tr(NT - 1)
                if h < 3:
                    load_w(Wqr, "Wqr", [(0, C_Q + 256 * (h + 1), 256), (256, C_R + 512 * (h + 1), 512)])
        P.barrier()
        st01.close()

        u2 = sb("u2", [128, 8, S_CORE], BF16)
        Wc = [sb(f"Wc{i}", [128, 8, 128], BF16) for i in range(2)]
        Wa = [sb(f"Wa{i}", [128, 8, 128], BF16) for i in range(2)]
        w_cp_v = w_cp.rearrange("(kc p) n -> p kc n", p=128)

        def load_wca(dcol):
            s = dcol % 2
            P.dma("pool", lambda e, s_, s=s, dcol=dcol: e.dma_start(
                out=Wc[s][:], in_=w_cp_v[:, :, dcol * 128:(dcol + 1) * 128]).then_inc(s_, 16),
                writes=[f"Wc{s}"], semkey=f"Wc{s}")
            P.dma("pool", lambda e, s_, s=s, dcol=dcol: e.dma_start(
                out=Wa[s][:], in_=w_in_v[:, :, C_GA + dcol * 128:C_GA + (dcol + 1) * 128]).then_inc(s_, 16),
                writes=[f"Wa{s}"], semkey=f"Wa{s}")
        with ExitStack() as st3:
            Wg = [sb(f"Wg{i}", [128, 8, 128], BF16, st=st3) for i in range(2)]

            def load_wg(cc):
                s = cc % 2
                P.dma("pool", lambda e, s_, s=s, cc=cc: e.dma_start(
                    out=Wg[s][:], in_=w_in_v[:, :, C_AGATE + cc * 128:C_AGATE + (cc + 1) * 128]).then_inc(s_, 16),
                    writes=[f"Wg{s}"], semkey=f"Wg{s}")
            cv = sb("cv", [128, 8, S_CORE], st=st3)
            convw = sb("convw", [128, 256], st=st3)
            ld("sp", convw[:], convw_d, "convw")
            i4 = sb("i4", [128, 32], st=st3)
            ld("sp", i4[:], i4_d, "i4")
            onesf = sb("onesf", [128, 128], st=st3)
            P.op("pool", lambda e: e.memset(onesf[:], 1.0 / D), writes=["onesf"])
            with ExitStack() as st3a:
                Wag = [sb(f"Wag{i}", [128, 8, 256], BF16, st=st3a) for i in range(2)]
                uT = [sb(f"uT{i}", [128, HALO + S_CORE], BF16, st=st3a) for i in range(2)]
                Ust = [sb(f"Ust{i}", [128, 4, HALO + S_CORE], BF16, st=st3a) for i in range(2)]
                Wp = [sb(f"Wp{i}", [128, 32, 32], BF16, st=st3a) for i in range(2)]
                for i in range(2):
                    P.op("pool", lambda e, i=i: e.memset(Ust[i][:, :, HALO + S_CORE - 4:HALO + S_CORE], 0.0), writes=[f"U{i}"])
                sgl = [sb(f"sgl{i}", [128, 512], st=st3a) for i in range(2)]
                def load_wag(cc):
                    s = cc % 2
                    for part, colbase in ((1, C_AGLU), (0, C_AVAL)):
                        P.dma("pool", lambda e, s_, s=s, part=part, colbase=colbase, cc=cc: e.dma_start(
                            out=Wag[s][:, :, part * 128:(part + 1) * 128],
                            in_=w_in_v[:, :, colbase + cc * 128:colbase + (cc + 1) * 128]).then_inc(s_, 16),
                            writes=[f"Wag{s}_{part}"], semkey=f"Wag{s}_{part}")
                def conv_mm(cc):
                    s = cc % 2
                    for tb in range(4):
                        b = tb % 2

                        def mmc(e, s=s, tb=tb, b=b):
                            last = None
                            for m in range(8):
                                o0 = 2 + 4 * m + tb * 512
                                for g in range(4):
                                    last = e.matmul(psS[32 * g:32 * (g + 1), b, :], lhsT=Wp[s][:, g * 8 + m, :],
                                                    rhs=Ust[s][:, g, o0:o0 + 512], start=(m == 0), stop=(m == 7),
                                                    tile_position=(0, 32 * g))
                            return last
                        P.op("pe", mmc, reads=[f"Wp{s}_{gm}" for gm in range(32)] + [f"U{s}"], writes=[f"psS{b}"])
                        P.op("act", lambda e, b=b, cc=cc, tb=tb: e.activation(
                            out=cv[:, cc, tb * 512:(tb + 1) * 512], in_=psS[:, b, :], func=AF.Identity,
                            bias=pp[:, PP_CB + cc:PP_CB + cc + 1], scale=1.0),
                            reads=[f"psS{b}", "pp"], writes=[f"cv{cc}_{tb}"])

                load_wag(0)
                load_wag(1)
                for cc in range(8):
                    s = cc % 2
                    for blk in range(5):
                        c0 = 0 if blk == 0 else HALO + (blk - 1) * 512
                        n = HALO if blk == 0 else 512
                        hk = ["hT_0_0", "hT_0_1"] if blk == 0 else ht_keys(range((blk - 1) * 4, (blk - 1) * 4 + 4))

                        for part in (1, 0):
                            def mma(e, s=s, c0=c0, n=n, part=part):
                                last = None
                                for kc in range(8):
                                    last = e.matmul(psA[:, part, 0:n], lhsT=Wag[s][:, kc, part * 128:(part + 1) * 128],
                                                    rhs=hT[:, kc, c0:c0 + n], start=(kc == 0), stop=(kc == 7))
                                return last
                            P.op("pe", mma, reads=[f"Wag{s}_{part}"] + hk, writes=[f"psA{part}"])
                        b = blk % 2
                        P.op("act", lambda e, b=b, n=n: e.activation(out=sgl[b][:, 0:n], in_=psA[:, 1, 0:n], func=AF.Sigmoid),
                             reads=["psA1"], writes=[f"sgl{b}"])
                        P.op("dve", lambda e, b=b, n=n, s=s, c0=c0: e.tensor_tensor(
                            out=uT[s][:, c0:c0 + n], in0=psA[:, 0, 0:n], in1=sgl[b][:, 0:n], op=ALU.mult),
                            reads=["psA0", f"sgl{b}"], writes=[f"uT{s}_{blk}"])
                    for gm in range(32):
                        P.op("dve", lambda e, s=s, gm=gm, cc=cc: e.tensor_scalar(
                            out=Wp[s][:, gm, :], in0=i4[:], scalar1=convw[:, cc * 32 + gm:cc * 32 + gm + 1], scalar2=None,
                            op0=ALU.mult), reads=["i4", "convw"], writes=[f"Wp{s}_{gm}"])
                    def mku(e, s_, s=s):
                        for g in range(4):
                            for j in range(4):
                                e.dma_start(out=Ust[s][j * 32:(j + 1) * 32, g, 0:HALO + S_CORE - j],
                                            in_=uT[s][g * 32:(g + 1) * 32, j:HALO + S_CORE]).then_inc(s_, 16)
                    P.dma("sp", mku, reads=[f"uT{s}_{i}" for i in range(5)], writes=[f"U{s}"], semkey=f"U{s}", ninc=256)
                    if cc >= 1:
                        conv_mm(cc - 1)
                    if cc + 2 < 8:
                        load_wag(cc + 2)
                conv_mm(7)
            P.barrier()
            rstd_t = sb("rstd_t", [128, S_CORE], st=st3)
            nmr_t = sb("nmr_t", [128, S_CORE], st=st3)
            sga = [sb(f"sga{i}", [128, 512], st=st3) for i in range(2)]
            t1 = [sb(f"t1_{i}", [128, 512], st=st3) for i in range(2)]
            t2 = [sb(f"t2_{i}", [128, 512], st=st3) for i in range(2)]
            zz = [sb(f"zz{i}", [128, 512], st=st3) for i in range(2)]
            with ExitStack() as st3b:
                load_wg(0)
                load_wg(1)
                sq = [sb(f"sq{i}", [128, 512], BF16, st=st3b) for i in range(2)]
                cvb = [sb(f"cvb{i}", [128, 512], BF16, st=st3b) for i in range(2)]
                onesb = sb("onesb", [128, 128], BF16, st=st3b)
                P.op("dve", lambda e: e.tensor_copy(out=onesb[:], in_=onesf[:]), reads=["onesf"], writes=["onesb"])
                mean_t2 = [sb(f"mean_t{i}", [128, 512], st=st3b) for i in range(2)]
                msq_t = sb("msq_t", [128, 512], st=st3b)
                LNB = [(psA, "psA"), (psB, "psB")]

                def ln_front(tb):
                    pt, pk = LNB[tb % 2]
                    for cc in range(8):
                        b = cc % 2
                        P.op("dve", lambda e, b=b, cc=cc, tb=tb: e.tensor_copy(
                            out=cvb[b][:], in_=cv[:, cc, tb * 512:(tb + 1) * 512]),
                            reads=[f"cv{cc}_{tb}"], writes=[f"cvb{b}"])
                        P.op("pe", lambda e, b=b, cc=cc, pt=pt: e.matmul(pt[:, 0, :], lhsT=onesb[:], rhs=cvb[b][:],
                                                                         start=(cc == 0), stop=(cc == 7)),
                             reads=["onesb", f"cvb{b}"], writes=[pk + "0"])
                        P.op("act", lambda e, b=b, cc=cc, tb=tb: e.activation(
                            out=sq[b][:], in_=cv[:, cc, tb * 512:(tb + 1) * 512], func=AF.Square),
                            reads=[f"cv{cc}_{tb}"], writes=[f"sq{b}"])
                        P.op("pe", lambda e, b=b, cc=cc, pt=pt: e.matmul(pt[:, 1, :], lhsT=onesb[:], rhs=sq[b][:],
                                                                         start=(cc == 0), stop=(cc == 7)),
                             reads=["onesb", f"sq{b}"], writes=[pk + "1"])

                def ln_back(tb):
                    pt, pk = LNB[tb % 2]
                    sl = slice(tb * 512, (tb + 1) * 512)
                    mean_t = mean_t2[tb % 2]
                    mk = f"mean_t{tb % 2}"
                    P.op("dve", lambda e, pt=pt, mean_t=mean_t: e.tensor_copy(out=mean_t[:], in_=pt[:, 0, :]), reads=[pk + "0"], writes=[mk])
                    P.op("dve", lambda e, mean_t=mean_t: e.tensor_tensor(out=msq_t[:], in0=mean_t[:], in1=mean_t[:], op=ALU.mult),
                         reads=[mk], writes=["msq_t"])
                    P.op("dve", lambda e, pt=pt: e.tensor_tensor(out=msq_t[:], in0=pt[:, 1, :], in1=msq_t[:], op=ALU.subtract),
                         reads=[pk + "1", "msq_t"], writes=["msq_t"])
                    P.op("act", lambda e: e.activation(out=msq_t[:], in_=msq_t[:], func=AF.Ln, bias=eps5[:], scale=1.0),
                         reads=["msq_t", "eps5"], writes=["msq_t"])
                    P.op("act", lambda e, sl=sl: e.activation(out=rstd_t[:, sl], in_=msq_t[:], func=AF.Exp, scale=-0.5),
                         reads=["msq_t"], writes=[f"rstd_t{tb}"])
                    P.op("pool", lambda e, sl=sl, mean_t=mean_t: e.tensor_tensor(out=nmr_t[:, sl], in0=mean_t[:], in1=rstd_t[:, sl],
                                                                                op=ALU.mult),
                         reads=[mk, f"rstd_t{tb}"], writes=[f"nmr_t{tb}"])

                for tb in range(5):
                    if tb < 4:
                        ln_front(tb)
                    if tb >= 1:
                        ln_back(tb - 1)
            with ExitStack() as st3c:
                load_wca(0)
                load_wca(1)
                items = [(cc, tb) for cc in range(8) for tb in range(4)]

                def n_front(i):
                    cc, tb = items[i]
                    s = cc % 2
                    b = i % 2
                    c0 = HALO + tb * 512
                    sl = slice(tb * 512, (tb + 1) * 512)

                    def mmg2(e, s=s, b=b, c0=c0):
                        last = None
                        for kc in range(8):
                            last = e.matmul(psB[:, b, :], lhsT=Wg[s][:, kc, :], rhs=hT[:, kc, c0:c0 + 512],
                                            start=(kc == 0), stop=(kc == 7))
                        return last
                    P.op("pe", mmg2, reads=[f"Wg{s}"] + ht_keys(range(tb * 4, tb * 4 + 4)), writes=[f"psB{b}"])
                    P.op("dve", lambda e, b=b, cc=cc, sl=sl: e.tensor_tensor(
                        out=t1[b][:], in0=cv[:, cc, sl], in1=rstd_t[:, sl], op=ALU.mult),
                        reads=[f"cv{cc}_{tb}", f"rstd_t{tb}"], writes=[f"t1_{b}"])
                    P.op("dve", lambda e, b=b, sl=sl: e.tensor_tensor(
                        out=t2[b][:], in0=t1[b][:], in1=nmr_t[:, sl], op=ALU.subtract),
                        reads=[f"t1_{b}", f"nmr_t{tb}"], writes=[f"t2_{b}"])

                def n_back(i):
                    cc, tb = items[i]
                    b = i % 2
                    sl = slice(tb * 512, (tb + 1) * 512)
                    P.op("act", lambda e, b=b: e.activation(out=sga[b][:], in_=psB[:, b, :], func=AF.Silu),
                         reads=[f"psB{b}"], writes=[f"sga{b}"])
                    P.op("act", lambda e, b=b, cc=cc: e.activation(
                        out=zz[b][:], in_=t2[b][:], func=AF.Silu,
                        bias=pp[:, PP_LB + cc:PP_LB + cc + 1], scale=pp[:, PP_LG + cc:PP_LG + cc + 1]),
                        reads=[f"t2_{b}", "pp"], writes=[f"zz{b}"])
                    P.op("dve", lambda e, b=b, cc=cc, sl=sl: e.tensor_tensor(
                        out=u2[:, cc, sl], in0=zz[b][:], in1=sga[b][:], op=ALU.mult),
                        reads=[f"zz{b}", f"sga{b}"], writes=[f"u2_{cc}_{tb}"])

                for i in range(len(items) + 1):
                    if i < len(items):
                        n_front(i)
                    if i >= 1:
                        n_back(i - 1)
                    if i % 4 == 3 and i // 4 + 2 < 8:
                        load_wg(i // 4 + 2)
        P.barrier()
        mT = sb("mT", [128, 8, S_CORE], BF16)
        Wo = sb("Wo", [128, 8, D], BF16)
        with ExitStack() as st5:
            Wrp = sb("Wrp", [128, 16, D], BF16, st=st5)
            Wgb = sb("Wgb", [128, 8, D], BF16, st=st5)
            def load_wgb(half):
                P.dma("pool", lambda e, s, half=half: e.dma_start(
                    out=Wgb[:, :, half * 512:(half + 1) * 512],
                    in_=w_in_v[:, :, C_GB + half * 512:C_GB + (half + 1) * 512]).then_inc(s, 16),
                    writes=[f"Wgb{half}"], semkey=f"Wgb{half}")

            def load_wrp(ec):
                s = ec % 2
                ld("sp", wst[s][:], w_rp[ec * 128:(ec + 1) * 128, :], f"wst{s}")
                P.op("act", lambda e, s=s, ec=ec: e.activation(out=Wrp[:, ec, :], in_=wst[s][:], func=AF.Copy,
                                                               scale=pp[:, PP_GN + ec:PP_GN + ec + 1]),
                     reads=[f"wst{s}", "pp"], writes=[f"Wrp{ec}"])
            WRP_KEYS = [f"Wrp{ec}" for ec in range(16)]
            ogT0 = sb("ogT0", [128, 16, 512], BF16, st=st5)

            def rd_ogT_op(dst, key, tb):
                def rd_ogT(e, s_, dst=dst, tb=tb):
                    for ec in range(16):
                        e.dma_start_transpose(out=dst[:, ec, :],
                                              in_=og_d[tb * 512:(tb + 1) * 512, ec * 128:(ec + 1) * 128]).then_inc(s_, 16)
                P.dma("sp", rd_ogT, reads=[f"ogd_{n}_{h}" for n in range(tb * 4, tb * 4 + 4) for h in range(4)],
                      writes=[key], semkey=key, ninc=256)
            with ExitStack() as st3d:
                wst = [sb(f"wst{i}", [128, D], st=st3d) for i in range(2)]
                sgq = [sb(f"sgq{i}", [128, 512], st=st3d) for i in range(2)]
                ta = [sb(f"ta{i}", [128, 512], st=st3d) for i in range(2)]
                it = 0
                for dcol in range(8):
                    s = dcol % 2
                    for tb in range(4):
                        b = it % 2
                        it += 1
                        c0 = HALO + tb * 512
                        sl = slice(tb * 512, (tb + 1) * 512)

                        def mmga(e, s=s, b=b, c0=c0):
                            last = None
                            for kc in range(8):
                                last = e.matmul(psB[:, b, :], lhsT=Wa[s][:, kc, :], rhs=hT[:, kc, c0:c0 + 512],
                                                start=(kc == 0), stop=(kc == 7))
                            return last
                        P.op("pe", mmga, reads=[f"Wa{s}"] + ht_keys(range(tb * 4, tb * 4 + 4)), writes=[f"psB{b}"])
                        P.op("act", lambda e, b=b: e.activation(out=sgq[b][:], in_=psB[:, b, :], func=AF.Sigmoid),
                             reads=[f"psB{b}"], writes=[f"sgq{b}"])

                        def mmya(e, s=s, b=b, sl=sl):
                            last = None
                            for cc in range(8):
                                last = e.matmul(psA[:, b, :], lhsT=Wc[s][:, cc, :], rhs=u2[:, cc, sl],
                                                start=(cc == 0), stop=(cc == 7))
                            return last
                        P.op("pe", mmya, reads=[f"Wc{s}"] + [f"u2_{cc}_{tb}" for cc in range(8)], writes=[f"psA{b}"])
                        P.op("dve", lambda e, b=b, dcol=dcol, sl=sl: e.tensor_tensor(
                            out=mT[:, dcol, sl], in0=psA[:, b, :], in1=sgq[b][:], op=ALU.mult),
                            reads=[f"psA{b}", f"sgq{b}"], writes=[f"mT{dcol}_{tb}"])
                    if dcol + 2 < 8:
                        load_wca(dcol + 2)
                    load_wrp(2 * dcol)
                    load_wrp(2 * dcol + 1)
                    if dcol in (5, 6):
                        load_wgb(dcol - 5)
                    if dcol == 3:
                        rd_ogT_op(ogT0, "ogT0", 0)
            P.barrier()
            with ExitStack() as st2:
                ogT = [ogT0, sb("ogT1", [128, 16, 512], BF16, st=st2)]
                sg0 = sb("sg0", [128, 512], st=st2)
                tb0 = sb("tb_0", [128, 512], st=st2)
                sg = [sg0, sg0]
                tb_ = [tb0, tb0]
                for tb in range(4):
                    s = tb % 2
                    if tb == 0:
                        rd_ogT_op(ogT[1], "ogT1", 1)
                    elif tb + 1 < 4:
                        rd_ogT_op(ogT[(tb + 1) % 2], f"ogT{(tb + 1) % 2}", tb + 1)
                    c0 = HALO + tb * 512
                    for dcol in range(8):
                        b = dcol % 2

                        def mmg(e, dcol=dcol, b=b, c0=c0):
                            last = None
                            for kc in range(8):
                                last = e.matmul(psB[:, b, :], lhsT=Wgb[:, kc, dcol * 128:(dcol + 1) * 128],
                                                rhs=hT[:, kc, c0:c0 + 512], start=(kc == 0), stop=(kc == 7))
                            return last
                        P.op("pe", mmg, reads=["Wgb0", "Wgb1"] + ht_keys(range(tb * 4, tb * 4 + 4)), writes=[f"psB{b}"])
                        P.op("act", lambda e, b=b: e.activation(out=sg[b][:], in_=psB[:, b, :], func=AF.Sigmoid),
                             reads=[f"psB{b}"], writes=["sg0"])

                        def mmy(e, dcol=dcol, b=b, s=s):
                            last = None
                            for ec in range(16):
                                last = e.matmul(psA[:, b, :], lhsT=Wrp[:, ec, dcol * 128:(dcol + 1) * 128],
                                                rhs=ogT[s][:, ec, :], start=(ec == 0), stop=(ec == 15))
                            return last
                        P.op("pe", mmy, reads=WRP_KEYS + [f"ogT{s}"], writes=[f"psA{b}"])
                        P.op("dve", lambda e, b=b: e.tensor_tensor(out=tb_[b][:], in0=psA[:, b, :], in1=sg[b][:], op=ALU.mult),
                             reads=[f"psA{b}", "sg0"], writes=["tb_0"])
                        P.op("pool", lambda e, dcol=dcol, b=b, tb=tb: e.tensor_tensor(
                            out=mT[:, dcol, tb * 512:(tb + 1) * 512], in0=tb_[b][:], in1=mT[:, dcol, tb * 512:(tb + 1) * 512],
                            op=ALU.add),
                            reads=["tb_0", f"mT{dcol}_{tb}"], writes=[f"mT{dcol}_{tb}"])
                    if tb == 1:
                        w_out_v = w_out.rearrange("(kc p) n -> p kc n", p=128)
                        for half in range(2):
                            P.dma("pool", lambda e, s_, half=half: e.dma_start(
                                out=Wo[:, :, half * 512:(half + 1) * 512],
                                in_=w_out_v[:, :, half * 512:(half + 1) * 512]).then_inc(s_, 16),
                                writes=[f"Wo{half}"], semkey=f"Wo{half}")

        P.barrier()

        with ExitStack() as st4:
            fgainB = sb("fgainB", [128, D], st=st4)
            ld("sp", fgainB[:], fgainB_d, "fgainB")
            xr = [sb(f"xr{i}", [128, D], st=st4) for i in range(3)]
            yy = [sb(f"yy{i}", [128, D], st=st4) for i in range(2)]
            oo = [sb(f"oo{i}", [128, D], st=st4) for i in range(2)]
            junk2 = sb("junk2", [128, D], st=st4)
            ss2 = [sb(f"ss2_{i}", [128, 1], st=st4) for i in range(2)]
            rs2 = [sb(f"rs2_{i}", [128, 1], st=st4) for i in range(2)]
            pso = [psA, psB]
            out_keys = []

            def o_load(n):
                ld("sp", xr[n % 3][:], x[n * 128:(n + 1) * 128, :], f"xr{n % 3}")

            def o_mm(n):
                s = n % 2
                tb = n // 4

                def mmo2(e, n=n, s=s):
                    last = None
                    for half in range(2):
                        for dc in range(8):
                            last = e.matmul(pso[s][:, half, :], lhsT=mT[:, dc, n * 128:(n + 1) * 128],
                                            rhs=Wo[:, dc, half * 512:(half + 1) * 512], start=(dc == 0), stop=(dc == 7))
                    return last
                P.op("pe", mmo2, reads=["Wo0", "Wo1"] + [f"mT{dc}_{tb}" for dc in range(8)],
                     writes=[f"ps{'AB'[s]}0", f"ps{'AB'[s]}1"])

            def o_y(n):
                s = n % 2
                for half in range(2):
                    P.op("dve", lambda e, s=s, half=half, n=n: e.tensor_tensor(
                        out=yy[s][:, half * 512:(half + 1) * 512], in0=pso[s][:, half, :],
                        in1=xr[n % 3][:, half * 512:(half + 1) * 512], op=ALU.add),
                        reads=[f"ps{'AB'[s]}{half}", f"xr{n % 3}"], writes=[f"yy{s}_{half}"])
                P.op("act", lambda e, s=s: e.activation(out=junk2[:], in_=yy[s][:], func=AF.Square, accum_out=ss2[s][:]),
                     reads=[f"yy{s}_0", f"yy{s}_1"], writes=["junk2", f"ss2_{s}"])
                P.op("act", lambda e, s=s: e.activation(out=ss2[s][:], in_=ss2[s][:], func=AF.Sqrt, bias=eps6[:], scale=1.0 / D),
                     reads=[f"ss2_{s}", "eps6"], writes=[f"ss2_{s}"])

            def o_z(n):
                s = n % 2
                P.op("dve", lambda e, s=s: e.reciprocal(out=rs2[s][:], in_=ss2[s][:]),
                     reads=[f"ss2_{s}"], writes=[f"rs2_{s}"])
                P.op("dve", lambda e, s=s: e.scalar_tensor_tensor(
                    out=oo[s][:], in0=yy[s][:], scalar=rs2[s][:], in1=fgainB[:], op0=ALU.mult, op1=ALU.mult),
                    reads=[f"yy{s}_0", f"yy{s}_1", f"rs2_{s}", "fgainB"], writes=[f"oo{s}"])
                P.dma("sp", lambda e, s_, s=s, n=n: e.dma_start(out=out_d[n * 128:(n + 1) * 128, :], in_=oo[s][:]).then_inc(s_, 16),
                      reads=[f"oo{s}"], writes=[f"out{n}"], semkey=f"outd{s}")
                out_keys.append(f"out{n}")

            o_load(0)
            o_load(1)
            o_mm(0)
            for n in range(NT + 1):
                if n + 2 < NT:
                    o_load(n + 2)
                if n + 1 < NT:
                    o_mm(n + 1)
                if n < NT:
                    o_y(n)
                if n >= 1:
                    o_z(n - 1)
            P.op("sp", None, reads=out_keys)
        P.emit(nc, gst)
    return nc


_NC_CACHE = {}


def _consts(j):
    LG = log_gammas()
    pos = (j * S_CORE + np.arange(S_CORE, dtype=np.float64))
    inv_freq = 10000.0 ** (-np.arange(0, 256, 2, dtype=np.float64) / 256.0)
    ang = inv_freq[:, None] * pos[None, :]
    cosT = np.cos(ang).astype(np.float32)
    sinT = np.sin(ang).astype(np.float32)
    pc = np.zeros((128, NPP), np.float64)
    p = np.arange(128, dtype=np.float64)
    for h in range(4):
        for n in range(NT):
            pc[:, PP_ZE + h * 16 + n] = np.exp(LG[h] * (2047.0 - (128.0 * n + p))) / 16.0
        pc[:, PP_EPS + h] = 1e-5 / np.exp(2.0 * LG[h] * (p + 1.0))
        for r in range(3):
            pc[:, PP_COEF + h * 3 + r] = math.exp(LG[h] * 2048.0 * (j - r)) if r < j else 0.0
    mask = np.zeros((128, 4, 128), np.float64)
    jj = np.arange(128)[:, None]
    ii = np.arange(128)[None, :]
    for h in range(4):
        mask[:, h, :] = np.where(jj <= ii, np.exp(-LG[h] * (jj + 1.0)) / 16.0, 0.0)
    return cosT, sinT, pc, mask.reshape(128, 512).astype(np.float32)


def kernel(x, norm_gain, w_in, conv_dw_w, conv_dw_b, conv_ln_g, conv_ln_b,
           w_conv_proj, ret_gn_g, w_ret_proj, w_out, final_gain):
    x = np.asarray(x, np.float32)
    f = lambda a: np.ascontiguousarray(np.asarray(a, np.float32))
    w_in0 = f(w_in[0])
    w_cp0 = f(w_conv_proj[0])
    w_rp0 = f(w_ret_proj[0])
    w_out0 = f(w_out[0])
    gainB = f(np.broadcast_to(np.asarray(norm_gain, np.float32)[0][None, :], (128, D)))
    fgainB = f(np.broadcast_to(np.asarray(final_gain, np.float32)[None, :], (128, D)))
    cw = np.zeros((32, D), np.float32)
    cw[:31] = np.asarray(conv_dw_w, np.float32)[0]
    convw = f(cw.reshape(8, 4, 8, 4, 32).transpose(1, 4, 2, 3, 0).reshape(128, 256))
    i4 = f(np.tile(np.eye(32, dtype=np.float32), (4, 1)))

    def colT(v, nch):
        return np.asarray(v, np.float32).reshape(nch, 128).T

    ident = np.eye(128, dtype=np.float32)
    if "nc" not in _NC_CACHE:
        _NC_CACHE["nc"] = build_program()
    nc = _NC_CACHE["nc"]
    in_maps = []
    for c in range(8):
        b, j = c // 4, c % 4
        cosT, sinT, pc, mask = _consts(j)
        pc[:, PP_CB:PP_CB + 8] = colT(conv_dw_b[0], 8)
        pc[:, PP_LG:PP_LG + 8] = colT(conv_ln_g[0], 8)
        pc[:, PP_LB:PP_LB + 8] = colT(conv_ln_b[0], 8)
        pc[:, PP_GN:PP_GN + 16] = colT(ret_gn_g[0], 16)
        xs = f(x[b, j * S_CORE:(j + 1) * S_CORE, :])
        if j == 0:
            xhalo = np.zeros((HALO, D), np.float32)
        else:
            xhalo = f(x[b, j * S_CORE - HALO:j * S_CORE, :])
        in_maps.append({
            "x": xs, "xh": xhalo, "w_in": w_in0, "w_cp": w_cp0, "w_rp": w_rp0, "w_out": w_out0,
            "gainB": gainB, "fgainB": fgainB, "pp": f(pc.astype(np.float32)), "convw": convw,
            "maskT": mask, "ident": ident, "cosT": cosT, "sinT": sinT, "i4": i4,
        })
    res = run_bass_kernel_spmd(nc, in_maps, core_ids=list(range(8)))
    out = np.empty((2, 4 * S_CORE, D), np.float32)
    for c in range(8):
        b, j = c // 4, c % 4
        out[b, j * S_CORE:(j + 1) * S_CORE, :] = res.results[c]["out"]
    return out
```

```python
import math
import numpy as np
from contextlib import ExitStack
import concourse.bass as bass
import concourse.mybir as mybir
from concourse.bass_utils import run_bass_kernel_spmd

F32 = mybir.dt.float32
BF16 = mybir.dt.bfloat16
AF = mybir.ActivationFunctionType
ALU = mybir.AluOpType

ENG_NAMES = ("pe", "act", "dve", "pool", "sp")

S_CORE = 2048
NT = 16
D = 1024
HALO = 32
NCOLS = 11264
C_AVAL, C_AGLU, C_AGATE, C_Q, C_K, C_V, C_R, C_GA, C_GB = 0, 1024, 2048, 3072, 4096, 5120, 7168, 9216, 10240
PP_CB, PP_LG, PP_LB, PP_GN, PP_ZE, PP_EPS, PP_COEF, NPP = 0, 8, 16, 24, 40, 104, 108, 120
GROUPS = [[0, 1, 2, 3], [4, 5, 6, 7]]


class Op:
    __slots__ = ("eng", "fn", "reads", "writes", "is_dma", "ninc", "deps",
                 "needs_inc", "sem", "val", "idx", "semkey", "barrier")

    def __init__(self, eng, fn, reads, writes, is_dma, ninc, semkey):
        self.eng = eng
        self.fn = fn
        self.reads = tuple(reads)
        self.writes = tuple(writes)
        self.is_dma = is_dma
        self.ninc = ninc
        self.deps = ()
        self.needs_inc = False
        self.sem = None
        self.val = 0
        self.semkey = semkey
        self.barrier = False


class Prog:
    def __init__(self):
        self.ops = []

    def op(self, eng, fn, reads=(), writes=()):
        self.ops.append(Op(eng, fn, reads, writes, False, 1, None))

    def dma(self, eng, fn, reads=(), writes=(), semkey=None, ninc=16):
        assert semkey is not None
        self.ops.append(Op(eng, fn, reads, writes, True, ninc, semkey))

    def barrier(self):
        o = Op(None, None, (), (), False, 0, None)
        o.barrier = True
        self.ops.append(o)

    def finalize(self):
        last_w = {}
        readers = {}
        last_on = {}
        pending = {e: [] for e in ENG_NAMES}
        for i, o in enumerate(self.ops):
            o.idx = i
            if o.barrier:
                lst = list(last_on.values())
                for e in ENG_NAMES:
                    pending[e] = list(lst)
                continue
            deps = set()
            for r in o.reads:
                if r in last_w:
                    deps.add(last_w[r])
            for w in o.writes:
                if w in last_w:
                    deps.add(last_w[w])
                for rd in readers.get(w, ()):
                    deps.add(rd)
            if pending[o.eng]:
                deps.update(pending[o.eng])
                pending[o.eng] = []
            deps.discard(i)
            real = []
            for d in deps:
                dop = self.ops[d]
                if dop.fn is None:
                    continue
                if dop.is_dma or dop.eng != o.eng or o.eng != "pe" or o.is_dma:
                    real.append(d)
                    dop.needs_inc = True
            o.deps = tuple(sorted(real))
            for w in o.writes:
                last_w[w] = i
                readers[w] = []
            for r in o.reads:
                readers.setdefault(r, []).append(i)
            if o.fn is not None:
                last_on[("dma", o.semkey) if o.is_dma else ("eng", o.eng)] = i
        cnt = {}
        for o in self.ops:
            if o.barrier:
                continue
            if o.is_dma:
                key = ("dma", o.semkey)
                cnt[key] = cnt.get(key, 0) + o.ninc
                o.val = cnt[key]
            elif o.needs_inc:
                key = ("eng", o.eng)
                cnt[key] = cnt.get(key, 0) + 1
                o.val = cnt[key]
        self.dma_keys = sorted({o.semkey for o in self.ops if o.is_dma}, key=str)

    def emit(self, nc, stack):
        self.finalize()
        sems = {}
        for e in ENG_NAMES:
            sems[("eng", e)] = stack.enter_context(nc.semaphore("s_" + e))
        for k in self.dma_keys:
            sems[("dma", k)] = stack.enter_context(nc.semaphore("d_" + str(k)))
        for o in self.ops:
            if o.barrier:
                continue
            o.sem = sems[("dma", o.semkey)] if o.is_dma else sems[("eng", o.eng)]
        block = stack.enter_context(nc.Block())
        by_eng = {e: [o for o in self.ops if o.eng == e] for e in ENG_NAMES}
        ops = self.ops

        def run(engine, lst):
            waited = {}
            for o in lst:
                need = {}
                for d in o.deps:
                    dop = ops[d]
                    k = id(dop.sem)
                    if dop.val > need.get(k, (0, None))[0]:
                        need[k] = (dop.val, dop.sem)
                for k, (v, s) in need.items():
                    if waited.get(k, 0) < v:
                        engine.wait_ge(s, v)
                        waited[k] = v
                if o.fn is None:
                    continue
                if o.is_dma:
                    o.fn(engine, o.sem)
                else:
                    ins = o.fn(engine)
                    if o.needs_inc:
                        ins.then_inc(o.sem, 1)

        @block.tensor
        def _(e):
            run(e, by_eng["pe"])

        @block.scalar
        def _(e):
            run(e, by_eng["act"])

        @block.vector
        def _(e):
            run(e, by_eng["dve"])

        @block.gpsimd
        def _(e):
            run(e, by_eng["pool"])

        @block.sync
        def _(e):
            run(e, by_eng["sp"])


def log_gammas():
    return [math.log1p(-2.0 ** (-5.0 - h)) for h in range(4)]


def build_program():
    nc = bass.Bass("TRN2", target_bir_lowering=False)
    P = Prog()
    LG = log_gammas()

    def din(name, shape, dt=F32):
        return nc.dram_tensor(name, shape, dt, kind="ExternalInput").ap()

    x = din("x", [S_CORE, D])
    xh = din("xh", [HALO, D])
    w_in = din("w_in", [D, NCOLS])
    w_cp = din("w_cp", [D, D])
    w_rp = din("w_rp", [2 * D, D])
    w_out = din("w_out", [D, D])
    gainB_d = din("gainB", [128, D])
    fgainB_d = din("fgainB", [128, D])
    pp_d = din("pp", [128, NPP])
    convw_d = din("convw", [128, 256])
    i4_d = din("i4", [128, 32])
    mask_d = din("maskT", [128, 4 * 128])
    ident_d = din("ident", [128, 128])
    cos_d = din("cosT", [128, S_CORE])
    sin_d = din("sinT", [128, S_CORE])
    out_d = nc.dram_tensor("out", [S_CORE, D], F32, kind="ExternalOutput").ap()
    st_in = [nc.dram_tensor(f"st_in{h}", [128, 1024], F32, kind="Internal").ap() for h in range(4)]
    st_out = [nc.dram_tensor(f"st_out{h}", [4 * 128, 1024], F32, kind="Internal").ap() for h in range(4)]
    og_d = nc.dram_tensor("og_spill", [S_CORE, 2 * D], BF16, kind="Internal").ap()

    w_in_v = w_in.rearrange("(kc p) n -> p kc n", p=128)

    with ExitStack() as gst:
        def sb(name, shape, dt=F32, st=None):
            return (st or gst).enter_context(nc.sbuf_tensor("sb_" + name, shape, dt))

        def pst(name, shape, dt=F32):
            return gst.enter_context(nc.psum_tensor("ps_" + name, shape, dt))

        hT = sb("hT", [128, 8, HALO + S_CORE], BF16)
        ident = sb("ident", [128, 128])
        identb = sb("identb", [128, 128], BF16)
        pp = sb("pp", [128, NPP])
        eps6 = sb("eps6", [128, 1])
        eps5 = sb("eps5", [128, 1])
        psA = pst("psA", [128, 2, 512])
        psB = pst("psB", [128, 2, 512])
        psS = pst("psS", [128, 2, 512])
        psX = pst("psX", [128, 512])
        psM = pst("psM", [128, 512])
        sps = psM[:, 128:256]
        otr = psM[:, 256:512].bitcast(BF16).rearrange("p (c d) -> p c d", c=4)
        ktr = [psX[:, 0:128].bitcast(BF16).rearrange("p (c d) -> p c d", c=2),
               psM[:, 0:128].bitcast(BF16).rearrange("p (c d) -> p c d", c=2)]
        KTRB = ["bankX", "bankM"]
        ktr4 = ktr + [psB[:, 0, 0:128].bitcast(BF16).rearrange("p (c d) -> p c d", c=2),
                      psB[:, 1, 0:128].bitcast(BF16).rearrange("p (c d) -> p c d", c=2)]
        KTRB4 = KTRB + ["psB0", "psB1"]

        def ld(q, dst, src, key, reads=()):
            P.dma(q, lambda e, s: e.dma_start(out=dst, in_=src).then_inc(s, 16),
                  reads=reads, writes=[key], semkey=key)

        ld("sp", ident[:], ident_d, "ident")
        ld("sp", pp[:], pp_d, "pp")
        P.op("dve", lambda e: e.tensor_copy(out=identb[:], in_=ident[:]), reads=["ident"], writes=["identb"])
        P.op("pool", lambda e: e.memset(eps6[:], 1e-6), writes=["eps6"])
        P.op("pool", lambda e: e.memset(eps5[:], 1e-5), writes=["eps5"])

        st01 = ExitStack()
        st01.__enter__()
        Wkv = sb("Wkv", [128, 8, 768], BF16, st=st01)
        Wqr = sb("Wqr", [128, 8, 768], BF16, st=st01)

        def load_w(buf, key, cols):
            for (do, sc, n) in cols:
                P.dma("pool", lambda e, s, do=do, sc=sc, n=n: e.dma_start(
                    out=buf[:, :, do:do + n], in_=w_in_v[:, :, sc:sc + n]).then_inc(s, 16),
                    writes=[f"{key}_{do}"], semkey=f"{key}_{do}")
        load_w(Wkv, "Wkv", [(0, C_K, 256), (256, C_V, 512)])
        with ExitStack() as st0:
            gainB = sb("gainB", [128, D], st=st0)
            xs = [sb(f"xs{i}", [128, D], st=st0) for i in range(3)]
            xg = [sb(f"xg{i}", [128, D], st=st0) for i in range(3)]
            junk = sb("junk", [128, D], st=st0)
            ss = [sb(f"ss{i}", [128, 1], st=st0) for i in range(3)]
            rs = [sb(f"rs{i}", [128, 1], st=st0) for i in range(3)]
            dg = [sb(f"dg{i}", [128, 128], st=st0) for i in range(3)]
            ld("sp", gainB[:], gainB_d, "gainB")
            pss = [psA, psB, psS]
            def p0_a(t):
                rows = HALO if t == 0 else 128
                src = xh if t == 0 else x[(t - 1) * 128:t * 128, :]
                s = t % 3
                ld("sp", xs[s][0:rows, :], src, f"xs{s}")
                P.op("act", lambda e, s=s, rows=rows: e.activation(
                    out=junk[0:rows, :], in_=xs[s][0:rows, :], func=AF.Square, accum_out=ss[s][0:rows, :]),
                    reads=[f"xs{s}"], writes=["junk", f"ss{s}"])
                P.op("act", lambda e, s=s, rows=rows: e.activation(
                    out=ss[s][0:rows, :], in_=ss[s][0:rows, :], func=AF.Sqrt, bias=eps6[0:rows, :], scale=1.0 / D),
                    reads=[f"ss{s}", "eps6"], writes=[f"ss{s}"])
                P.op("dve", lambda e, s=s, rows=rows: e.reciprocal(out=rs[s][0:rows, :], in_=ss[s][0:rows, :]),
                     reads=[f"ss{s}"], writes=[f"rs{s}"])
                P.op("dve", lambda e, s=s, rows=rows: e.tensor_scalar(
                    out=dg[s][0:rows, 0:rows], in0=ident[0:rows, 0:rows], scalar1=rs[s][0:rows, :], scalar2=None,
                    op0=ALU.mult), reads=[f"rs{s}", "ident"], writes=[f"dg{s}"])
                P.op("pool", lambda e, s=s, rows=rows: e.tensor_tensor(
                    out=xg[s][0:rows, :], in0=xs[s][0:rows, :], in1=gainB[0:rows, :], op=ALU.mult),
                    reads=[f"xs{s}", "gainB"], writes=[f"xg{s}"])

            def p0_b(t):
                rows = HALO if t == 0 else 128
                col0 = 0 if t == 0 else HALO + (t - 1) * 128
                s = t % 3

                def tr(e, s=s, rows=rows):
                    last = None
                    for dc in range(8):
                        last = e.matmul(pss[s][:, dc // 4, (dc % 4) * 128:(dc % 4) * 128 + rows],
                                        lhsT=xg[s][0:rows, dc * 128:(dc + 1) * 128],
                                        rhs=dg[s][0:rows, 0:rows], start=True, stop=True)
                    return last
                P.op("pe", tr, reads=[f"xg{s}", f"dg{s}"], writes=[f"ps{'ABS'[s]}0", f"ps{'ABS'[s]}1"])
                for half in range(2):
                    eng = "act" if half == 0 else "dve"

                    def ev(e, s=s, rows=rows, half=half, col0=col0, eng=eng):
                        src_ap = pss[s][:, half, :].rearrange("p (c t) -> p c t", c=4)[:, :, 0:rows]
                        dst_ap = hT[:, half * 4:(half + 1) * 4, col0:col0 + rows]
                        if eng == "act":
                            return e.activation(out=dst_ap, in_=src_ap, func=AF.Copy)
                        return e.tensor_copy(out=dst_ap, in_=src_ap)
                    P.op(eng, ev, reads=[f"ps{'ABS'[s]}{half}"], writes=[f"hT_{t}_{half}"])

            p0_a(0)
            p0_a(1)
            for t in range(NT + 1):
                if t + 2 < NT + 1:
                    p0_a(t + 2)
                p0_b(t)
        HT_ALL = [f"hT_{t}_{half}" for t in range(NT + 1) for half in range(2)]

        def ht_keys(tiles):
            return [f"hT_{t + 1}_{half}" for t in tiles for half in range(2)]
        P.barrier()

        with ExitStack() as st1:
            load_w(Wqr, "Wqr", [(0, C_Q, 256), (256, C_R, 512)])
            maskT = sb("maskT", [128, 4, 128], st=st1)
            ld("sp", maskT[:], mask_d.rearrange("p (h i) -> p h i", h=4), "maskT")
            dgc = sb("dgc", [128, 12, 128], st=st1)
            for hr in range(12):
                P.op("dve", lambda e, hr=hr: e.tensor_scalar(
                    out=dgc[:, hr, :], in0=ident[:], scalar1=pp[:, PP_COEF + hr:PP_COEF + hr + 1], scalar2=None,
                    op0=ALU.mult), reads=["ident", "pp"], writes=[f"dgc{hr}"])
            kT2 = [sb(f"kT{i}", [128, 2, S_CORE], BF16, st=st1) for i in range(2)]
            qT = sb("qT", [128, 2, S_CORE], BF16, st=st1)
            kend = sb("kend", [128, NT, 256], BF16, st=st1)
            vv2 = [sb(f"vv{i}", [128, NT, 512], BF16, st=st1) for i in range(2)]
            srall = sb("srall", [128, NT, 512], BF16, st=st1)
            cosS = sb("cosS", [128, S_CORE], st=st1)
            sinS = sb("sinS", [128, S_CORE], st=st1)
            ld("sp", cosS[:], cos_d, "cosS")
            ld("sp", sinS[:], sin_d, "sinS")
            rt = [[sb(f"rt{i}_{k}", [128, 512], st=st1) for k in range(4)] for i in range(1)]
            Rloc = sb("Rloc", [128, 2, 512], st=st1)
            Rg = sb("Rg", [128, 3, 1024], st=st1)
            Rb = sb("Rb", [128, 2, 512], BF16, st=st1)
            sT = [sb(f"sT{i}", [128, 128], BF16, st=st1) for i in range(2)]
            on = [sb(f"on{i}", [128, 512], st=st1) for i in range(2)]
            og = [sb(f"og{i}", [128, 512], BF16, st=st1) for i in range(4)]
            st6 = [sb(f"st6_{i}", [128, 6], st=st1) for i in range(2)]
            mv = [sb(f"mv{i}", [128, 2], st=st1) for i in range(2)]
            rsd = [sb(f"rsd{i}", [128, 1], st=st1) for i in range(2)]
            sdv = [sb(f"sdv{i}", [128, 1], st=st1) for i in range(2)]
            nmr = [sb(f"nmr{i}", [128, 1], st=st1) for i in range(2)]
            tabcnt = [0]

            def proj_rot(W, wkeys, woff, dst, dstkey, tb, ceng="pool", pp_=None, ppk="psA"):
                pp_ = psA if pp_ is None else pp_
                ti = tabcnt[0] % 2
                tabcnt[0] += 1
                cs_ap = cosS[:, tb * 512:(tb + 1) * 512]
                sn_ap = sinS[:, tb * 512:(tb + 1) * 512]
                c0 = HALO + tb * 512

                def mmf(e):
                    last = None
                    for c in range(2):
                        for kc in range(8):
                            last = e.matmul(pp_[:, c, :], lhsT=W[:, kc, woff + c * 128:woff + (c + 1) * 128],
                                            rhs=hT[:, kc, c0:c0 + 512], start=(kc == 0), stop=(kc == 7))
                    return last
                P.op("pe", mmf, reads=wkeys + ht_keys(range(tb * 4, tb * 4 + 4)), writes=[ppk + "0", ppk + "1"])
                r = rt[0]
                ri = 0
                P.op("dve", lambda e: e.tensor_tensor(out=r[0][:], in0=pp_[:, 0, :], in1=cs_ap, op=ALU.mult),
                     reads=[ppk + "0", "cosS"], writes=[f"rt{ri}0"])
                P.op("dve", lambda e: e.tensor_tensor(out=r[1][:], in0=pp_[:, 1, :], in1=sn_ap, op=ALU.mult),
                     reads=[ppk + "1", "sinS"], writes=[f"rt{ri}1"])
                P.op("dve", lambda e: e.tensor_tensor(out=r[2][:], in0=pp_[:, 1, :], in1=cs_ap, op=ALU.mult),
                     reads=[ppk + "1", "cosS"], writes=[f"rt{ri}2"])
                P.op("dve", lambda e: e.tensor_tensor(out=r[3][:], in0=pp_[:, 0, :], in1=sn_ap, op=ALU.mult),
                     reads=[ppk + "0", "sinS"], writes=[f"rt{ri}3"])
                P.op(ceng, lambda e: e.tensor_tensor(out=dst[:, 0, tb * 512:(tb + 1) * 512], in0=r[0][:], in1=r[1][:],
                                                     op=ALU.subtract),
                     reads=[f"rt{ri}0", f"rt{ri}1"], writes=[f"{dstkey}0_{tb}"])
                P.op(ceng, lambda e: e.tensor_tensor(out=dst[:, 1, tb * 512:(tb + 1) * 512], in0=r[2][:], in1=r[3][:],
                                                     op=ALU.add),
                     reads=[f"rt{ri}2", f"rt{ri}3"], writes=[f"{dstkey}1_{tb}"])

            WKV_KEYS_K = ["Wkv_0"]
            WKV_KEYS_V = ["Wkv_256"]
            WQR_KEYS_Q = ["Wqr_0"]
            WQR_KEYS_R = ["Wqr_256"]
            for h in range(4):
                lg = LG[h]
                hp = h % 2
                kT = kT2[hp]
                vv = vv2[hp]
                KT = f"kT{hp}c"
                VV = f"vv{hp}_"

                def p1_vproj(n, par=hp, use_x=False):
                    vb = n % 2
                    dst_ps = psX[:, :] if use_x else psB[:, vb, :]
                    pkey = "bankX" if use_x else f"psB{vb}"
                    vdst = vv2[par]

                    def mmv(e, n=n, dst_ps=dst_ps):
                        last = None
                        for kc in range(8):
                            last = e.matmul(dst_ps, lhsT=hT[:, kc, HALO + n * 128:HALO + (n + 1) * 128],
                                            rhs=Wkv[:, kc, 256:768], start=(kc == 0), stop=(kc == 7))
                        return last
                    P.op("pe", mmv, reads=WKV_KEYS_V + ht_keys([n]), writes=[pkey])
                    P.op("act", lambda e, n=n, dst_ps=dst_ps, vdst=vdst: e.activation(out=vdst[:, n, :], in_=dst_ps, func=AF.Copy),
                         reads=[], writes=[f"vv{par}_{n}", pkey])

                def p1_trk(n, h=h):
                    tb = n // 4
                    slots, skeys = (ktr, KTRB) if h == 0 else (ktr4, KTRB4)
                    ks = n % len(slots)
                    kt_ap, kt_key = slots[ks], skeys[ks]

                    def trk(e, n=n, kT=kT, kt_ap=kt_ap):
                        last = None
                        for c in range(2):
                            last = e.transpose(kt_ap[:, c, :], kT[:, c, n * 128:(n + 1) * 128], identb[:])
                        return last
                    P.op("pe", trk, reads=[f"{KT}0_{tb}", f"{KT}1_{tb}", "identb"], writes=[kt_key])
                    P.op("act", lambda e, n=n, h=h, kt_ap=kt_ap: e.activation(
                        out=kend[:, n, :], in_=kt_ap[:].rearrange("p c d -> p (c d)"), func=AF.Copy,
                        scale=pp[:, PP_ZE + h * 16 + n:PP_ZE + h * 16 + n + 1]),
                        reads=["pp"], writes=[f"kend{n}", kt_key])

                def p1_st(n):
                    def mms(e, n=n, vv=vv):
                        last = None
                        for c in range(2):
                            last = e.matmul(psS[:, c, :], lhsT=kend[:, n, c * 128:(c + 1) * 128], rhs=vv[:, n, :],
                                            start=(n == 0), stop=True, skip_group_check=True)
                        return last
                    P.op("pe", mms, reads=[f"kend{n}", f"{VV}{n}"], writes=["psS0", "psS1"])

                if h == 0:
                    for tb in range(4):
                        proj_rot(Wkv, WKV_KEYS_K, 0, kT, KT, tb)
                        for n in range(tb * 4, tb * 4 + 4):
                            p1_vproj(n)
                        if tb >= 1:
                            for n in range((tb - 1) * 4, tb * 4):
                                p1_trk(n)
                            for n in range((tb - 1) * 4, tb * 4):
                                p1_st(n)
                    for n in range(12, 16):
                        p1_trk(n)
                    for n in range(12, 16):
                        p1_st(n)
                else:
                    for g in range(4):
                        for n in range(g * 4, g * 4 + 4):
                            p1_trk(n)
                        for n in range(g * 4, g * 4 + 4):
                            p1_st(n)
                for c in range(2):
                    P.op("dve", lambda e, c=c: e.tensor_copy(out=Rloc[:, c, :], in_=psS[:, c, :]),
                         reads=[f"psS{c}"], writes=[f"Rloc{c}"])
                P.dma("sp", lambda e, s, h=h: e.dma_start(
                    out=st_in[h], in_=Rloc[:].rearrange("p c e -> p (c e)")).then_inc(s, 16),
                    reads=["Rloc0", "Rloc1"], writes=[f"st_in{h}"], semkey=f"st_in{h}")
                P.dma("pool", lambda e, s, h=h: e.collective_compute(
                    "AllGather", ALU.bypass, replica_groups=GROUPS, ins=[st_in[h]], outs=[st_out[h]]).then_inc(s, 1),
                    reads=[f"st_in{h}"], writes=[f"st_out{h}"], semkey=f"cc{h}", ninc=1)
                if h < 3:
                    load_w(Wkv, "Wkv", [(0, C_K + 256 * (h + 1), 256), (256, C_V + 512 * (h + 1), 512)])

                def p2_rproj(n):
                    rb = n % 2

                    def mmr(e, n=n, rb=rb):
                        last = None
                        for kc in range(8):
                            last = e.matmul(psB[:, rb, :], lhsT=hT[:, kc, HALO + n * 128:HALO + (n + 1) * 128],
                                            rhs=Wqr[:, kc, 256:768], start=(kc == 0), stop=(kc == 7))
                        return last
                    P.op("pe", mmr, reads=WQR_KEYS_R + ht_keys([n]), writes=[f"psB{rb}"])
                    P.op("act", lambda e, rb=rb, n=n: e.activation(out=srall[:, n, :], in_=psB[:, rb, :], func=AF.Silu),
                         reads=[f"psB{rb}"], writes=[f"sr{n}"])

                for tb in range(4):
                    proj_rot(Wqr, WQR_KEYS_Q, 0, qT, "qT", tb, ceng="dve")
                    for n in range(tb * 4, tb * 4 + 4):
                        p2_rproj(n)

                P.dma("sp", lambda e, s, h=h: e.dma_start(
                    out=Rg[:], in_=st_out[h][0:384, :].rearrange("(r p) f -> p r f", p=128)).then_inc(s, 16),
                    reads=[f"st_out{h}"], writes=["Rg"], semkey="Rg")

                def inj(e, h=h):
                    last = None
                    for c in range(2):
                        for r in range(3):
                            last = e.matmul(psS[:, c, :], lhsT=dgc[:, h * 3 + r, :], rhs=Rg[:, r, c * 512:(c + 1) * 512],
                                            start=(r == 0), stop=(r == 2), skip_group_check=True)
                    return last
                P.op("pe", inj, reads=["Rg"] + [f"dgc{h * 3 + r}" for r in range(3)] + ["Rloc0", "Rloc1"],
                     writes=["psS0", "psS1"])

                sps_h, spk = (psX[:, 0:128], "bankX") if h == 3 else (sps, "bankM")

                def p2_scores(n, h=h, sps_h=sps_h, spk=spk):
                    tb = n // 4
                    b = n % 2

                    def mmsc(e, n=n, b=b, kT=kT, sps_h=sps_h):
                        last = None
                        for c in range(2):
                            last = e.matmul(sps_h, lhsT=kT[:, c, n * 128:(n + 1) * 128], rhs=qT[:, c, n * 128:(n + 1) * 128],
                                            start=(c == 0), stop=(c == 1))
                        return last
                    P.op("pe", mmsc, reads=[f"{KT}0_{tb}", f"{KT}1_{tb}", f"qT0_{tb}", f"qT1_{tb}"], writes=[spk])
                    P.op("dve", lambda e, h=h, b=b, sps_h=sps_h: e.tensor_tensor(out=sT[b][:], in0=sps_h, in1=maskT[:, h, :],
                                                                              op=ALU.mult),
                         reads=["maskT"], writes=[f"sT{b}", spk])

                def p2_rb(n, lg=lg):
                    scl = math.exp(lg * 128.0 * (n - 16))
                    P.op("act", lambda e, scl=scl: e.activation(out=Rb[:, 0, :], in_=psS[:, 0, :], func=AF.Copy, scale=float(scl)),
                         reads=["psS0"], writes=["Rb0"])
                    P.op("dve", lambda e, scl=scl: e.tensor_scalar(out=Rb[:, 1, :], in0=psS[:, 1, :], scalar1=float(scl), scalar2=None,
                                                                   op0=ALU.mult),
                         reads=["psS1"], writes=["Rb1"])

                OB = [(psA[:, 0, :], "psA0"), (psA[:, 1, :], "psA1")]
                if h == 3:
                    OB = OB + [(psB[:, 0, :], "psB0"), (psB[:, 1, :], "psB1")]

                def p2_o(n, OB=OB):
                    tb = n // 4
                    b = n % 2
                    oap, okey = OB[n % len(OB)]

                    P.op("pe", lambda e, n=n, b=b, vv=vv, oap=oap: e.matmul(oap, lhsT=sT[b][:], rhs=vv[:, n, :],
                                                                             start=True, stop=False),
                         reads=[f"sT{b}", f"{VV}{n}"], writes=[okey])

                    def mmo(e, n=n, b=b, oap=oap):
                        last = None
                        for c in range(2):
                            last = e.matmul(oap, lhsT=qT[:, c, n * 128:(n + 1) * 128], rhs=Rb[:, c, :],
                                            start=False, stop=(c == 1))
                        return last
                    P.op("pe", mmo, reads=[f"qT0_{tb}", f"qT1_{tb}", "Rb0", "Rb1"], writes=[okey])
                    if n < NT - 1:
                        def mms2(e, n=n, vv=vv):
                            last = None
                            for c in range(2):
                                last = e.matmul(psS[:, c, :], lhsT=kend[:, n, c * 128:(c + 1) * 128], rhs=vv[:, n, :],
                                                start=False, stop=True, skip_group_check=True)
                            return last
                        P.op("pe", mms2, reads=[f"kend{n}", f"{VV}{n}", "Rb0", "Rb1"], writes=["psS0", "psS1"])

                def p2_stats_a(n, OB=OB):
                    b = n % 2
                    oap, okey = OB[n % len(OB)]
                    P.op("dve", lambda e, b=b, oap=oap: e.bn_stats(out=st6[b][:], in_=oap), reads=[okey], writes=[f"st6_{b}"])

                def p2_stats_b(n):
                    b = n % 2
                    P.op("dve", lambda e, b=b: e.bn_aggr(out=mv[b][:], in_=st6[b][:]), reads=[f"st6_{b}"], writes=[f"mv{b}"])

                def p2_norm_act1(n, h=h):
                    b = n % 2
                    P.op("act", lambda e, h=h, b=b: e.activation(
                        out=sdv[b][:], in_=mv[b][:, 1:2], func=AF.Sqrt, bias=pp[:, PP_EPS + h:PP_EPS + h + 1], scale=1.0),
                        reads=[f"mv{b}", "pp"], writes=[f"sdv{b}"])

                def p2_norm_dve(n):
                    b = n % 2
                    P.op("dve", lambda e, b=b: e.reciprocal(out=rsd[b][:], in_=sdv[b][:]), reads=[f"sdv{b}"], writes=[f"rsd{b}"])
                    P.op("dve", lambda e, b=b: e.scalar_tensor_tensor(
                        out=nmr[b][:], in0=mv[b][:, 0:1], scalar=-1.0, in1=rsd[b][:], op0=ALU.mult, op1=ALU.mult),
                        reads=[f"mv{b}", f"rsd{b}"], writes=[f"nmr{b}"])

                def p2_norm_act2(n, OB=OB):
                    b = n % 2
                    oap, okey = OB[n % len(OB)]
                    P.op("act", lambda e, b=b, oap=oap: e.activation(out=on[b][:], in_=oap, func=AF.Identity,
                                                                       bias=nmr[b][:], scale=rsd[b][:]),
                         reads=[okey, f"rsd{b}", f"nmr{b}"], writes=[f"on{b}"])
                    P.op("pool", lambda e, b=b, n=n: e.tensor_tensor(out=og[n % 4][:], in0=on[b][:], in1=srall[:, n, :], op=ALU.mult),
                         reads=[f"on{b}", f"sr{n}"], writes=[f"og{n % 4}"])

                def p2_tr(n, h=h):
                    ob = n % 4
                    P.dma("sp", lambda e, s, ob=ob, h=h, n=n: e.dma_start(
                        out=og_d[n * 128:(n + 1) * 128, h * 512:(h + 1) * 512], in_=og[ob][:]).then_inc(s, 16),
                        reads=[f"og{ob}"], writes=[f"ogd_{n}_{h}"], semkey=f"ogd{ob}")

                p2_scores(0)
                p2_rb(0)
                for n in range(NT):
                    if n + 1 < NT:
                        p2_scores(n + 1)
                    if n >= 1:
                        p2_norm_act1(n - 1)
                        p2_norm_dve(n - 1)
                    p2_o(n)
                    p2_stats_a(n)
                    if n + 1 < NT:
                        p2_rb(n + 1)
                    p2_stats_b(n)
                    if n >= 1:
                        p2_norm_act2(n - 1)
                    if n >= 2:
                        p2_tr(n - 2)
                    if h < 3:
                        if n % 4 == 0:
                            proj_rot(Wkv, WKV_KEYS_K, 0, kT2[1 - hp], f"kT{1 - hp}c", n // 4, pp_=psB, ppk="psB")
                        p1_vproj(n, par=1 - hp, use_x=True)
                p2_norm_act1(NT - 1)
                p2_norm_dve(NT - 1)
                p2_norm_act2(NT - 1)
                p2_tr(NT - 2)
                p2_tr(NT - 1)
                if h < 3:
                    load_w(Wqr, "Wqr", [(0, C_Q + 256 * (h + 1), 256), (256, C_R + 512 * (h + 1), 512)])
        P.barrier()
        st01.close()

        u2 = sb("u2", [128, 8, S_CORE], BF16)
        Wc = [sb(f"Wc{i}", [128, 8, 128], BF16) for i in range(2)]
        Wa = [sb(f"Wa{i}", [128, 8, 128], BF16) for i in range(2)]
        w_cp_v = w_cp.rearrange("(kc p) n -> p kc n", p=128)

        def load_wca(dcol):
            s = dcol % 2
            P.dma("pool", lambda e, s_, s=s, dcol=dcol: e.dma_start(
                out=Wc[s][:], in_=w_cp_v[:, :, dcol * 128:(dcol + 1) * 128]).then_inc(s_, 16),
                writes=[f"Wc{s}"], semkey=f"Wc{s}")
            P.dma("pool", lambda e, s_, s=s, dcol=dcol: e.dma_start(
                out=Wa[s][:], in_=w_in_v[:, :, C_GA + dcol * 128:C_GA + (dcol + 1) * 128]).then_inc(s_, 16),
                writes=[f"Wa{s}"], semkey=f"Wa{s}")
        with ExitStack() as st3:
            Wg = [sb(f"Wg{i}", [128, 8, 128], BF16, st=st3) for i in range(2)]

            def load_wg(cc):
                s = cc % 2
                P.dma("pool", lambda e, s_, s=s, cc=cc: e.dma_start(
                    out=Wg[s][:], in_=w_in_v[:, :, C_AGATE + cc * 128:C_AGATE + (cc + 1) * 128]).then_inc(s_, 16),
                    writes=[f"Wg{s}"], semkey=f"Wg{s}")
            cv = sb("cv", [128, 8, S_CORE], st=st3)
            convw = sb("convw", [128, 256], st=st3)
            ld("sp", convw[:], convw_d, "convw")
            i4 = sb("i4", [128, 32], st=st3)
            ld("sp", i4[:], i4_d, "i4")
            onesf = sb("onesf", [128, 128], st=st3)
            P.op("pool", lambda e: e.memset(onesf[:], 1.0 / D), writes=["onesf"])
            with ExitStack() as st3a:
                Wag = [sb(f"Wag{i}", [128, 8, 256], BF16, st=st3a) for i in range(2)]
                uT = [sb(f"uT{i}", [128, HALO + S_CORE], BF16, st=st3a) for i in range(2)]
                Ust = [sb(f"Ust{i}", [128, 4, HALO + S_CORE], BF16, st=st3a) for i in range(2)]
                Wp = [sb(f"Wp{i}", [128, 32, 32], BF16, st=st3a) for i in range(2)]
                for i in range(2):
                    P.op("pool", lambda e, i=i: e.memset(Ust[i][:, :, HALO + S_CORE - 4:HALO + S_CORE], 0.0), writes=[f"U{i}"])
                sgl = [sb(f"sgl{i}", [128, 512], st=st3a) for i in range(2)]
                def load_wag(cc):
                    s = cc % 2
                    for part, colbase in ((1, C_AGLU), (0, C_AVAL)):
                        P.dma("pool", lambda e, s_, s=s, part=part, colbase=colbase, cc=cc: e.dma_start(
                            out=Wag[s][:, :, part * 128:(part + 1) * 128],
                            in_=w_in_v[:, :, colbase + cc * 128:colbase + (cc + 1) * 128]).then_inc(s_, 16),
                            writes=[f"Wag{s}_{part}"], semkey=f"Wag{s}_{part}")
                def conv_mm(cc):
                    s = cc % 2
                    for tb in range(4):
                        b = tb % 2

                        def mmc(e, s=s, tb=tb, b=b):
                            last = None
                            for m in range(8):
                                o0 = 2 + 4 * m + tb * 512
                                for g in range(4):
                                    last = e.matmul(psS[32 * g:32 * (g + 1), b, :], lhsT=Wp[s][:, g * 8 + m, :],
                                                    rhs=Ust[s][:, g, o0:o0 + 512], start=(m == 0), stop=(m == 7),
                                                    tile_position=(0, 32 * g))
                            return last
                        P.op("pe", mmc, reads=[f"Wp{s}_{gm}" for gm in range(32)] + [f"U{s}"], writes=[f"psS{b}"])
                        P.op("act", lambda e, b=b, cc=cc, tb=tb: e.activation(
                            out=cv[:, cc, tb * 512:(tb + 1) * 512], in_=psS[:, b, :], func=AF.Identity,
                            bias=pp[:, PP_CB + cc:PP_CB + cc + 1], scale=1.0),
                            reads=[f"psS{b}", "pp"], writes=[f"cv{cc}_{tb}"])

                load_wag(0)
                load_wag(1)
                for cc in range(8):
                    s = cc % 2
                    for blk in range(5):
                        c0 = 0 if blk == 0 else HALO + (blk - 1) * 512
                        n = HALO if blk == 0 else 512
                        hk = ["hT_0_0", "hT_0_1"] if blk == 0 else ht_keys(range((blk - 1) * 4, (blk - 1) * 4 + 4))

                        for part in (1, 0):
                            def mma(e, s=s, c0=c0, n=n, part=part):
                                last = None
                                for kc in range(8):
                                    last = e.matmul(psA[:, part, 0:n], lhsT=Wag[s][:, kc, part * 128:(part + 1) * 128],
                                                    rhs=hT[:, kc, c0:c0 + n], start=(kc == 0), stop=(kc == 7))
                                return last
                            P.op("pe", mma, reads=[f"Wag{s}_{part}"] + hk, writes=[f"psA{part}"])
                        b = blk % 2
                        P.op("act", lambda e, b=b, n=n: e.activation(out=sgl[b][:, 0:n], in_=psA[:, 1, 0:n], func=AF.Sigmoid),
                             reads=["psA1"], writes=[f"sgl{b}"])
                        P.op("dve", lambda e, b=b, n=n, s=s, c0=c0: e.tensor_tensor(
                            out=uT[s][:, c0:c0 + n], in0=psA[:, 0, 0:n], in1=sgl[b][:, 0:n], op=ALU.mult),
                            reads=["psA0", f"sgl{b}"], writes=[f"uT{s}_{blk}"])
                    for gm in range(32):
                        P.op("dve", lambda e, s=s, gm=gm, cc=cc: e.tensor_scalar(
                            out=Wp[s][:, gm, :], in0=i4[:], scalar1=convw[:, cc * 32 + gm:cc * 32 + gm + 1], scalar2=None,
                            op0=ALU.mult), reads=["i4", "convw"], writes=[f"Wp{s}_{gm}"])
                    def mku(e, s_, s=s):
                        for g in range(4):
                            for j in range(4):
                                e.dma_start(out=Ust[s][j * 32:(j + 1) * 32, g, 0:HALO + S_CORE - j],
                                            in_=uT[s][g * 32:(g + 1) * 32, j:HALO + S_CORE]).then_inc(s_, 16)
                    P.dma("sp", mku, reads=[f"uT{s}_{i}" for i in range(5)], writes=[f"U{s}"], semkey=f"U{s}", ninc=256)
                    if cc >= 1:
                        conv_mm(cc - 1)
                    if cc + 2 < 8:
                        load_wag(cc + 2)
                conv_mm(7)
            P.barrier()
            rstd_t = sb("rstd_t", [128, S_CORE], st=st3)
            nmr_t = sb("nmr_t", [128, S_CORE], st=st3)
            sga = [sb(f"sga{i}", [128, 512], st=st3) for i in range(2)]
            t1 = [sb(f"t1_{i}", [128, 512], st=st3) for i in range(2)]
            t2 = [sb(f"t2_{i}", [128, 512], st=st3) for i in range(2)]
            zz = [sb(f"zz{i}", [128, 512], st=st3) for i in range(2)]
            with ExitStack() as st3b:
                load_wg(0)
                load_wg(1)
                sq = [sb(f"sq{i}", [128, 512], BF16, st=st3b) for i in range(2)]
                cvb = [sb(f"cvb{i}", [128, 512], BF16, st=st3b) for i in range(2)]
                onesb = sb("onesb", [128, 128], BF16, st=st3b)
                P.op("dve", lambda e: e.tensor_copy(out=onesb[:], in_=onesf[:]), reads=["onesf"], writes=["onesb"])
                mean_t2 = [sb(f"mean_t{i}", [128, 512], st=st3b) for i in range(2)]
                msq_t = sb("msq_t", [128, 512], st=st3b)
                LNB = [(psA, "psA"), (psB, "psB")]

                def ln_front(tb):
                    pt, pk = LNB[tb % 2]
                    for cc in range(8):
                        b = cc % 2
                        P.op("dve", lambda e, b=b, cc=cc, tb=tb: e.tensor_copy(
                            out=cvb[b][:], in_=cv[:, cc, tb * 512:(tb + 1) * 512]),
                            reads=[f"cv{cc}_{tb}"], writes=[f"cvb{b}"])
                        P.op("pe", lambda e, b=b, cc=cc, pt=pt: e.matmul(pt[:, 0, :], lhsT=onesb[:], rhs=cvb[b][:],
                                                                         start=(cc == 0), stop=(cc == 7)),
                             reads=["onesb", f"cvb{b}"], writes=[pk + "0"])
                        P.op("act", lambda e, b=b, cc=cc, tb=tb: e.activation(
                            out=sq[b][:], in_=cv[:, cc, tb * 512:(tb + 1) * 512], func=AF.Square),
                            reads=[f"cv{cc}_{tb}"], writes=[f"sq{b}"])
                        P.op("pe", lambda e, b=b, cc=cc, pt=pt: e.matmul(pt[:, 1, :], lhsT=onesb[:], rhs=sq[b][:],
                                                                         start=(cc == 0), stop=(cc == 7)),
                             reads=["onesb", f"sq{b}"], writes=[pk + "1"])

                def ln_back(tb):
                    pt, pk = LNB[tb % 2]
                    sl = slice(tb * 512, (tb + 1) * 512)
                    mean_t = mean_t2[tb % 2]
                    mk = f"mean_t{tb % 2}"
                    P.op("dve", lambda e, pt=pt, mean_t=mean_t: e.tensor_copy(out=mean_t[:], in_=pt[:, 0, :]), reads=[pk + "0"], writes=[mk])
                    P.op("dve", lambda e, mean_t=mean_t: e.tensor_tensor(out=msq_t[:], in0=mean_t[:], in1=mean_t[:], op=ALU.mult),
                         reads=[mk], writes=["msq_t"])
                    P.op("dve", lambda e, pt=pt: e.tensor_tensor(out=msq_t[:], in0=pt[:, 1, :], in1=msq_t[:], op=ALU.subtract),
                         reads=[pk + "1", "msq_t"], writes=["msq_t"])
                    P.op("act", lambda e: e.activation(out=msq_t[:], in_=msq_t[:], func=AF.Ln, bias=eps5[:], scale=1.0),
                         reads=["msq_t", "eps5"], writes=["msq_t"])
                    P.op("act", lambda e, sl=sl: e.activation(out=rstd_t[:, sl], in_=msq_t[:], func=AF.Exp, scale=-0.5),
                         reads=["msq_t"], writes=[f"rstd_t{tb}"])
                    P.op("pool", lambda e, sl=sl, mean_t=mean_t: e.tensor_tensor(out=nmr_t[:, sl], in0=mean_t[:], in1=rstd_t[:, sl],
                                                                                op=ALU.mult),
                         reads=[mk, f"rstd_t{tb}"], writes=[f"nmr_t{tb}"])

                for tb in range(5):
                    if tb < 4:
                        ln_front(tb)
                    if tb >= 1:
                        ln_back(tb - 1)
            with ExitStack() as st3c:
                load_wca(0)
                load_wca(1)
                items = [(cc, tb) for cc in range(8) for tb in range(4)]

                def n_front(i):
                    cc, tb = items[i]
                    s = cc % 2
                    b = i % 2
                    c0 = HALO + tb * 512
                    sl = slice(tb * 512, (tb + 1) * 512)

                    def mmg2(e, s=s, b=b, c0=c0):
                        last = None
                        for kc in range(8):
                            last = e.matmul(psB[:, b, :], lhsT=Wg[s][:, kc, :], rhs=hT[:, kc, c0:c0 + 512],
                                            start=(kc == 0), stop=(kc == 7))
                        return last
                    P.op("pe", mmg2, reads=[f"Wg{s}"] + ht_keys(range(tb * 4, tb * 4 + 4)), writes=[f"psB{b}"])
                    P.op("dve", lambda e, b=b, cc=cc, sl=sl: e.tensor_tensor(
                        out=t1[b][:], in0=cv[:, cc, sl], in1=rstd_t[:, sl], op=ALU.mult),
                        reads=[f"cv{cc}_{tb}", f"rstd_t{tb}"], writes=[f"t1_{b}"])
                    P.op("dve", lambda e, b=b, sl=sl: e.tensor_tensor(
                        out=t2[b][:], in0=t1[b][:], in1=nmr_t[:, sl], op=ALU.subtract),
                        reads=[f"t1_{b}", f"nmr_t{tb}"], writes=[f"t2_{b}"])

                def n_back(i):
                    cc, tb = items[i]
                    b = i % 2
                    sl = slice(tb * 512, (tb + 1) * 512)
                    P.op("act", lambda e, b=b: e.activation(out=sga[b][:], in_=psB[:, b, :], func=AF.Silu),
                         reads=[f"psB{b}"], writes=[f"sga{b}"])
                    P.op("act", lambda e, b=b, cc=cc: e.activation(
                        out=zz[b][:], in_=t2[b][:], func=AF.Silu,
                        bias=pp[:, PP_LB + cc:PP_LB + cc + 1], scale=pp[:, PP_LG + cc:PP_LG + cc + 1]),
                        reads=[f"t2_{b}", "pp"], writes=[f"zz{b}"])
                    P.op("dve", lambda e, b=b, cc=cc, sl=sl: e.tensor_tensor(
                        out=u2[:, cc, sl], in0=zz[b][:], in1=sga[b][:], op=ALU.mult),
                        reads=[f"zz{b}", f"sga{b}"], writes=[f"u2_{cc}_{tb}"])

                for i in range(len(items) + 1):
                    if i < len(items):
                        n_front(i)
                    if i >= 1:
                        n_back(i - 1)
                    if i % 4 == 3 and i // 4 + 2 < 8:
                        load_wg(i // 4 + 2)
        P.barrier()
        mT = sb("mT", [128, 8, S_CORE], BF16)
        Wo = sb("Wo", [128, 8, D], BF16)
        with ExitStack() as st5:
            Wrp = sb("Wrp", [128, 16, D], BF16, st=st5)
            Wgb = sb("Wgb", [128, 8, D], BF16, st=st5)
            def load_wgb(half):
                P.dma("pool", lambda e, s, half=half: e.dma_start(
                    out=Wgb[:, :, half * 512:(half + 1) * 512],
                    in_=w_in_v[:, :, C_GB + half * 512:C_GB + (half + 1) * 512]).then_inc(s, 16),
                    writes=[f"Wgb{half}"], semkey=f"Wgb{half}")

            def load_wrp(ec):
                s = ec % 2
                ld("sp", wst[s][:], w_rp[ec * 128:(ec + 1) * 128, :], f"wst{s}")
                P.op("act", lambda e, s=s, ec=ec: e.activation(out=Wrp[:, ec, :], in_=wst[s][:], func=AF.Copy,
                                                               scale=pp[:, PP_GN + ec:PP_GN + ec + 1]),
                     reads=[f"wst{s}", "pp"], writes=[f"Wrp{ec}"])
            WRP_KEYS = [f"Wrp{ec}" for ec in range(16)]
            ogT0 = sb("ogT0", [128, 16, 512], BF16, st=st5)

            def rd_ogT_op(dst, key, tb):
                def rd_ogT(e, s_, dst=dst, tb=tb):
                    for ec in range(16):
                        e.dma_start_transpose(out=dst[:, ec, :],
                                              in_=og_d[tb * 512:(tb + 1) * 512, ec * 128:(ec + 1) * 128]).then_inc(s_, 16)
                P.dma("sp", rd_ogT, reads=[f"ogd_{n}_{h}" for n in range(tb * 4, tb * 4 + 4) for h in range(4)],
                      writes=[key], semkey=key, ninc=256)
            with ExitStack() as st3d:
                wst = [sb(f"wst{i}", [128, D], st=st3d) for i in range(2)]
                sgq = [sb(f"sgq{i}", [128, 512], st=st3d) for i in range(2)]
                ta = [sb(f"ta{i}", [128, 512], st=st3d) for i in range(2)]
                it = 0
                for dcol in range(8):
                    s = dcol % 2
                    for tb in range(4):
                        b = it % 2
                        it += 1
                        c0 = HALO + tb * 512
                        sl = slice(tb * 512, (tb + 1) * 512)

                        def mmga(e, s=s, b=b, c0=c0):
                            last = None
                            for kc in range(8):
                                last = e.matmul(psB[:, b, :], lhsT=Wa[s][:, kc, :], rhs=hT[:, kc, c0:c0 + 512],
                                                start=(kc == 0), stop=(kc == 7))
                            return last
                        P.op("pe", mmga, reads=[f"Wa{s}"] + ht_keys(range(tb * 4, tb * 4 + 4)), writes=[f"psB{b}"])
                        P.op("act", lambda e, b=b: e.activation(out=sgq[b][:], in_=psB[:, b, :], func=AF.Sigmoid),
                             reads=[f"psB{b}"], writes=[f"sgq{b}"])

                        def mmya(e, s=s, b=b, sl=sl):
                            last = None
                            for cc in range(8):
                                last = e.matmul(psA[:, b, :], lhsT=Wc[s][:, cc, :], rhs=u2[:, cc, sl],
                                                start=(cc == 0), stop=(cc == 7))
                            return last
                        P.op("pe", mmya, reads=[f"Wc{s}"] + [f"u2_{cc}_{tb}" for cc in range(8)], writes=[f"psA{b}"])
                        P.op("dve", lambda e, b=b, dcol=dcol, sl=sl: e.tensor_tensor(
                            out=mT[:, dcol, sl], in0=psA[:, b, :], in1=sgq[b][:], op=ALU.mult),
                            reads=[f"psA{b}", f"sgq{b}"], writes=[f"mT{dcol}_{tb}"])
                    if dcol + 2 < 8:
                        load_wca(dcol + 2)
                    load_wrp(2 * dcol)
                    load_wrp(2 * dcol + 1)
                    if dcol in (5, 6):
                        load_wgb(dcol - 5)
                    if dcol == 3:
                        rd_ogT_op(ogT0, "ogT0", 0)
            P.barrier()
            with ExitStack() as st2:
                ogT = [ogT0, sb("ogT1", [128, 16, 512], BF16, st=st2)]
                sg0 = sb("sg0", [128, 512], st=st2)
                tb0 = sb("tb_0", [128, 512], st=st2)
                sg = [sg0, sg0]
                tb_ = [tb0, tb0]
                for tb in range(4):
                    s = tb % 2
                    if tb == 0:
                        rd_ogT_op(ogT[1], "ogT1", 1)
                    elif tb + 1 < 4:
                        rd_ogT_op(ogT[(tb + 1) % 2], f"ogT{(tb + 1) % 2}", tb + 1)
                    c0 = HALO + tb * 512
                    for dcol in range(8):
                        b = dcol % 2

                        def mmg(e, dcol=dcol, b=b, c0=c0):
                            last = None
                            for kc in range(8):
                                last = e.matmul(psB[:, b, :], lhsT=Wgb[:, kc, dcol * 128:(dcol + 1) * 128],
                                                rhs=hT[:, kc, c0:c0 + 512], start=(kc == 0), stop=(kc == 7))
                            return last
                        P.op("pe", mmg, reads=["Wgb0", "Wgb1"] + ht_keys(range(tb * 4, tb * 4 + 4)), writes=[f"psB{b}"])
                        P.op("act", lambda e, b=b: e.activation(out=sg[b][:], in_=psB[:, b, :], func=AF.Sigmoid),
                             reads=[f"psB{b}"], writes=["sg0"])

                        def mmy(e, dcol=dcol, b=b, s=s):
                            last = None
                            for ec in range(16):
                                last = e.matmul(psA[:, b, :], lhsT=Wrp[:, ec, dcol * 128:(dcol + 1) * 128],
                                                rhs=ogT[s][:, ec, :], start=(ec == 0), stop=(ec == 15))
                            return last
                        P.op("pe", mmy, reads=WRP_KEYS + [f"ogT{s}"], writes=[f"psA{b}"])
                        P.op("dve", lambda e, b=b: e.tensor_tensor(out=tb_[b][:], in0=psA[:, b, :], in1=sg[b][:], op=ALU.mult),
                             reads=[f"psA{b}", "sg0"], writes=["tb_0"])
                        P.op("pool", lambda e, dcol=dcol, b=b, tb=tb: e.tensor_tensor(
                            out=mT[:, dcol, tb * 512:(tb + 1) * 512], in0=tb_[b][:], in1=mT[:, dcol, tb * 512:(tb + 1) * 512],
                            op=ALU.add),
                            reads=["tb_0", f"mT{dcol}_{tb}"], writes=[f"mT{dcol}_{tb}"])
                    if tb == 1:
                        w_out_v = w_out.rearrange("(kc p) n -> p kc n", p=128)
                        for half in range(2):
                            P.dma("pool", lambda e, s_, half=half: e.dma_start(
                                out=Wo[:, :, half * 512:(half + 1) * 512],
                                in_=w_out_v[:, :, half * 512:(half + 1) * 512]).then_inc(s_, 16),
                                writes=[f"Wo{half}"], semkey=f"Wo{half}")

        P.barrier()

        with ExitStack() as st4:
            fgainB = sb("fgainB", [128, D], st=st4)
            ld("sp", fgainB[:], fgainB_d, "fgainB")
            xr = [sb(f"xr{i}", [128, D], st=st4) for i in range(3)]
            yy = [sb(f"yy{i}", [128, D], st=st4) for i in range(2)]
            oo = [sb(f"oo{i}", [128, D], st=st4) for i in range(2)]
            junk2 = sb("junk2", [128, D], st=st4)
            ss2 = [sb(f"ss2_{i}", [128, 1], st=st4) for i in range(2)]
            rs2 = [sb(f"rs2_{i}", [128, 1], st=st4) for i in range(2)]
            pso = [psA, psB]
            out_keys = []

            def o_load(n):
                ld("sp", xr[n % 3][:], x[n * 128:(n + 1) * 128, :], f"xr{n % 3}")

            def o_mm(n):
                s = n % 2
                tb = n // 4

                def mmo2(e, n=n, s=s):
                    last = None
                    for half in range(2):
                        for dc in range(8):
                            last = e.matmul(pso[s][:, half, :], lhsT=mT[:, dc, n * 128:(n + 1) * 128],
                                            rhs=Wo[:, dc, half * 512:(half + 1) * 512], start=(dc == 0), stop=(dc == 7))
                    return last
                P.op("pe", mmo2, reads=["Wo0", "Wo1"] + [f"mT{dc}_{tb}" for dc in range(8)],
                     writes=[f"ps{'AB'[s]}0", f"ps{'AB'[s]}1"])

            def o_y(n):
                s = n % 2
                for half in range(2):
                    P.op("dve", lambda e, s=s, half=half, n=n: e.tensor_tensor(
                        out=yy[s][:, half * 512:(half + 1) * 512], in0=pso[s][:, half, :],
                        in1=xr[n % 3][:, half * 512:(half + 1) * 512], op=ALU.add),
                        reads=[f"ps{'AB'[s]}{half}", f"xr{n % 3}"], writes=[f"yy{s}_{half}"])
                P.op("act", lambda e, s=s: e.activation(out=junk2[:], in_=yy[s][:], func=AF.Square, accum_out=ss2[s][:]),
                     reads=[f"yy{s}_0", f"yy{s}_1"], writes=["junk2", f"ss2_{s}"])
                P.op("act", lambda e, s=s: e.activation(out=ss2[s][:], in_=ss2[s][:], func=AF.Sqrt, bias=eps6[:], scale=1.0 / D),
                     reads=[f"ss2_{s}", "eps6"], writes=[f"ss2_{s}"])

            def o_z(n):
                s = n % 2
                P.op("dve", lambda e, s=s: e.reciprocal(out=rs2[s][:], in_=ss2[s][:]),
                     reads=[f"ss2_{s}"], writes=[f"rs2_{s}"])
                P.op("dve", lambda e, s=s: e.scalar_tensor_tensor(
                    out=oo[s][:], in0=yy[s][:], scalar=rs2[s][:], in1=fgainB[:], op0=ALU.mult, op1=ALU.mult),
                    reads=[f"yy{s}_0", f"yy{s}_1", f"rs2_{s}", "fgainB"], writes=[f"oo{s}"])
                P.dma("sp", lambda e, s_, s=s, n=n: e.dma_start(out=out_d[n * 128:(n + 1) * 128, :], in_=oo[s][:]).then_inc(s_, 16),
                      reads=[f"oo{s}"], writes=[f"out{n}"], semkey=f"outd{s}")
                out_keys.append(f"out{n}")

            o_load(0)
            o_load(1)
            o_mm(0)
            for n in range(NT + 1):
                if n + 2 < NT:
                    o_load(n + 2)
                if n + 1 < NT:
                    o_mm(n + 1)
                if n < NT:
                    o_y(n)
                if n >= 1:
                    o_z(n - 1)
            P.op("sp", None, reads=out_keys)
        P.emit(nc, gst)
    return nc


_NC_CACHE = {}


def _consts(j):
    LG = log_gammas()
    pos = (j * S_CORE + np.arange(S_CORE, dtype=np.float64))
    inv_freq = 10000.0 ** (-np.arange(0, 256, 2, dtype=np.float64) / 256.0)
    ang = inv_freq[:, None] * pos[None, :]
    cosT = np.cos(ang).astype(np.float32)
    sinT = np.sin(ang).astype(np.float32)
    pc = np.zeros((128, NPP), np.float64)
    p = np.arange(128, dtype=np.float64)
    for h in range(4):
        for n in range(NT):
            pc[:, PP_ZE + h * 16 + n] = np.exp(LG[h] * (2047.0 - (128.0 * n + p))) / 16.0
        pc[:, PP_EPS + h] = 1e-5 / np.exp(2.0 * LG[h] * (p + 1.0))
        for r in range(3):
            pc[:, PP_COEF + h * 3 + r] = math.exp(LG[h] * 2048.0 * (j - r)) if r < j else 0.0
    mask = np.zeros((128, 4, 128), np.float64)
    jj = np.arange(128)[:, None]
    ii = np.arange(128)[None, :]
    for h in range(4):
        mask[:, h, :] = np.where(jj <= ii, np.exp(-LG[h] * (jj + 1.0)) / 16.0, 0.0)
    return cosT, sinT, pc, mask.reshape(128, 512).astype(np.float32)


def kernel(x, norm_gain, w_in, conv_dw_w, conv_dw_b, conv_ln_g, conv_ln_b,
           w_conv_proj, ret_gn_g, w_ret_proj, w_out, final_gain):
    x = np.asarray(x, np.float32)
    f = lambda a: np.ascontiguousarray(np.asarray(a, np.float32))
    w_in0 = f(w_in[0])
    w_cp0 = f(w_conv_proj[0])
    w_rp0 = f(w_ret_proj[0])
    w_out0 = f(w_out[0])
    gainB = f(np.broadcast_to(np.asarray(norm_gain, np.float32)[0][None, :], (128, D)))
    fgainB = f(np.broadcast_to(np.asarray(final_gain, np.float32)[None, :], (128, D)))
    cw = np.zeros((32, D), np.float32)
    cw[:31] = np.asarray(conv_dw_w, np.float32)[0]
    convw = f(cw.reshape(8, 4, 8, 4, 32).transpose(1, 4, 2, 3, 0).reshape(128, 256))
    i4 = f(np.tile(np.eye(32, dtype=np.float32), (4, 1)))

    def colT(v, nch):
        return np.asarray(v, np.float32).reshape(nch, 128).T

    ident = np.eye(128, dtype=np.float32)
    if "nc" not in _NC_CACHE:
        _NC_CACHE["nc"] = build_program()
    nc = _NC_CACHE["nc"]
    in_maps = []
    for c in range(8):
        b, j = c // 4, c % 4
        cosT, sinT, pc, mask = _consts(j)
        pc[:, PP_CB:PP_CB + 8] = colT(conv_dw_b[0], 8)
        pc[:, PP_LG:PP_LG + 8] = colT(conv_ln_g[0], 8)
        pc[:, PP_LB:PP_LB + 8] = colT(conv_ln_b[0], 8)
        pc[:, PP_GN:PP_GN + 16] = colT(ret_gn_g[0], 16)
        xs = f(x[b, j * S_CORE:(j + 1) * S_CORE, :])
        if j == 0:
            xhalo = np.zeros((HALO, D), np.float32)
        else:
            xhalo = f(x[b, j * S_CORE - HALO:j * S_CORE, :])
        in_maps.append({
            "x": xs, "xh": xhalo, "w_in": w_in0, "w_cp": w_cp0, "w_rp": w_rp0, "w_out": w_out0,
            "gainB": gainB, "fgainB": fgainB, "pp": f(pc.astype(np.float32)), "convw": convw,
            "maskT": mask, "ident": ident, "cosT": cosT, "sinT": sinT, "i4": i4,
        })
    res = run_bass_kernel_spmd(nc, in_maps, core_ids=list(range(8)))
    out = np.empty((2, 4 * S_CORE, D), np.float32)
    for c in range(8):
        b, j = c // 4, c % 4
        out[b, j * S_CORE:(j + 1) * S_CORE, :] = res.results[c]["out"]
    return out
```
